# Optimizing a Trainium2 kernel written in Bass

```python
import jax, jax.numpy as jnp
from jax import lax
import numpy as np

D_MODEL = 1024
BATCH = 4
SEQ = 8192
DEPTH = 4

N_MIXERS = 2
EXPAND = 2
D_INNER = EXPAND * D_MODEL
CONV_WIDTH = 31
CONV_IN_PROJ = 3 * D_INNER
RET_HEADS = 4
RET_QK_DIM = D_MODEL // RET_HEADS
RET_V_DIM = D_INNER // RET_HEADS
RET_IN_PROJ = 2 * RET_HEADS * RET_QK_DIM + 2 * D_INNER
CHUNK = 128
ROPE_BASE = 10000.0
EPS = 1e-6
N_CONV_LAYERS = (DEPTH + 1) // 2
N_RET_LAYERS = DEPTH // 2

kernel_name = "hybrid_conformer_conv_retention_trunk"


def rms_norm(x, g):
    xf = x.astype(jnp.float32)
    y = xf * lax.rsqrt(jnp.mean(xf * xf, axis=-1, keepdims=True) + EPS)
    return (y * g.astype(jnp.float32)).astype(x.dtype)


def conv_module(h, w_in, dw_w, dw_b, ln_g, ln_b, w_out):
    proj = h @ w_in
    a, b, z = jnp.split(proj, 3, axis=-1)
    u = a * jax.nn.sigmoid(b)
    c = lax.conv_general_dilated(
        u, dw_w[:, None, :].astype(u.dtype), window_strides=(1,),
        padding=((CONV_WIDTH - 1, 0),),
        dimension_numbers=("NWC", "WIO", "NWC"),
        feature_group_count=D_INNER) + dw_b
    cf = c.astype(jnp.float32)
    mu = jnp.mean(cf, axis=-1, keepdims=True)
    var = jnp.mean(jnp.square(cf - mu), axis=-1, keepdims=True)
    cn = ((cf - mu) * lax.rsqrt(var + EPS) * ln_g.astype(jnp.float32) + ln_b.astype(jnp.float32))
    y = jax.nn.silu(cn).astype(h.dtype) * jax.nn.silu(z)
    return y @ w_out


def apply_rotary(t, cos, sin):
    t1, t2 = jnp.split(t.astype(jnp.float32), 2, axis=-1)
    return jnp.concatenate([t1 * cos - t2 * sin, t1 * sin + t2 * cos], axis=-1).astype(t.dtype)


def retention_module(h, positions, w_in, w_out):
    bsz, seq, _ = h.shape
    nc = seq // CHUNK
    qk_w = RET_HEADS * RET_QK_DIM
    proj = h @ w_in
    q, k, v, g = jnp.split(proj, [qk_w, 2 * qk_w, 2 * qk_w + D_INNER], axis=-1)
    q = q.reshape(bsz, seq, RET_HEADS, RET_QK_DIM)
    k = k.reshape(bsz, seq, RET_HEADS, RET_QK_DIM)
    v = v.reshape(bsz, seq, RET_HEADS, RET_V_DIM)

    inv_freq = ROPE_BASE ** (-jnp.arange(RET_QK_DIM // 2, dtype=jnp.float32) / (RET_QK_DIM // 2))
    ang = positions.astype(jnp.float32)[..., None] * inv_freq
    cos, sin = jnp.cos(ang)[:, :, None, :], jnp.sin(ang)[:, :, None, :]
    q = apply_rotary(q, cos, sin)
    k = apply_rotary(k, cos, sin) * (RET_QK_DIM ** -0.5)

    qc = q.reshape(bsz, nc, CHUNK, RET_HEADS, RET_QK_DIM)
    kc = k.reshape(bsz, nc, CHUNK, RET_HEADS, RET_QK_DIM)
    vc = v.reshape(bsz, nc, CHUNK, RET_HEADS, RET_V_DIM)

    log_gamma = jnp.log(1.0 - 2.0 ** (-5.0 - jnp.arange(RET_HEADS, dtype=jnp.float32)))
    idx = jnp.arange(CHUNK, dtype=jnp.float32)
    diff = idx[:, None] - idx[None, :]
    causal = diff >= 0
    inner_decay = jnp.where(causal[None], jnp.exp(jnp.where(causal, diff, 0.0)[None] * log_gamma[:, None, None]), 0.0)
    q_decay = jnp.exp((idx + 1.0)[:, None] * log_gamma)[..., None]
    k_decay = jnp.exp((CHUNK - 1.0 - idx)[:, None] * log_gamma)[..., None]
    chunk_decay = jnp.exp(CHUNK * log_gamma)[:, None, None]

    scores = jnp.einsum("bnihd,bnjhd->bnhij", qc, kc) * inner_decay
    inner = jnp.einsum("bnhij,bnjhe->bnihe", scores, vc)

    def step(state, xs):
        qn, kn, vn = xs
        cross = jnp.einsum("bihd,bhde->bihe", qn * q_decay, state)
        state = state * chunk_decay + jnp.einsum("bjhd,bjhe->bhde", kn * k_decay, vn)
        return state, cross

    state0 = jnp.zeros((bsz, RET_HEADS, RET_QK_DIM, RET_V_DIM), jnp.float32)
    _, cross = lax.scan(step, state0, (jnp.moveaxis(qc, 1, 0), jnp.moveaxis(kc, 1, 0), jnp.moveaxis(vc, 1, 0)))
    o = (inner + jnp.moveaxis(cross, 0, 1)).astype(jnp.float32).reshape(bsz, seq, RET_HEADS, RET_V_DIM)

    o = o * lax.rsqrt(jnp.mean(o * o, axis=-1, keepdims=True) + EPS)
    o = o.reshape(bsz, seq, D_INNER).astype(h.dtype)
    return (jax.nn.silu(g) * o) @ w_out


def setup_inputs(seed: int = 0) -> dict:
    key = jax.random.key(seed)
    ks = jax.random.split(key, 14)
    f32 = jnp.float32
    x = jax.random.normal(ks[0], (BATCH, SEQ, D_MODEL), f32)
    positions = jnp.broadcast_to(jnp.arange(SEQ, dtype=jnp.int32), (BATCH, SEQ))
    conv_norm = 1.0 + 0.01 * jax.random.normal(ks[1], (N_CONV_LAYERS, D_MODEL), f32)
    conv_w_in = jax.random.normal(ks[2], (N_CONV_LAYERS, D_MODEL, CONV_IN_PROJ), f32) * D_MODEL ** -0.5
    conv_dw_w = jax.random.normal(ks[3], (N_CONV_LAYERS, CONV_WIDTH, D_INNER), f32) * CONV_WIDTH ** -0.5
    conv_dw_b = 0.01 * jax.random.normal(ks[4], (N_CONV_LAYERS, D_INNER), f32)
    conv_ln_g = 1.0 + 0.01 * jax.random.normal(ks[5], (N_CONV_LAYERS, D_INNER), f32)
    conv_ln_b = 0.01 * jax.random.normal(ks[6], (N_CONV_LAYERS, D_INNER), f32)
    conv_w_out = jax.random.normal(ks[7], (N_CONV_LAYERS, D_INNER, D_MODEL), f32) * (0.5 * D_INNER ** -0.5)
    ret_norm = 1.0 + 0.01 * jax.random.normal(ks[8], (N_RET_LAYERS, D_MODEL), f32)
    ret_w_in = jax.random.normal(ks[9], (N_RET_LAYERS, D_MODEL, RET_IN_PROJ), f32) * D_MODEL ** -0.5
    ret_w_out = jax.random.normal(ks[10], (N_RET_LAYERS, D_INNER, D_MODEL), f32) * (0.5 * D_INNER ** -0.5)
    final_norm = 1.0 + 0.01 * jax.random.normal(ks[11], (D_MODEL,), f32)
    return {"x": x, "positions": positions,
            "conv_norm": conv_norm, "conv_w_in": conv_w_in, "conv_dw_w": conv_dw_w, "conv_dw_b": conv_dw_b,
            "conv_ln_g": conv_ln_g, "conv_ln_b": conv_ln_b, "conv_w_out": conv_w_out,
            "ret_norm": ret_norm, "ret_w_in": ret_w_in, "ret_w_out": ret_w_out,
            "final_norm": final_norm}


def reference(x, positions, conv_norm, conv_w_in, conv_dw_w, conv_dw_b, conv_ln_g, conv_ln_b, conv_w_out,
              ret_norm, ret_w_in, ret_w_out, final_norm):
    for i in range(DEPTH):
        j = i // N_MIXERS
        if i % N_MIXERS == 0:
            h = rms_norm(x, conv_norm[j])
            y = conv_module(h, conv_w_in[j], conv_dw_w[j], conv_dw_b[j], conv_ln_g[j], conv_ln_b[j], conv_w_out[j])
        else:
            h = rms_norm(x, ret_norm[j])
            y = retention_module(h, positions, ret_w_in[j], ret_w_out[j])
        x = x + y.astype(x.dtype)
    return rms_norm(x, final_norm)
```

```python
import math
from contextlib import ExitStack

import numpy as np
import ml_dtypes

import concourse.bass as bass
import concourse.mybir as mybir
from concourse.bass_utils import run_bass_kernel_spmd

F32 = mybir.dt.float32
BF16 = mybir.dt.bfloat16
I32 = mybir.dt.int32
AF = mybir.ActivationFunctionType
ALU = mybir.AluOpType

D = 1024
E = 2048
H = 4
DK = 256
DV = 512
KW = 31
HALO = KW - 1
EPS = 1e-6
TWO_PI = 2.0 * math.pi
MAGIC = 12582912.0
CW1 = 6.28125
CW2 = TWO_PI - 6.28125
PI_LO = 3.1415925
GAMMA = [1.0 - 2.0 ** (-5.0 - h) for h in range(H)]
CDEC = [g ** 128 for g in GAMMA]

ENGS = ("pe", "act", "dve", "pool", "sp")
BLK = dict(pe="tensor", act="scalar", dve="vector", pool="gpsimd", sp="sync")
import os as _os
SAME_ENGINE_SYNC = _os.environ.get("K_SES", "1") == "1"


class Res:
    __slots__ = ("name", "w", "r")

    def __init__(self, name):
        self.name = name
        self.w = None
        self.r = []


class Prog:
    def __init__(self, nc, stack):
        self.nc = nc
        self.stack = stack
        self.sem = {e: stack.enter_context(nc.semaphore("sem_" + e)) for e in ENGS}
        self.cnt = {e: 0 for e in ENGS}
        self.dsem = {}
        self.dcnt = {}
        self.persist = set()
        self.cc_sems = set()
        self.waited = {e: {} for e in ENGS}
        self.streams = {e: [] for e in ENGS}
        self.extra = {e: set() for e in ENGS}

    def _semh(self, k):
        return self.sem[k] if k in self.sem else self.dsem[k]

    def _emit(self, eng, fn, reads, writes, ev):
        deps = set(self.extra[eng])
        self.extra[eng] = set()
        for r in reads:
            if r.w is not None:
                deps.add(r.w)
        for w in writes:
            if w.w is not None:
                deps.add(w.w)
            deps.update(w.r)
        best = {}
        for k, v in deps:
            if v > best.get(k, 0):
                best[k] = v
        waits = []
        wd = self.waited[eng]
        for k, v in best.items():
            if k == eng and (eng == "pe" or not SAME_ENGINE_SYNC):
                continue
            if wd.get(k, 0) >= v:
                continue
            wd[k] = v
            waits.append((k, v))
        self.streams[eng].append((waits, fn, ev))
        if ev is not None:
            for r in reads:
                r.r.append(ev)
            for w in writes:
                w.w = ev
                w.r = []

    def op(self, eng, fn, reads=(), writes=()):
        self.cnt[eng] += 1
        ev = (eng, self.cnt[eng])
        self._emit(eng, fn, reads, writes, ev)
        return ev

    def dma(self, q, semname, out, in_, reads=(), writes=(), persist=False):
        if semname not in self.dsem:
            self.dsem[semname] = self.stack.enter_context(self.nc.semaphore("d_" + semname))
            self.dcnt[semname] = 0
            if persist:
                self.persist.add(semname)
        self.dcnt[semname] += 16
        ev = (semname, self.dcnt[semname])
        self._emit(q, lambda e: e.dma_start(out=out, in_=in_), reads, writes, ev)
        return ev

    def collective(self, semname, kind, rg, in_ap, out_ap, reads=(), writes=()):
        if semname not in self.dsem:
            self.dsem[semname] = self.stack.enter_context(self.nc.semaphore("c_" + semname))
            self.dcnt[semname] = 0
            self.cc_sems.add(semname)
        self.dcnt[semname] += 1
        ev = (semname, self.dcnt[semname])
        self._emit("pool", lambda e: e.collective_compute(kind, ALU.bypass, replica_groups=rg, ins=[in_ap], outs=[out_ap]),
                   reads, writes, ev)
        return ev

    def barrier(self, include_persist=False):
        tg = set((e, self.cnt[e]) for e in ENGS if self.cnt[e] > 0)
        for k, v in self.dcnt.items():
            if v > 0 and (include_persist or k not in self.persist):
                tg.add((k, v))
        for e in ENGS:
            self.extra[e] |= tg

    def final_wait(self):
        self.barrier(include_persist=True)
        self._emit("sp", None, (), (), None)

    def flush(self):
        with self.nc.Block() as block:
            for e in ENGS:
                stream = self.streams[e]
                if not stream:
                    continue

                def body(engobj, stream=stream):
                    for waits, fn, ev in stream:
                        for k, v in waits:
                            engobj.wait_ge(self._semh(k), v)
                        if fn is None:
                            continue
                        ins = fn(engobj)
                        if ev[0] in self.cc_sems:
                            ins.then_inc(self._semh(ev[0]))
                        else:
                            ins.then_inc(self._semh(ev[0]), 16 if ev[0] in self.dsem else 1)

                getattr(block, BLK[e])(body)
        self.streams = {e: [] for e in ENGS}


def build_program(T, layers, first_ext, last_ext, do_final, pair=False, n_cores=8):
    NT = T // 128
    nc = bass.Bass("TRN2", target_bir_lowering=False)
    dr = {}

    def ext_in(name, shape, dt):
        dr[name] = nc.dram_tensor(name, list(shape), dt, kind="ExternalInput").ap()
        return dr[name]

    x_ext = ext_in("x", [T, D], F32)
    posT = ext_in("posT", [128, NT], I32)
    ident_d = ext_in("ident", [128, 128], BF16)
    ones_d = ext_in("ones", [128, 128], BF16)
    invf_d = ext_in("invf", [128, 128], F32)
    maskT_d = ext_in("maskT", [128, H * 128], F32)
    qdT_d = ext_in("qdT", [128, H * 128], F32)
    kd_d = ext_in("kd", [128, H], F32)
    gfin_d = ext_in("gfin", [128, D], F32)
    lw = []
    for li, (kind, slot) in enumerate(layers):
        d = {}
        d["gfm"] = ext_in(f"gfm{li}", [128, 8], F32)
        d["w_in"] = ext_in(f"w_in{li}", [D, 3 * E], F32)
        d["w_out"] = ext_in(f"w_out{li}", [E, D], F32)
        if kind == "conv":
            d["dww"] = ext_in(f"dww{li}", [128, 16 * KW], F32)
            d["dwb"] = ext_in(f"dwb{li}", [128, 16], F32)
            d["lng"] = ext_in(f"lng{li}", [128, 16], F32)
            d["lnb"] = ext_in(f"lnb{li}", [128, 16], F32)
        lw.append(d)
    out_ext = nc.dram_tensor("out", [T, D], F32, kind="ExternalOutput").ap()
    xs = nc.dram_tensor("xs", [T, D], F32, kind="Internal").ap()
    ys = nc.dram_tensor("ys", [NT, 128, 16 * 128], BF16, kind="Internal").ap()

    if pair:
        RG = [[2 * i, 2 * i + 1] for i in range(n_cores // 2)]
        xprev0_d = ext_in("xprev0", [128, D], F32)
        flag_d = ext_in("flag", [128, 1], F32)
        ol = nc.dram_tensor("ol", [NT, 128, E], F32, kind="Internal").ap()
        qs = nc.dram_tensor("qs", [NT, 128, 8 * 128], BF16, kind="Internal").ap()
        sgs = nc.dram_tensor("sgs", [NT, 128, E], BF16, kind="Internal").ap()
        ol_res = [Res(f"ol{t}") for t in range(NT)]
        qs_res = [Res(f"qs{t}") for t in range(NT)]
        sgs_res = [Res(f"sgs{t}") for t in range(NT)]
        ccx_in = nc.dram_tensor("ccx_in", [128, D], F32)
        ccx_out = nc.dram_tensor("ccx_out", [256, D], F32)
        ccs_in = [nc.dram_tensor(f"ccs_in{i}", [128, 8 * 512], F32) for i in range(len(layers))]
        ccs_out = [nc.dram_tensor(f"ccs_out{i}", [256, 8 * 512], F32) for i in range(len(layers))]
        ccx_r, ccs_r = Res("ccx"), Res("ccs")
    xs_res = [Res(f"xs{t}") for t in range(NT)]
    ys_res = [Res(f"ys{t}") for t in range(NT)]
    ysq_res = [[Res(f"ys{t}_{q}") for q in range(4)] for t in range(NT)]

    with ExitStack() as top:
        P = Prog(nc, top)

        uniq = [0]

        def sb(stack, name, shape, dt):
            uniq[0] += 1
            return stack.enter_context(nc.sbuf_tensor(f"{name}_{uniq[0]}", list(shape), dt))

        def ps(stack, name, shape, dt):
            uniq[0] += 1
            return stack.enter_context(nc.psum_tensor(f"{name}_{uniq[0]}", list(shape), dt))

        Win = sb(top, "Win", [128, 8, 3 * E], BF16)
        Win_rs = [Res(f"Win{k}") for k in range(8)]
        ident = sb(top, "identS", [128, 128], BF16)
        ones = sb(top, "onesS", [128, 128], BF16)
        invf = sb(top, "invfS", [128, 128], F32)
        maskT = sb(top, "maskTS", [128, H * 128], F32)
        qdT = sb(top, "qdTS", [128, H, 128], F32)
        kd = sb(top, "kdS", [128, H], F32)
        posi = sb(top, "posi", [128, NT], I32)
        posf = sb(top, "posf", [128, NT], F32)
        negpi = sb(top, "negpi", [128, 1], F32)
        const_r = Res("consts")
        pos_r = Res("pos")

        for dst, src in ((ident, ident_d), (ones, ones_d), (invf, invf_d), (maskT, maskT_d),
                         (kd, kd_d), (posi, posT)):
            P.dma("sp", "const", dst[:], src, writes=[const_r])
        P.dma("sp", "const", qdT[:].rearrange("p h t -> p (h t)"), qdT_d, writes=[const_r])
        P.op("dve", lambda e: e.tensor_copy(out=posf[:], in_=posi[:]), reads=[const_r], writes=[pos_r])
        P.op("dve", lambda e: e.memset(negpi[:], math.pi), writes=[pos_r])
        epsc = sb(top, "epsc", [128, 1], F32)
        if pair:
            flag = sb(top, "flagS", [128, 1], F32)
            P.dma("sp", "const", flag[:], flag_d, writes=[const_r])
        P.op("dve", lambda e: e.memset(epsc[:], EPS), writes=[pos_r])

        def load_win(li):
            src = lw[li]["w_in"].rearrange("(kc p) n -> p kc n", p=128)
            for kc in range(8):
                P.dma("pool", "win", Win[:, kc, :], src[:, kc, :], writes=[Win_rs[kc]], persist=True)

        load_win(0)

        def rms_to_hT(st, xt, xt_r, junk, ss, rstd, xn, small_r, xn_r, PT, PT_r, hT_out, hT_r, gfm, gfm_r, junk_r=None,
                      part=3):
            if part & 1:
                rms_chain(xt, xt_r, junk, ss, rstd, xn, small_r, xn_r, junk_r)
            if part & 2:
                rms_tr(xn, xn_r, PT, PT_r, hT_out, hT_r, gfm, gfm_r)

        def rms_chain(xt, xt_r, junk, ss, rstd, xn, small_r, xn_r, junk_r):
            P.op("dve", lambda e: e.memset(ss[:], 0.0), writes=[small_r])
            P.op("act", lambda e: e.activation(out=junk[:], in_=xt[:], func=AF.Square, accum_out=ss[:, 0:1]),
                 reads=[xt_r], writes=[small_r] + ([junk_r] if junk_r is not None else []))
            P.op("act", lambda e: e.activation(out=rstd[:], in_=ss[:], func=AF.Sqrt, bias=epsc[:, 0:1], scale=1.0 / D),
                 reads=[small_r, pos_r], writes=[small_r])
            P.op("dve", lambda e: e.reciprocal(out=rstd[:], in_=rstd[:]), reads=[small_r], writes=[small_r])
            P.op("act", lambda e: e.activation(out=xn[:], in_=xt[:], func=AF.Identity, scale=rstd[:, 0:1]),
                 reads=[xt_r, small_r], writes=[xn_r])

        def rms_tr(xn, xn_r, PT, PT_r, hT_out, hT_r, gfm, gfm_r):
            def tr(e):
                ins = None
                for kc in range(8):
                    ins = e.transpose(out=PT[:, kc, :], in_=xn[:, kc * 128:(kc + 1) * 128], identity=ident[:])
                return ins
            P.op("pe", tr, reads=[xn_r, const_r], writes=[PT_r])
            P.op("dve", lambda e: e.tensor_tensor(out=hT_out, in0=PT[:],
                                                  in1=gfm[:].unsqueeze(2).to_broadcast([128, 8, 128]),
                                                  op=ALU.mult), reads=[PT_r, gfm_r], writes=[hT_r])

        def emit_tables(t, ang, ang2, ang_r, tabs_out, out_r):
            for tab, off in ((tabs_out[0], 0.0), (tabs_out[1], 0.25)):
                P.op("dve", lambda e, t=t: e.tensor_scalar(out=ang[:], in0=invf[:], scalar1=posf[:, t:t + 1],
                                                           scalar2=None, op0=ALU.mult),
                     reads=[const_r, pos_r], writes=[ang_r])
                P.op("dve", lambda e, off=off: e.tensor_scalar(out=ang2[:], in0=ang[:], scalar1=1.0 / TWO_PI,
                                                               scalar2=off, op0=ALU.mult, op1=ALU.add),
                     reads=[ang_r], writes=[ang_r])
                P.op("dve", lambda e: e.tensor_scalar(out=ang2[:], in0=ang2[:], scalar1=MAGIC, scalar2=None,
                                                      op0=ALU.add), reads=[ang_r], writes=[ang_r])
                P.op("dve", lambda e: e.tensor_scalar(out=ang2[:], in0=ang2[:], scalar1=-MAGIC, scalar2=None,
                                                      op0=ALU.add), reads=[ang_r], writes=[ang_r])
                for cw in (CW1, CW2):
                    P.op("dve", lambda e, cw=cw: e.scalar_tensor_tensor(out=ang[:], in0=ang2[:], scalar=-cw, in1=ang[:],
                                                                        op0=ALU.mult, op1=ALU.add),
                         reads=[ang_r], writes=[ang_r])
                P.op("dve", lambda e, off=off: e.tensor_scalar(out=ang[:], in0=ang[:], scalar1=off * TWO_PI,
                                                               scalar2=PI_LO, op0=ALU.add, op1=ALU.min),
                     reads=[ang_r], writes=[ang_r])
                P.op("dve", lambda e: e.tensor_scalar(out=ang[:], in0=ang[:], scalar1=-PI_LO, scalar2=None,
                                                      op0=ALU.max), reads=[ang_r], writes=[ang_r])
                P.op("act", lambda e, tab=tab: e.activation(out=tab, in_=ang[:], func=AF.Sin),
                     reads=[ang_r], writes=[out_r])

        pre_tabs = pair and layers[0][0] == "conv" and any(k == "ret" for k, _ in layers)
        if pre_tabs:
            tabs_d = nc.dram_tensor("tabs", [NT, 128, 256], F32, kind="Internal").ap()
            tabs_res = [Res(f"tabs{t}") for t in range(NT)]

        for li, (kind, slot) in enumerate(layers):
            is_first = li == 0
            is_last = li == len(layers) - 1
            x_src = x_ext if (is_first and first_ext) else xs
            x_dst = out_ext if (is_last and last_ext) else xs
            W = lw[li]

            with ExitStack() as st:
                gfm = sb(st, "gfm", [128, 8], F32)
                gfm_r = Res("gfm")
                P.dma("sp", "lc_g", gfm[:], W["gfm"], writes=[gfm_r])
                xin = [sb(st, f"xin{i}", [128, D], F32) for i in range(2)]
                xin_r = [Res(f"xin{i}") for i in range(2)]
                junk = sb(st, "junk", [128, D], BF16) if kind == "ret" else None
                ss = sb(st, "ss", [128, 1], F32)
                rstd = sb(st, "rstd", [128, 1], F32)
                small_r = Res("small")
                xn = sb(st, "xn", [128, D], BF16)
                xn_r = Res("xn")
                PT = [ps(st, f"PT{i}", [128, 8, 128], BF16) for i in range(2)]
                PT_r = [Res(f"PT{i}") for i in range(2)]

                if kind == "ret":
                    NF = 6
                    Fb = [ps(st, f"F{i}", [128, 512], F32) for i in range(NF)]
                    Fb_r = [Res(f"F{i}") for i in range(NF)]
                    hT = [sb(st, f"hT{i}", [128, 8, 128], BF16) for i in range(2)]
                    hT_r = [Res(f"hT{i}") for i in range(2)]
                    ang = sb(st, "ang", [128, 128], F32)
                    ang2 = sb(st, "ang2", [128, 128], F32)
                    ang3 = sb(st, "ang3", [128, 128], F32)
                    cstab = [sb(st, f"cstab{i}", [128, 2, 128], F32) for i in range(2)]
                    sint = [cstab[i][:, 0, :] for i in range(2)]
                    cost = [cstab[i][:, 1, :] for i in range(2)]
                    ang_r = Res("ang")
                    cs_r = [Res("cs0"), Res("cs1")]
                    tA = sb(st, "tA", [128, 2, 128], F32)
                    tB = sb(st, "tB", [128, 2, 128], F32)
                    tmp_r = Res("tmp")
                    qrot = sb(st, "qrot", [128, 1024], BF16)
                    krot = sb(st, "krot", [128, 1024], BF16)
                    qrot_r, krot_r = Res("qrot"), Res("krot")
                    khat = [sb(st, f"khat{i}", [128, H, DK], BF16) for i in range(2)]
                    khat_r = [Res("khat0"), Res("khat1")]
                    qT = [sb(st, f"qT{i}", [128, 8, 128], BF16) for i in range(2)]
                    kT = [sb(st, f"kT{i}", [128, 8, 128], BF16) for i in range(2)]
                    qT_r, kT_r = [Res("qT0"), Res("qT1")], [Res("kT0"), Res("kT1")]
                    v = [sb(st, f"v{i}", [128, E], BF16) for i in range(2)]
                    sg = [sb(st, f"sg{i}", [128, E], BF16) for i in range(2)]
                    v_r = [[Res(f"v{i}_{h}") for h in range(H)] for i in range(2)]
                    sg_r = [[Res(f"sg{i}_{h}") for h in range(H)] for i in range(2)]
                    Pm = sb(st, "Pm", [128, H * 128], BF16)
                    Pm_r = Res("Pm")
                    S = sb(st, "S", [128, 8, 512], F32)
                    Sbf = sb(st, "Sbf", [128, 8, 512], BF16)
                    S_r = [Res(f"S{i}") for i in range(8)]
                    Sbf_r = [Res(f"Sbf{i}") for i in range(8)]
                    ssq = sb(st, "ssq", [128, H], F32)
                    rs4 = sb(st, "rs4", [128, H], F32)
                    ssq_r = Res("ssq")
                    if not pair:
                        og = sb(st, "og", [128, E], BF16)
                        og_r = Res("og")
                        ogT = [sb(st, f"ogT{i}", [128, 16, 128], BF16) for i in range(2)]
                        ogT_r = [Res(f"ogT{i}") for i in range(2)]
                    else:
                        olb = [sb(st, f"olb{i}", [128, E], F32) for i in range(2)]
                        olb_r = [Res(f"olb{i}") for i in range(2)]

                    P.op("dve", lambda e: e.memset(S[:], 0.0), writes=S_r)
                    P.op("pool", lambda e: e.memset(Sbf[:], 0.0), writes=Sbf_r)
                    fctr = [0]

                    def load_x(t):
                        nb = t % 2
                        P.dma("sp", f"xin{nb}", xin[nb][:], x_src[t * 128:(t + 1) * 128, :],
                              reads=[xs_res[t]], writes=[xin_r[nb]])

                    def tables(t):
                        b = t % 2
                        if pre_tabs:
                            P.dma("sp", f"tab_ld{b}", cstab[b][:].rearrange("p a f -> p (a f)"), tabs_d[t],
                                  reads=[tabs_res[t]], writes=[cs_r[b]])
                        else:
                            emit_tables(t, ang, ang2, ang_r, (sint[b], cost[b]), cs_r[b])

                    def prep(t):
                        b = t % 2
                        tables(t)
                        rms_to_hT(st, xin[b], xin_r[b], junk, ss, rstd, xn, small_r, xn_r,
                                  PT[0], PT_r[0], hT[b][:], hT_r[b], gfm, gfm_r)
                        if t + 2 < NT:
                            load_x(t + 2)

                    def stage1(t):
                        b = t % 2
                        for n in (0, 4, 8, 1, 5, 9, 2, 6, 10, 3, 7, 11):
                            fi = fctr[0] % 4
                            fctr[0] += 1
                            F_, F_r = Fb[fi], Fb_r[fi]

                            def mm(e, n=n, F_=F_, b=b):
                                ins = None
                                for kc in range(8):
                                    ins = e.matmul(F_[:], lhsT=hT[b][:, kc, :], rhs=Win[:, kc, n * 512:(n + 1) * 512],
                                                   start=(kc == 0), stop=(kc == 7))
                                return ins
                            P.op("pe", mm, reads=[hT_r[b]] + Win_rs, writes=[F_r])
                            if n < 4:
                                dst, dst_r = (qrot, qrot_r) if n < 2 else (krot, krot_r)
                                c0 = (n % 2) * 512
                                Fv = F_[:].rearrange("p (h two f) -> p h two f", h=2, two=2)
                                t1, t2 = Fv[:, :, 0, :], Fv[:, :, 1, :]
                                dv_ = dst[:, c0:c0 + 512].rearrange("p (h two f) -> p h two f", h=2, two=2)
                                o1, o2 = dv_[:, :, 0, :], dv_[:, :, 1, :]
                                cb_ = cost[b].unsqueeze(1).to_broadcast([128, 2, 128])
                                sb_ = sint[b].unsqueeze(1).to_broadcast([128, 2, 128])
                                rd = [F_r, cs_r[b]]
                                P.op("dve", lambda e, t1=t1, cb_=cb_: e.tensor_tensor(out=tA[:], in0=t1, in1=cb_, op=ALU.mult),
                                     reads=rd, writes=[tmp_r])
                                P.op("dve", lambda e, t2=t2, sb_=sb_: e.tensor_tensor(out=tB[:], in0=t2, in1=sb_, op=ALU.mult),
                                     reads=rd, writes=[tmp_r])
                                P.op("dve", lambda e, o1=o1: e.tensor_tensor(out=o1, in0=tA[:], in1=tB[:], op=ALU.subtract),
                                     reads=[tmp_r], writes=[dst_r])
                                P.op("dve", lambda e, t1=t1, sb_=sb_: e.tensor_tensor(out=tA[:], in0=t1, in1=sb_, op=ALU.mult),
                                     reads=rd, writes=[tmp_r])
                                P.op("dve", lambda e, t2=t2, cb_=cb_: e.tensor_tensor(out=tB[:], in0=t2, in1=cb_, op=ALU.mult),
                                     reads=rd, writes=[tmp_r])
                                P.op("dve", lambda e, o2=o2: e.tensor_tensor(out=o2, in0=tA[:], in1=tB[:], op=ALU.add),
                                     reads=[tmp_r, F_r], writes=[dst_r])
                            elif n < 8:
                                h = n - 4
                                P.op("act", lambda e, h=h, F_=F_, b=b: e.activation(out=v[b][:, h * 512:(h + 1) * 512], in_=F_[:], func=AF.Copy),
                                     reads=[F_r], writes=[v_r[b][h]])
                            else:
                                h = n - 8
                                P.op("act", lambda e, h=h, F_=F_, b=b: e.activation(out=sg[b][:, h * 512:(h + 1) * 512], in_=F_[:], func=AF.Silu),
                                     reads=[F_r], writes=[sg_r[b][h]])

                    def stage1b(t):
                        b = t % 2

                        def trq(e):
                            ins = None
                            for c in range(8):
                                ins = e.transpose(out=PT[0][:, c, :], in_=qrot[:, c * 128:(c + 1) * 128], identity=ident[:])
                            return ins
                        P.op("pe", trq, reads=[qrot_r, const_r], writes=[PT_r[0]])
                        P.op("dve", lambda e, b=b: e.tensor_tensor(
                            out=qT[b][:].rearrange("p (h two) t -> p h two t", two=2),
                            in0=PT[0][:].rearrange("p (h two) t -> p h two t", two=2),
                            in1=qdT[:].unsqueeze(2).to_broadcast([128, H, 2, 128]), op=ALU.mult),
                            reads=[PT_r[0], const_r], writes=[qT_r[b]])

                        def trk(e):
                            ins = None
                            for c in range(8):
                                ins = e.transpose(out=PT[1][:, c, :], in_=krot[:, c * 128:(c + 1) * 128], identity=ident[:])
                            return ins
                        P.op("pe", trk, reads=[krot_r, const_r], writes=[PT_r[1]])
                        P.op("act", lambda e, b=b: e.activation(out=kT[b][:], in_=PT[1][:], func=AF.Copy),
                             reads=[PT_r[1]], writes=[kT_r[b]])
                        P.op("pool", lambda e, b=b: e.tensor_tensor(
                            out=khat[b][:], in0=krot[:].rearrange("p (h d) -> p h d", h=H),
                            in1=kd[:].unsqueeze(2).to_broadcast([128, H, DK]), op=ALU.mult),
                            reads=[krot_r, const_r], writes=[khat_r[b]])

                    def stage2a(t):
                        b = t % 2
                        Fs, Fs_r = Fb[4], Fb_r[4]

                        def sc(e, Fs=Fs, b=b):
                            ins = None
                            for h in range(H):
                                for dc in range(2):
                                    ins = e.matmul(Fs[:, h * 128:(h + 1) * 128], lhsT=kT[b][:, 2 * h + dc, :],
                                                   rhs=qT[b][:, 2 * h + dc, :], start=(dc == 0), stop=(dc == 1))
                            return ins
                        P.op("pe", sc, reads=[kT_r[b], qT_r[b]], writes=[Fs_r])
                        P.op("dve", lambda e, Fs=Fs: e.tensor_tensor(out=Pm[:], in0=Fs[:], in1=maskT[:], op=ALU.mult),
                             reads=[Fs_r, const_r], writes=[Pm_r])

                    def stage2b(t):
                        b = ob = t % 2
                        for h in range(H):
                            fi = fctr[0] % 4
                            fctr[0] += 1
                            Fo, Fo_r = Fb[fi], Fb_r[fi]

                            def om(e, h=h, Fo=Fo, b=b):
                                e.matmul(Fo[:], lhsT=Pm[:, h * 128:(h + 1) * 128], rhs=v[b][:, h * 512:(h + 1) * 512],
                                         start=True, stop=False)
                                e.matmul(Fo[:], lhsT=qT[b][:, 2 * h, :], rhs=Sbf[:, 2 * h, :], start=False, stop=False)
                                return e.matmul(Fo[:], lhsT=qT[b][:, 2 * h + 1, :], rhs=Sbf[:, 2 * h + 1, :],
                                                start=False, stop=True)
                            P.op("pe", om, reads=[Pm_r, v_r[b][h], qT_r[b], Sbf_r[2 * h], Sbf_r[2 * h + 1]], writes=[Fo_r])
                            if pair:
                                if h % 2 == 0:
                                    P.op("act", lambda e, h=h, Fo=Fo, ob=ob: e.activation(
                                        out=olb[ob][:, h * 512:(h + 1) * 512], in_=Fo[:], func=AF.Copy),
                                        reads=[Fo_r], writes=[olb_r[ob]])
                                else:
                                    P.op("dve", lambda e, h=h, Fo=Fo, ob=ob: e.tensor_copy(
                                        out=olb[ob][:, h * 512:(h + 1) * 512], in_=Fo[:]),
                                        reads=[Fo_r], writes=[olb_r[ob]])
                                continue
                            P.op("dve", lambda e, h=h: e.memset(ssq[:, h:h + 1], 0.0), writes=[ssq_r])
                            P.op("act", lambda e, h=h, Fo=Fo: e.activation(out=junk[:, 0:512], in_=Fo[:], func=AF.Square,
                                                                          accum_out=ssq[:, h:h + 1]),
                                 reads=[Fo_r], writes=[ssq_r, small_r])
                            P.op("act", lambda e, h=h: e.activation(out=rs4[:, h:h + 1], in_=ssq[:, h:h + 1], func=AF.Sqrt,
                                                                    bias=epsc[:, 0:1], scale=1.0 / DV),
                                 reads=[ssq_r, pos_r], writes=[ssq_r])
                            P.op("dve", lambda e, h=h: e.reciprocal(out=rs4[:, h:h + 1], in_=rs4[:, h:h + 1]),
                                 reads=[ssq_r], writes=[ssq_r])
                            P.op("dve", lambda e, h=h, Fo=Fo, b=b: e.scalar_tensor_tensor(
                                out=og[:, h * 512:(h + 1) * 512], in0=Fo[:], scalar=rs4[:, h:h + 1],
                                in1=sg[b][:, h * 512:(h + 1) * 512], op0=ALU.mult, op1=ALU.mult),
                                reads=[Fo_r, ssq_r, sg_r[b][h]], writes=[og_r])
                        if pair:
                            P.dma("pool", f"ol_st{ob}", ol[t], olb[ob][:], reads=[olb_r[ob]], writes=[ol_res[t]])
                            P.dma("pool", f"qs_st{b}", qs[t], qT[b][:].rearrange("p k t -> p (k t)"), reads=[qT_r[b]],
                                  writes=[qs_res[t]])
                            P.dma("pool", f"sg_st{b}", sgs[t], sg[b][:], reads=sg_r[b], writes=[sgs_res[t]])
                        for h in range(H):
                            for dc in range(2):
                                i = 2 * h + dc
                                fi = 4 + (fctr[0] % 2)
                                fctr[0] += 1
                                Fk, Fk_r = Fb[fi], Fb_r[fi]
                                P.op("pe", lambda e, h=h, dc=dc, Fk=Fk, b=b: e.matmul(
                                    Fk[:], lhsT=khat[b][:, h, dc * 128:(dc + 1) * 128], rhs=v[b][:, h * 512:(h + 1) * 512],
                                    start=True, stop=True), reads=[khat_r[b], v_r[b][h]], writes=[Fk_r])
                                P.op("dve", lambda e, i=i, h=h, Fk=Fk: e.scalar_tensor_tensor(
                                    out=S[:, i, :], in0=S[:, i, :], scalar=float(CDEC[h]), in1=Fk[:],
                                    op0=ALU.mult, op1=ALU.add), reads=[Fk_r, S_r[i]], writes=[S_r[i]])
                                P.op("act", lambda e, i=i: e.activation(out=Sbf[:, i, :], in_=S[:, i, :], func=AF.Copy),
                                     reads=[S_r[i]], writes=[Sbf_r[i]])
                        if not pair:
                            for half in range(2):
                                def tro(e, half=half):
                                    ins = None
                                    for c in range(8):
                                        cc = half * 8 + c
                                        ins = e.transpose(out=PT[half][:, c, :], in_=og[:, cc * 128:(cc + 1) * 128],
                                                          identity=ident[:])
                                    return ins
                                P.op("pe", tro, reads=[og_r, const_r], writes=[PT_r[half]])
                                if half == 0:
                                    P.op("act", lambda e, ob=ob: e.activation(out=ogT[ob][:, 0:8, :], in_=PT[0][:], func=AF.Copy),
                                         reads=[PT_r[0]], writes=[ogT_r[ob]])
                                else:
                                    P.op("dve", lambda e, ob=ob: e.tensor_copy(out=ogT[ob][:, 8:16, :], in_=PT[1][:]),
                                         reads=[PT_r[1]], writes=[ogT_r[ob]])
                            P.dma("pool", f"ys_st{ob}", ys[t], ogT[ob][:].rearrange("p k t -> p (k t)"),
                                  reads=[ogT_r[ob]], writes=[ys_res[t]])

                    load_x(0)
                    if NT > 1:
                        load_x(1)
                    prep(0)
                    if NT > 1:
                        prep(1)
                    stage1(0)
                    stage1b(0)
                    for t in range(NT):
                        if t + 2 < NT:
                            prep(t + 2)
                        if t + 1 < NT:
                            stage1(t + 1)
                        stage2a(t)
                        if t + 1 < NT:
                            stage1b(t + 1)
                        stage2b(t)
                    if pair:
                        P.dma("pool", "ccs_st", ccs_in[li].ap(), S[:].rearrange("p i e -> p (i e)"), reads=S_r, writes=[ccs_r])
                        P.collective("ccs", "AllGather", RG, ccs_in[li].ap().opt(), ccs_out[li].ap().opt(),
                                     reads=[ccs_r], writes=[ccs_r])
                else:
                    G = 512 if T >= 512 else T
                    NG = T // G
                    TPG = G // 128
                    NF = 6
                    Fb = [ps(st, f"F{i}", [128, 512], F32) for i in range(NF)]
                    Fb_r = [Res(f"F{i}") for i in range(NF)]
                    hT = sb(st, "hTg", [128, 8, G], BF16)
                    hT_r = Res("hTg")
                    U = sb(st, "U", [128, 16, HALO + G], BF16)
                    U_r = [Res(f"U{c}") for c in range(16)]
                    SZ = sb(st, "SZ", [128, 16, G], BF16)
                    SZ_r = [Res(f"SZ{c}") for c in range(16)]
                    csb = sb(st, "csb", [128, 16, G], F32)
                    csb_r = [Res(f"csb{c}") for c in range(16)]
                    sigb = [sb(st, f"sigb{i}", [128, G], F32) for i in range(2)]
                    sigb_r = [Res(f"sigb{i}") for i in range(2)]
                    csq = [sb(st, f"csq{i}", [128, G], BF16) for i in range(2)]
                    chi = [sb(st, f"chi{i}", [128, G], BF16) for i in range(2)]
                    cst_r = [Res(f"cst{i}") for i in range(2)]
                    y1 = [sb(st, f"y1{i}", [128, G], BF16) for i in range(2)]
                    y1_r = [Res(f"y1{i}") for i in range(2)]
                    ND = 3
                    NDT = 8
                    Dg = [sb(st, f"Dg{i}", [128, NDT, 128], BF16) for i in range(ND)]
                    Dg_r = [Res(f"Dg{i}") for i in range(ND)]
                    mean = sb(st, "mean", [128, G], F32)
                    msq = sb(st, "msq", [128, G], F32)
                    rstdl = msq
                    stat_r = Res("stat")
                    dww = sb(st, "dww", [128, 16, KW], F32)
                    dwb = sb(st, "dwb", [128, 16], F32)
                    lng = sb(st, "lng", [128, 16], F32)
                    lnb = sb(st, "lnb", [128, 16], F32)
                    cw_r = Res("cw")
                    P.dma("sp", "lc_c", dww[:].rearrange("p c j -> p (c j)"), W["dww"], writes=[cw_r])
                    P.dma("sp", "lc_c", dwb[:], W["dwb"], writes=[cw_r])
                    P.dma("sp", "lc_c", lng[:], W["lng"], writes=[cw_r])
                    P.dma("sp", "lc_c", lnb[:], W["lnb"], writes=[cw_r])
                    P.dma("sp", "xin0", xin[0][:], x_src[0:128, :], reads=[xs_res[0]], writes=[xin_r[0]])
                    if not pair:
                        P.op("pool", lambda e: e.memset(U[:, :, 0:HALO], 0.0), writes=U_r)
                    else:
                        xp_src = xprev0_d if is_first else ccx_out.ap()[0:128, :]
                        P.dma("sp", "xin1", xin[1][:], xp_src, reads=[ccx_r], writes=[xin_r[1]])
                        P.op("dve", lambda e: e.tensor_scalar(out=xin[1][:], in0=xin[1][:], scalar1=flag[:, 0:1], scalar2=None,
                                                              op0=ALU.mult), reads=[xin_r[1], const_r], writes=[xin_r[1]])
                        rms_to_hT(st, xin[1], xin_r[1], xn, ss, rstd, xn, small_r, xn_r,
                                  PT[0], PT_r[0], hT[:, :, 0:128], hT_r, gfm, gfm_r, junk_r=xn_r)
                        for cb in range(16):
                            s3 = (cb % 2) * 3
                            Fa, Fg = Fb[s3], Fb[s3 + 1]
                            Fa_r, Fg_r = Fb_r[s3], Fb_r[s3 + 1]
                            for F_, F_r, c0 in ((Fa, Fa_r, cb * 128), (Fg, Fg_r, E + cb * 128)):
                                def mmh(e, F_=F_, c0=c0):
                                    ins = None
                                    for kc in range(8):
                                        ins = e.matmul(F_[:, 0:128], lhsT=Win[:, kc, c0:c0 + 128], rhs=hT[:, kc, 0:128],
                                                       start=(kc == 0), stop=(kc == 7))
                                    return ins
                                P.op("pe", mmh, reads=[hT_r] + Win_rs, writes=[F_r])
                            sbi = cb % 2
                            P.op("act", lambda e, Fg=Fg, sbi=sbi: e.activation(out=sigb[sbi][:, 0:128], in_=Fg[:, 0:128], func=AF.Sigmoid),
                                 reads=[Fg_r], writes=[sigb_r[sbi]])
                            P.op("dve", lambda e, Fa=Fa, sbi=sbi, cb=cb: e.tensor_tensor(
                                out=U[:, cb, 0:HALO], in0=Fa[:, 128 - HALO:128], in1=sigb[sbi][:, 128 - HALO:128], op=ALU.mult),
                                reads=[Fa_r, sigb_r[sbi]], writes=[U_r[cb]])
                    dctr = [0]
                    Fsum, Fsq = Fb[4], Fb[5]
                    Fsum_r, Fsq_r = Fb_r[4], Fb_r[5]

                    def prep_tile(g, tt, part=3):
                        t = g * TPG + tt
                        b = t % 2
                        if (part & 1) and t + 1 < NT:
                            nb = (t + 1) % 2
                            P.dma("sp", f"xin{nb}", xin[nb][:], x_src[(t + 1) * 128:(t + 2) * 128, :],
                                  reads=[xs_res[t + 1]], writes=[xin_r[nb]])
                        rms_to_hT(st, xin[b], xin_r[b], xn, ss, rstd, xn, small_r, xn_r,
                                  PT[0], PT_r[0], hT[:, :, tt * 128:(tt + 1) * 128], hT_r, gfm, gfm_r, junk_r=xn_r, part=part)

                    def prep_g(g):
                        for tt in range(TPG):
                            prep_tile(g, tt)

                    def proj(F_, F_r, c0):
                        def mm(e, F_=F_, c0=c0):
                            ins = None
                            for kc in range(8):
                                ins = e.matmul(F_[:, 0:G], lhsT=Win[:, kc, c0:c0 + 128], rhs=hT[:, kc, :],
                                               start=(kc == 0), stop=(kc == 7))
                            return ins
                        P.op("pe", mm, reads=[hT_r] + Win_rs, writes=[F_r])

                    def ab(g, cb):
                        s2 = (cb % 2) * 2
                        Fa, Fg = Fb[s2], Fb[s2 + 1]
                        Fa_r, Fg_r = Fb_r[s2], Fb_r[s2 + 1]
                        proj(Fa, Fa_r, cb * 128)
                        proj(Fg, Fg_r, E + cb * 128)
                        sbi = cb % 2
                        P.op("act", lambda e, Fg=Fg, sbi=sbi: e.activation(out=sigb[sbi][:], in_=Fg[:, 0:G], func=AF.Sigmoid),
                             reads=[Fg_r], writes=[sigb_r[sbi]])
                        P.op("dve", lambda e, Fa=Fa, sbi=sbi, cb=cb: e.tensor_tensor(
                            out=U[:, cb, HALO:HALO + G], in0=Fa[:, 0:G], in1=sigb[sbi][:], op=ALU.mult),
                            reads=[Fa_r, sigb_r[sbi]], writes=[U_r[cb]])

                    def zph(g, cb):
                        Fz, Fz_r = Fb[cb % 4], Fb_r[cb % 4]
                        proj(Fz, Fz_r, 2 * E + cb * 128)
                        P.op("act", lambda e, Fz=Fz, cb=cb: e.activation(out=SZ[:, cb, :], in_=Fz[:, 0:G], func=AF.Silu),
                             reads=[Fz_r], writes=[SZ_r[cb]])

                    def conv_phase(g):
                        pend_stm = []
                        for cb in range(16):
                            Fc, Fc_r = Fb[cb % 2], Fb_r[cb % 2]
                            for j0 in range(0, KW, NDT):
                                nj = min(NDT, KW - j0)
                                di = dctr[0] % ND
                                dctr[0] += 1
                                P.op("dve", lambda e, di=di, cb=cb, j0=j0, nj=nj: e.tensor_tensor(
                                    out=Dg[di][:, 0:nj, :], in0=ident[:].unsqueeze(1).to_broadcast([128, nj, 128]),
                                    in1=dww[:, cb, j0:j0 + nj].unsqueeze(2).to_broadcast([128, nj, 128]), op=ALU.mult),
                                    reads=[const_r, cw_r], writes=[Dg_r[di]])

                                def cv(e, di=di, cb=cb, j0=j0, nj=nj, Fc=Fc):
                                    ins = None
                                    for n in range(nj):
                                        j = j0 + n
                                        ins = e.matmul(Fc[:, 0:G], lhsT=Dg[di][:, n, :], rhs=U[:, cb, j:j + G],
                                                       start=(j == 0), stop=(j == KW - 1))
                                    return ins
                                P.op("pe", cv, reads=[Dg_r[di], U_r[cb]], writes=[Fc_r])
                            ci = cb % 2
                            P.op("act", lambda e, cb=cb, Fc=Fc: e.activation(out=csb[:, cb, :], in_=Fc[:, 0:G], func=AF.Identity,
                                                                            bias=dwb[:, cb:cb + 1], scale=1.0),
                                 reads=[Fc_r, cw_r], writes=[csb_r[cb]])
                            P.op("act", lambda e, cb=cb, ci=ci: e.activation(out=csq[ci][:], in_=csb[:, cb, :], func=AF.Square),
                                 reads=[csb_r[cb]], writes=[cst_r[ci]])
                            P.op("act", lambda e, cb=cb, ci=ci: e.activation(out=chi[ci][:], in_=csb[:, cb, :], func=AF.Identity),
                                 reads=[csb_r[cb]], writes=[cst_r[ci]])

                            def stm(e, cb=cb, ci=ci):
                                e.matmul(Fsum[:, 0:G], lhsT=ones[:], rhs=chi[ci][:], start=(cb == 0), stop=(cb == 15))
                                return e.matmul(Fsq[:, 0:G], lhsT=ones[:], rhs=csq[ci][:], start=(cb == 0), stop=(cb == 15))
                            pend_stm.append((stm, ci))
                            if len(pend_stm) > 1:
                                f_, ci_ = pend_stm.pop(0)
                                P.op("pe", f_, reads=[cst_r[ci_], const_r], writes=[Fsum_r, Fsq_r])
                            if g + 1 < NG and cb // 4 < TPG:
                                if cb % 4 == 0:
                                    prep_tile(g + 1, cb // 4, part=1)
                                elif cb % 4 == 2:
                                    prep_tile(g + 1, cb // 4, part=2)
                        while pend_stm:
                            f_, ci_ = pend_stm.pop(0)
                            P.op("pe", f_, reads=[cst_r[ci_], const_r], writes=[Fsum_r, Fsq_r])
                        if g + 1 < NG:
                            P.op("pool", lambda e: e.tensor_copy(out=U[:, :, 0:HALO], in_=U[:, :, G:G + HALO]),
                                 reads=U_r, writes=U_r)
                        P.op("dve", lambda e: e.tensor_scalar(out=mean[:], in0=Fsum[:, 0:G], scalar1=1.0 / E, scalar2=None,
                                                              op0=ALU.mult), reads=[Fsum_r], writes=[stat_r])
                        P.op("dve", lambda e: e.tensor_tensor(out=msq[:], in0=mean[:], in1=mean[:], op=ALU.mult),
                             reads=[stat_r], writes=[stat_r])
                        P.op("dve", lambda e: e.scalar_tensor_tensor(out=rstdl[:], in0=Fsq[:, 0:G], scalar=1.0 / E, in1=msq[:],
                                                                     op0=ALU.mult, op1=ALU.subtract),
                             reads=[Fsq_r, stat_r], writes=[stat_r])
                        P.op("act", lambda e: e.activation(out=rstdl[:], in_=rstdl[:], func=AF.Sqrt, bias=epsc[:, 0:1], scale=1.0),
                             reads=[stat_r, pos_r], writes=[stat_r])
                        P.op("dve", lambda e: e.reciprocal(out=rstdl[:], in_=rstdl[:]), reads=[stat_r], writes=[stat_r])

                    def tail1(g, cb):
                        yi = cb % 2
                        P.op("dve", lambda e, cb=cb: e.tensor_tensor(out=csb[:, cb, :], in0=csb[:, cb, :], in1=mean[:],
                                                                     op=ALU.subtract),
                             reads=[csb_r[cb], stat_r], writes=[csb_r[cb]])
                        P.op("dve", lambda e, cb=cb: e.tensor_tensor(out=csb[:, cb, :], in0=csb[:, cb, :], in1=rstdl[:],
                                                                     op=ALU.mult),
                             reads=[csb_r[cb], stat_r], writes=[csb_r[cb]])
                        P.op("act", lambda e, cb=cb, yi=yi: e.activation(out=y1[yi][:], in_=csb[:, cb, :], func=AF.Sigmoid,
                                                                        bias=lnb[:, cb:cb + 1], scale=lng[:, cb:cb + 1]),
                             reads=[csb_r[cb], cw_r], writes=[y1_r[yi]])
                        P.op("act", lambda e, cb=cb: e.activation(out=csb[:, cb, :], in_=csb[:, cb, :], func=AF.Identity,
                                                                  bias=lnb[:, cb:cb + 1], scale=lng[:, cb:cb + 1]),
                             reads=[csb_r[cb], cw_r], writes=[csb_r[cb]])

                    def tail2(g, cb):
                        yi = cb % 2
                        P.op("dve", lambda e, cb=cb, yi=yi: e.tensor_tensor(out=csb[:, cb, :], in0=csb[:, cb, :], in1=y1[yi][:],
                                                                            op=ALU.mult),
                             reads=[csb_r[cb], y1_r[yi]], writes=[csb_r[cb]])
                        P.op("dve", lambda e, cb=cb: e.tensor_tensor(out=SZ[:, cb, :], in0=SZ[:, cb, :], in1=csb[:, cb, :],
                                                                     op=ALU.mult),
                             reads=[csb_r[cb], SZ_r[cb]], writes=[SZ_r[cb]])

                    def store_q(g, q):
                        for tt in range(TPG):
                            t = g * TPG + tt
                            P.dma("pool", f"ys_st{q}", ys[t].rearrange("p (k t) -> p k t", k=16)[:, 4 * q:4 * q + 4, :],
                                  SZ[:, 4 * q:4 * q + 4, tt * 128:(tt + 1) * 128], reads=SZ_r[4 * q:4 * q + 4],
                                  writes=[ysq_res[t][q]])

                    prep_g(0)
                    for cb in range(16):
                        ab(0, cb)
                    for g in range(NG):
                        for cb in range(16):
                            zph(g, cb)
                        conv_phase(g)
                        tail1(g, 0)
                        for cb in range(16):
                            if g + 1 < NG:
                                ab(g + 1, cb)
                            if cb + 1 < 16:
                                tail1(g, cb + 1)
                            tail2(g, cb)
                            if cb % 4 == 3:
                                store_q(g, cb // 4)
                P.barrier()
                P.flush()

            with ExitStack() as st:
                rp = pair and kind == "ret"
                send_x = pair and (li + 1 < len(layers)) and layers[li + 1][0] == "conv"
                Wout = sb(st, "Wout", [128, 16, D], BF16)
                Wout_r = Res("Wout")
                P.dma("pool", "wout", Wout[:], W["w_out"].rearrange("(kc p) n -> p kc n", p=128), writes=[Wout_r])
                if li + 1 < len(layers):
                    load_win(li + 1)
                NX = 3
                xr = [sb(st, f"xr{i}", [128, D], F32) for i in range(NX)]
                xr_r = [Res(f"xr{i}") for i in range(NX)]
                fin = do_final and is_last
                if not rp:
                    yt = [sb(st, f"yt{i}", [128, 16, 128], BF16) for i in range(2)]
                    yt_r = [Res(f"yt{i}") for i in range(2)]
                    Gb = [ps(st, f"G{i}", [128, 2, 512], F32) for i in range(2)]
                    Gb_r = [Res(f"G{i}") for i in range(2)]
                else:
                    olbB = sb(st, "olbB", [128, 2, E], F32)
                    olbB_r = [Res("olbB0"), Res("olbB1")]
                    qsb = [sb(st, f"qsb{i}", [128, 8, 128], BF16) for i in range(2)]
                    qsb_r = [Res(f"qsb{i}") for i in range(2)]
                    sgb = [sb(st, f"sgb{i}", [128, E], BF16) for i in range(2)]
                    sgb_r = [Res(f"sgb{i}") for i in range(2)]
                    ogBs = [sb(st, f"ogB{i}", [128, E], BF16) for i in range(2)]
                    ogBs_r = [Res("ogB0"), Res("ogB1")]
                    ogB, ogB_r = ogBs[0], ogBs_r[0]
                    ogTB = sb(st, "ogTB", [128, 16, 128], BF16)
                    ogTB_r = Res("ogTB")
                    SinBf = sb(st, "SinBf", [128, 8, 512], BF16)
                    SinBf_r = Res("SinBf")
                    ssqB = sb(st, "ssqB", [128, H], F32)
                    rs4B = sb(st, "rs4B", [128, H], F32)
                    ssqB_r = Res("ssqB")
                    Cb = [ps(st, f"C{i}", [128, 512], F32) for i in range(H)]
                    Cb_r = [Res(f"C{i}") for i in range(H)]
                    PTB = [ps(st, f"PTB{i}", [128, 8, 128], BF16) for i in range(2)]
                    PTB_r = [Res(f"PTB{i}") for i in range(2)]
                    Gb = [ps(st, "G0", [128, 2, 512], F32)]
                    Gb_r = [Res("G0")]
                    P.dma("sp", "sin_ld", olbB[:].rearrange("p a e -> p (a e)"), ccs_out[li].ap()[0:128, :],
                          reads=[ccs_r], writes=olbB_r)
                    P.op("dve", lambda e: e.tensor_scalar(out=SinBf[:].rearrange("p i e -> p (i e)"),
                                                          in0=olbB[:].rearrange("p a e -> p (a e)"),
                                                          scalar1=flag[:, 0:1], scalar2=None, op0=ALU.mult),
                         reads=olbB_r + [const_r], writes=[SinBf_r])
                if fin:
                    gfin = sb(st, "gfinS", [128, D], F32)
                    gfin_r = Res("gfin")
                    P.dma("sp", "lc_f", gfin[:], gfin_d, writes=[gfin_r])
                    ss = sb(st, "ssB", [128, 1], F32)
                    rstd = sb(st, "rstdB", [128, 1], F32)
                    small_r = Res("smallB")
                    junk = sb(st, "junkB", [128, D], BF16)
                    fj, fj_r = junk[:], small_r

                def loads(t):
                    b, b3 = t % 2, t % NX
                    if not rp:
                        P.dma("sp", f"yt{b}", yt[b][:].rearrange("p k t -> p (k t)"), ys[t], reads=[ys_res[t]] + ysq_res[t], writes=[yt_r[b]])
                    else:
                        P.dma("sp", f"olB{b}", olbB[:, b, :], ol[t], reads=[ol_res[t]], writes=[olbB_r[b]])
                        P.dma("sp", f"qsB{b}", qsb[b][:].rearrange("p k t -> p (k t)"), qs[t], reads=[qs_res[t]], writes=[qsb_r[b]])
                        P.dma("sp", f"sgB{b}", sgb[b][:], sgs[t], reads=[sgs_res[t]], writes=[sgb_r[b]])
                    P.dma("sp", f"xr{b3}", xr[b3][:], x_src[t * 128:(t + 1) * 128, :], reads=[xs_res[t]], writes=[xr_r[b3]])
                def B1pe(t):
                    b = t % 2
                    for h in range(H):
                        def cm(e, h=h, b=b):
                            e.matmul(Cb[h][:], lhsT=qsb[b][:, 2 * h, :], rhs=SinBf[:, 2 * h, :], start=True, stop=False)
                            return e.matmul(Cb[h][:], lhsT=qsb[b][:, 2 * h + 1, :], rhs=SinBf[:, 2 * h + 1, :],
                                            start=False, stop=True)
                        P.op("pe", cm, reads=[qsb_r[b], SinBf_r], writes=[Cb_r[h]])

                def B1(t):
                    b = t % 2
                    ogx, ogx_r = ogBs[b], ogBs_r[b]
                    P.op("dve", lambda e: e.memset(ssqB[:], 0.0), writes=[ssqB_r])
                    for h in range(H):
                        osl = olbB[:, b, h * 512:(h + 1) * 512]
                        P.op("dve", lambda e, h=h, osl=osl, t=t: e.scalar_tensor_tensor(
                            out=osl, in0=Cb[h][:], scalar=float(CDEC[h] ** t), in1=osl, op0=ALU.mult, op1=ALU.add),
                            reads=[Cb_r[h], olbB_r[b]], writes=[olbB_r[b]])
                        P.op("act", lambda e, h=h, osl=osl, ogx=ogx: e.activation(out=ogx[:, h * 512:(h + 1) * 512], in_=osl, func=AF.Square,
                                                                                  accum_out=ssqB[:, h:h + 1]),
                             reads=[olbB_r[b]], writes=[ssqB_r, ogx_r])
                    P.op("act", lambda e: e.activation(out=rs4B[:], in_=ssqB[:], func=AF.Sqrt,
                                                       bias=epsc[:, 0:1], scale=1.0 / DV),
                         reads=[ssqB_r, pos_r], writes=[ssqB_r])
                    P.op("dve", lambda e: e.reciprocal(out=rs4B[:], in_=rs4B[:]), reads=[ssqB_r], writes=[ssqB_r])
                    for h in range(H):
                        osl = olbB[:, b, h * 512:(h + 1) * 512]
                        P.op("dve", lambda e, h=h, osl=osl, b=b, ogx=ogx: e.scalar_tensor_tensor(
                            out=ogx[:, h * 512:(h + 1) * 512], in0=osl, scalar=rs4B[:, h:h + 1],
                            in1=sgb[b][:, h * 512:(h + 1) * 512], op0=ALU.mult, op1=ALU.mult),
                            reads=[olbB_r[b], ssqB_r, sgb_r[b]], writes=[ogx_r])

                mk_tabs = pre_tabs and li == 0
                if mk_tabs:
                    angB = sb(st, "angB", [128, 128], F32)
                    ang2B = sb(st, "ang2B", [128, 128], F32)
                    angB_r = Res("angB")
                    tabB = [sb(st, f"tabB{i}", [128, 2, 128], F32) for i in range(2)]
                    tabB_r = [Res("tabB0"), Res("tabB1")]
                loads(0)
                if NT > 1:
                    loads(1)
                if rp:
                    B1pe(0)
                    B1(0)
                for t in range(NT):
                    b, b3 = t % 2, t % NX
                    if rp:
                        if t + 2 < NT:
                            loads(t + 2)
                        if t + 1 < NT:
                            B1pe(t + 1)
                    if not rp:
                        Gt, Gt_r = Gb[b], Gb_r[b]

                        def mm(e, b=b, Gt=Gt):
                            ins = None
                            for n in range(2):
                                for kc in range(16):
                                    ins = e.matmul(Gt[:, n, :], lhsT=yt[b][:, kc, :], rhs=Wout[:, kc, n * 512:(n + 1) * 512],
                                                   start=(kc == 0), stop=(kc == 15))
                            return ins
                        P.op("pe", mm, reads=[yt_r[b], Wout_r], writes=[Gt_r])
                        if t + 2 < NT:
                            loads(t + 2)
                    else:
                        ogx, ogx_r = ogBs[b], ogBs_r[b]
                        for half in range(2):
                            def tro(e, half=half, ogx=ogx):
                                ins = None
                                for c in range(8):
                                    cc = half * 8 + c
                                    ins = e.transpose(out=PTB[half][:, c, :], in_=ogx[:, cc * 128:(cc + 1) * 128],
                                                      identity=ident[:])
                                return ins
                            P.op("pe", tro, reads=[ogx_r, const_r], writes=[PTB_r[half]])
                            if half == 0:
                                P.op("act", lambda e: e.activation(out=ogTB[:, 0:8, :], in_=PTB[0][:], func=AF.Copy),
                                     reads=[PTB_r[0]], writes=[ogTB_r])
                            else:
                                P.op("dve", lambda e: e.tensor_copy(out=ogTB[:, 8:16, :], in_=PTB[1][:]),
                                     reads=[PTB_r[1]], writes=[ogTB_r])
                        Gt, Gt_r = Gb[0], Gb_r[0]

                        def mm(e, Gt=Gt):
                            ins = None
                            for n in range(2):
                                for kc in range(16):
                                    ins = e.matmul(Gt[:, n, :], lhsT=ogTB[:, kc, :], rhs=Wout[:, kc, n * 512:(n + 1) * 512],
                                                   start=(kc == 0), stop=(kc == 15))
                            return ins
                        P.op("pe", mm, reads=[ogTB_r, Wout_r], writes=[Gt_r])
                        if t + 1 < NT:
                            B1(t + 1)
                    P.op("dve", lambda e, b3=b3, Gt=Gt: e.tensor_tensor(
                        out=xr[b3][:], in0=xr[b3][:], in1=Gt[:].rearrange("p n c -> p (n c)"), op=ALU.add),
                        reads=[Gt_r, xr_r[b3]], writes=[xr_r[b3]])
                    if mk_tabs:
                        emit_tables(t, angB, ang2B, angB_r, (tabB[b][:, 0, :], tabB[b][:, 1, :]), tabB_r[b])
                        P.dma("pool", f"tab_st{b}", tabs_d[t], tabB[b][:].rearrange("p a f -> p (a f)"),
                              reads=[tabB_r[b]], writes=[tabs_res[t]])
                    if send_x and t == NT - 1:
                        P.dma("pool", "ccx_st", ccx_in.ap(), xr[b3][:], reads=[xr_r[b3]], writes=[ccx_r])
                        P.collective("ccx", "AllGather", RG, ccx_in.ap().opt(), ccx_out.ap().opt(), reads=[ccx_r], writes=[ccx_r])
                    if fin:
                        P.op("dve", lambda e: e.memset(ss[:], 0.0), writes=[small_r])
                        P.op("act", lambda e, b3=b3: e.activation(out=fj, in_=xr[b3][:], func=AF.Square, accum_out=ss[:, 0:1]),
                             reads=[xr_r[b3]], writes=[small_r, fj_r])
                        P.op("act", lambda e: e.activation(out=rstd[:], in_=ss[:], func=AF.Sqrt, bias=epsc[:, 0:1], scale=1.0 / D),
                             reads=[small_r, pos_r], writes=[small_r])
                        P.op("dve", lambda e: e.reciprocal(out=rstd[:], in_=rstd[:]), reads=[small_r], writes=[small_r])
                        P.op("dve", lambda e, b3=b3: e.scalar_tensor_tensor(
                            out=xr[b3][:], in0=xr[b3][:], scalar=rstd[:, 0:1], in1=gfin[:], op0=ALU.mult, op1=ALU.mult),
                            reads=[xr_r[b3], small_r, gfin_r], writes=[xr_r[b3]])
                    P.dma("pool", f"xst{b3}", x_dst[t * 128:(t + 1) * 128, :], xr[b3][:], reads=[xr_r[b3]],
                          writes=[xs_res[t]])
                if is_last:
                    P.final_wait()
                else:
                    P.barrier()
                P.flush()
    return nc


def _consts():
    bf = ml_dtypes.bfloat16
    ident = np.eye(128, dtype=np.float32).astype(bf)
    ones = np.ones((128, 128), dtype=np.float32).astype(bf)
    inv_freq = (10000.0 ** (-np.arange(128, dtype=np.float32) / 128.0)).astype(np.float32)
    invf = np.broadcast_to(inv_freq[None, :], (128, 128)).astype(np.float32).copy()
    idx = np.arange(128, dtype=np.float64)
    maskT = np.zeros((128, H, 128), np.float64)
    qdT = np.zeros((128, H, 128), np.float64)
    kd = np.zeros((128, H), np.float64)
    for h in range(H):
        g = GAMMA[h]
        causal = (idx[None, :] >= idx[:, None]).astype(np.float64)
        maskT[:, h, :] = (g ** (-(idx[:, None] + 1.0))) * causal * (DK ** -0.5)
        qdT[:, h, :] = (g ** (idx[None, :] + 1.0))
        kd[:, h] = (g ** (127.0 - idx)) * (DK ** -0.5)
    return dict(ident=ident, ones=ones, invf=invf,
                maskT=maskT.reshape(128, H * 128).astype(np.float32),
                qdT=qdT.reshape(128, H * 128).astype(np.float32), kd=kd.astype(np.float32))


def _fm(vec, nblk):
    return np.ascontiguousarray(np.asarray(vec, np.float32).reshape(nblk, 128).T)


def _layer_inputs(li, kind, slot, inp):
    d = {}
    if kind == "conv":
        d[f"gfm{li}"] = _fm(inp["conv_norm"][slot], 8)
        d[f"w_in{li}"] = np.ascontiguousarray(inp["conv_w_in"][slot], dtype=np.float32)
        d[f"w_out{li}"] = np.ascontiguousarray(inp["conv_w_out"][slot], dtype=np.float32)
        dw = np.asarray(inp["conv_dw_w"][slot], np.float32)
        d[f"dww{li}"] = np.ascontiguousarray(dw.reshape(KW, 16, 128).transpose(2, 1, 0).reshape(128, 16 * KW))
        d[f"dwb{li}"] = _fm(inp["conv_dw_b"][slot], 16)
        d[f"lng{li}"] = _fm(inp["conv_ln_g"][slot], 16)
        d[f"lnb{li}"] = _fm(inp["conv_ln_b"][slot], 16)
    else:
        d[f"gfm{li}"] = _fm(inp["ret_norm"][slot], 8)
        d[f"w_in{li}"] = np.ascontiguousarray(inp["ret_w_in"][slot], dtype=np.float32)
        d[f"w_out{li}"] = np.ascontiguousarray(inp["ret_w_out"][slot], dtype=np.float32)
    return d


_PROG_CACHE = {}


def run_layers(x, positions, inp, layer_ids, do_final, n_cores, pair=False):
    B, S, _ = x.shape
    T = S // 2 if pair else S
    layers = [("conv" if i % 2 == 0 else "ret", i // 2) for i in layer_ids]
    key = (T, tuple(layers), do_final, pair, n_cores)
    if key not in _PROG_CACHE:
        _PROG_CACHE[key] = build_program(T, layers, True, True, do_final, pair=pair, n_cores=n_cores)
    nc = _PROG_CACHE[key]
    shared = dict(_consts())
    shared["gfin"] = np.ascontiguousarray(np.broadcast_to(np.asarray(inp["final_norm"], np.float32)[None, :], (128, D)))
    for li, (kind, slot) in enumerate(layers):
        shared.update(_layer_inputs(li, kind, slot, inp))
    in_maps = []
    for c in range(n_cores):
        m = dict(shared)
        if pair:
            b, hf = (c // 2) % B, c % 2
            lo = hf * T
            m["xprev0"] = (np.ascontiguousarray(x[b, lo - 128:lo], dtype=np.float32) if hf == 1
                           else np.zeros((128, D), np.float32))
            m["flag"] = np.full((128, 1), float(hf), np.float32)
        else:
            b, lo = c % B, 0
        m["x"] = np.ascontiguousarray(x[b, lo:lo + T], dtype=np.float32)
        m["posT"] = np.ascontiguousarray(np.asarray(positions[b, lo:lo + T], np.int32).reshape(T // 128, 128).T)
        in_maps.append(m)
    res = run_bass_kernel_spmd(nc, in_maps, core_ids=list(range(n_cores)))
    if pair:
        return np.stack([np.concatenate([res.results[2 * b]["out"], res.results[2 * b + 1]["out"]], axis=0)
                         for b in range(B)], axis=0)
    return np.stack([res.results[b]["out"] for b in range(B)], axis=0)


def kernel(x, positions, conv_norm, conv_w_in, conv_dw_w, conv_dw_b, conv_ln_g, conv_ln_b, conv_w_out,
           ret_norm, ret_w_in, ret_w_out, final_norm):
    inp = dict(conv_norm=np.asarray(conv_norm), conv_w_in=np.asarray(conv_w_in), conv_dw_w=np.asarray(conv_dw_w),
               conv_dw_b=np.asarray(conv_dw_b), conv_ln_g=np.asarray(conv_ln_g), conv_ln_b=np.asarray(conv_ln_b),
               conv_w_out=np.asarray(conv_w_out), ret_norm=np.asarray(ret_norm), ret_w_in=np.asarray(ret_w_in),
               ret_w_out=np.asarray(ret_w_out), final_norm=np.asarray(final_norm))
    x = np.asarray(x, dtype=np.float32)
    positions = np.asarray(positions)
    out = run_layers(x, positions, inp, [0, 1, 2, 3], True, 8, pair=True)
    return out.astype(np.float32)
```

```python
import math
from contextlib import ExitStack

import numpy as np
import ml_dtypes

import concourse.bass as bass
import concourse.mybir as mybir
from concourse.bass_utils import run_bass_kernel_spmd

F32 = mybir.dt.float32
BF16 = mybir.dt.bfloat16
I32 = mybir.dt.int32
AF = mybir.ActivationFunctionType
ALU = mybir.AluOpType

D = 1024
E = 2048
H = 4
DK = 256
DV = 512
KW = 31
HALO = KW - 1
EPS = 1e-6
TWO_PI = 2.0 * math.pi
MAGIC = 12582912.0
CW1 = 6.28125
CW2 = TWO_PI - 6.28125
PI_LO = 3.1415925
GAMMA = [1.0 - 2.0 ** (-5.0 - h) for h in range(H)]
CDEC = [g ** 128 for g in GAMMA]

ENGS = ("pe", "act", "dve", "pool", "sp")
BLK = dict(pe="tensor", act="scalar", dve="vector", pool="gpsimd", sp="sync")
import os as _os
SAME_ENGINE_SYNC = _os.environ.get("K_SES", "1") == "1"


class Res:
    __slots__ = ("name", "w", "r")

    def __init__(self, name):
        self.name = name
        self.w = None
        self.r = []


class Prog:
    def __init__(self, nc, stack):
        self.nc = nc
        self.stack = stack
        self.sem = {e: stack.enter_context(nc.semaphore("sem_" + e)) for e in ENGS}
        self.cnt = {e: 0 for e in ENGS}
        self.dsem = {}
        self.dcnt = {}
        self.persist = set()
        self.cc_sems = set()
        self.waited = {e: {} for e in ENGS}
        self.streams = {e: [] for e in ENGS}
        self.extra = {e: set() for e in ENGS}

    def _semh(self, k):
        return self.sem[k] if k in self.sem else self.dsem[k]

    def _emit(self, eng, fn, reads, writes, ev):
        deps = set(self.extra[eng])
        self.extra[eng] = set()
        for r in reads:
            if r.w is not None:
                deps.add(r.w)
        for w in writes:
            if w.w is not None:
                deps.add(w.w)
            deps.update(w.r)
        best = {}
        for k, v in deps:
            if v > best.get(k, 0):
                best[k] = v
        waits = []
        wd = self.waited[eng]
        for k, v in best.items():
            if k == eng and (eng == "pe" or not SAME_ENGINE_SYNC):
                continue
            if wd.get(k, 0) >= v:
                continue
            wd[k] = v
            waits.append((k, v))
        self.streams[eng].append((waits, fn, ev))
        if ev is not None:
            for r in reads:
                r.r.append(ev)
            for w in writes:
                w.w = ev
                w.r = []

    def op(self, eng, fn, reads=(), writes=()):
        self.cnt[eng] += 1
        ev = (eng, self.cnt[eng])
        self._emit(eng, fn, reads, writes, ev)
        return ev

    def dma(self, q, semname, out, in_, reads=(), writes=(), persist=False):
        if semname not in self.dsem:
            self.dsem[semname] = self.stack.enter_context(self.nc.semaphore("d_" + semname))
            self.dcnt[semname] = 0
            if persist:
                self.persist.add(semname)
        self.dcnt[semname] += 16
        ev = (semname, self.dcnt[semname])
        self._emit(q, lambda e: e.dma_start(out=out, in_=in_), reads, writes, ev)
        return ev

    def collective(self, semname, kind, rg, in_ap, out_ap, reads=(), writes=()):
        if semname not in self.dsem:
            self.dsem[semname] = self.stack.enter_context(self.nc.semaphore("c_" + semname))
            self.dcnt[semname] = 0
            self.cc_sems.add(semname)
        self.dcnt[semname] += 1
        ev = (semname, self.dcnt[semname])
        self._emit("pool", lambda e: e.collective_compute(kind, ALU.bypass, replica_groups=rg, ins=[in_ap], outs=[out_ap]),
                   reads, writes, ev)
        return ev

    def barrier(self, include_persist=False):
        tg = set((e, self.cnt[e]) for e in ENGS if self.cnt[e] > 0)
        for k, v in self.dcnt.items():
            if v > 0 and (include_persist or k not in self.persist):
                tg.add((k, v))
        for e in ENGS:
            self.extra[e] |= tg

    def final_wait(self):
        self.barrier(include_persist=True)
        self._emit("sp", None, (), (), None)

    def flush(self):
        with self.nc.Block() as block:
            for e in ENGS:
                stream = self.streams[e]
                if not stream:
                    continue

                def body(engobj, stream=stream):
                    for waits, fn, ev in stream:
                        for k, v in waits:
                            engobj.wait_ge(self._semh(k), v)
                        if fn is None:
                            continue
                        ins = fn(engobj)
                        if ev[0] in self.cc_sems:
                            ins.then_inc(self._semh(ev[0]))
                        else:
                            ins.then_inc(self._semh(ev[0]), 16 if ev[0] in self.dsem else 1)

                getattr(block, BLK[e])(body)
        self.streams = {e: [] for e in ENGS}


def build_program(T, layers, first_ext, last_ext, do_final, pair=False, n_cores=8):
    NT = T // 128
    nc = bass.Bass("TRN2", target_bir_lowering=False)
    dr = {}

    def ext_in(name, shape, dt):
        dr[name] = nc.dram_tensor(name, list(shape), dt, kind="ExternalInput").ap()
        return dr[name]

    x_ext = ext_in("x", [T, D], F32)
    posT = ext_in("posT", [128, NT], I32)
    ident_d = ext_in("ident", [128, 128], BF16)
    ones_d = ext_in("ones", [128, 128], BF16)
    invf_d = ext_in("invf", [128, 128], F32)
    maskT_d = ext_in("maskT", [128, H * 128], F32)
    qdT_d = ext_in("qdT", [128, H * 128], F32)
    kd_d = ext_in("kd", [128, H], F32)
    gfin_d = ext_in("gfin", [128, D], F32)
    lw = []
    for li, (kind, slot) in enumerate(layers):
        d = {}
        d["gfm"] = ext_in(f"gfm{li}", [128, 8], F32)
        d["w_in"] = ext_in(f"w_in{li}", [D, 3 * E], F32)
        d["w_out"] = ext_in(f"w_out{li}", [E, D], F32)
        if kind == "conv":
            d["dww"] = ext_in(f"dww{li}", [128, 16 * KW], F32)
            d["dwb"] = ext_in(f"dwb{li}", [128, 16], F32)
            d["lng"] = ext_in(f"lng{li}", [128, 16], F32)
            d["lnb"] = ext_in(f"lnb{li}", [128, 16], F32)
        lw.append(d)
    out_ext = nc.dram_tensor("out", [T, D], F32, kind="ExternalOutput").ap()
    xs = nc.dram_tensor("xs", [T, D], F32, kind="Internal").ap()
    ys = nc.dram_tensor("ys", [NT, 128, 16 * 128], BF16, kind="Internal").ap()

    if pair:
        RG = [[2 * i, 2 * i + 1] for i in range(n_cores // 2)]
        xprev0_d = ext_in("xprev0", [128, D], F32)
        flag_d = ext_in("flag", [128, 1], F32)
        ol = nc.dram_tensor("ol", [NT, 128, E], F32, kind="Internal").ap()
        qs = nc.dram_tensor("qs", [NT, 128, 8 * 128], BF16, kind="Internal").ap()
        sgs = nc.dram_tensor("sgs", [NT, 128, E], BF16, kind="Internal").ap()
        ol_res = [Res(f"ol{t}") for t in range(NT)]
        qs_res = [Res(f"qs{t}") for t in range(NT)]
        sgs_res = [Res(f"sgs{t}") for t in range(NT)]
        ccx_in = nc.dram_tensor("ccx_in", [128, D], F32)
        ccx_out = nc.dram_tensor("ccx_out", [256, D], F32)
        ccs_in = [nc.dram_tensor(f"ccs_in{i}", [128, 8 * 512], F32) for i in range(len(layers))]
        ccs_out = [nc.dram_tensor(f"ccs_out{i}", [256, 8 * 512], F32) for i in range(len(layers))]
        ccx_r, ccs_r = Res("ccx"), Res("ccs")
    xs_res = [Res(f"xs{t}") for t in range(NT)]
    ys_res = [Res(f"ys{t}") for t in range(NT)]
    ysq_res = [[Res(f"ys{t}_{q}") for q in range(4)] for t in range(NT)]

    with ExitStack() as top:
        P = Prog(nc, top)

        uniq = [0]

        def sb(stack, name, shape, dt):
            uniq[0] += 1
            return stack.enter_context(nc.sbuf_tensor(f"{name}_{uniq[0]}", list(shape), dt))

        def ps(stack, name, shape, dt):
            uniq[0] += 1
            return stack.enter_context(nc.psum_tensor(f"{name}_{uniq[0]}", list(shape), dt))

        Win = sb(top, "Win", [128, 8, 3 * E], BF16)
        Win_rs = [Res(f"Win{k}") for k in range(8)]
        ident = sb(top, "identS", [128, 128], BF16)
        ones = sb(top, "onesS", [128, 128], BF16)
        invf = sb(top, "invfS", [128, 128], F32)
        maskT = sb(top, "maskTS", [128, H * 128], F32)
        qdT = sb(top, "qdTS", [128, H, 128], F32)
        kd = sb(top, "kdS", [128, H], F32)
        posi = sb(top, "posi", [128, NT], I32)
        posf = sb(top, "posf", [128, NT], F32)
        negpi = sb(top, "negpi", [128, 1], F32)
        const_r = Res("consts")
        pos_r = Res("pos")

        for dst, src in ((ident, ident_d), (ones, ones_d), (invf, invf_d), (maskT, maskT_d),
                         (kd, kd_d), (posi, posT)):
            P.dma("sp", "const", dst[:], src, writes=[const_r])
        P.dma("sp", "const", qdT[:].rearrange("p h t -> p (h t)"), qdT_d, writes=[const_r])
        P.op("dve", lambda e: e.tensor_copy(out=posf[:], in_=posi[:]), reads=[const_r], writes=[pos_r])
        P.op("dve", lambda e: e.memset(negpi[:], math.pi), writes=[pos_r])
        epsc = sb(top, "epsc", [128, 1], F32)
        if pair:
            flag = sb(top, "flagS", [128, 1], F32)
            P.dma("sp", "const", flag[:], flag_d, writes=[const_r])
        P.op("dve", lambda e: e.memset(epsc[:], EPS), writes=[pos_r])

        def load_win(li, after=()):
            src = lw[li]["w_in"].rearrange("(kc p) n -> p kc n", p=128)
            for kc in range(8):
                P.dma("pool", "win", Win[:, kc, :], src[:, kc, :], reads=list(after), writes=[Win_rs[kc]], persist=True)

        load_win(0)

        def rms_to_hT(st, xt, xt_r, junk, ss, rstd, xn, small_r, xn_r, PT, PT_r, hT_out, hT_r, gfm, gfm_r, junk_r=None,
                      part=3):
            if part & 1:
                rms_chain(xt, xt_r, junk, ss, rstd, xn, small_r, xn_r, junk_r)
            if part & 2:
                rms_tr(xn, xn_r, PT, PT_r, hT_out, hT_r, gfm, gfm_r)

        def rms_chain(xt, xt_r, junk, ss, rstd, xn, small_r, xn_r, junk_r):
            P.op("dve", lambda e: e.memset(ss[:], 0.0), writes=[small_r])
            P.op("act", lambda e: e.activation(out=junk[:], in_=xt[:], func=AF.Square, accum_out=ss[:, 0:1]),
                 reads=[xt_r], writes=[small_r] + ([junk_r] if junk_r is not None else []))
            P.op("act", lambda e: e.activation(out=rstd[:], in_=ss[:], func=AF.Sqrt, bias=epsc[:, 0:1], scale=1.0 / D),
                 reads=[small_r, pos_r], writes=[small_r])
            P.op("dve", lambda e: e.reciprocal(out=rstd[:], in_=rstd[:]), reads=[small_r], writes=[small_r])
            P.op("act", lambda e: e.activation(out=xn[:], in_=xt[:], func=AF.Identity, scale=rstd[:, 0:1]),
                 reads=[xt_r, small_r], writes=[xn_r])

        def rms_tr(xn, xn_r, PT, PT_r, hT_out, hT_r, gfm, gfm_r):
            def tr(e):
                ins = None
                for kc in range(8):
                    ins = e.transpose(out=PT[:, kc, :], in_=xn[:, kc * 128:(kc + 1) * 128], identity=ident[:])
                return ins
            P.op("pe", tr, reads=[xn_r, const_r], writes=[PT_r])
            P.op("dve", lambda e: e.tensor_tensor(out=hT_out, in0=PT[:],
                                                  in1=gfm[:].unsqueeze(2).to_broadcast([128, 8, 128]),
                                                  op=ALU.mult), reads=[PT_r, gfm_r], writes=[hT_r])

        def emit_tables(t, ang, ang2, ang_r, tabs_out, out_r):
            for tab, off in ((tabs_out[0], 0.0), (tabs_out[1], 0.25)):
                P.op("dve", lambda e, t=t: e.tensor_scalar(out=ang[:], in0=invf[:], scalar1=posf[:, t:t + 1],
                                                           scalar2=None, op0=ALU.mult),
                     reads=[const_r, pos_r], writes=[ang_r])
                P.op("dve", lambda e, off=off: e.tensor_scalar(out=ang2[:], in0=ang[:], scalar1=1.0 / TWO_PI,
                                                               scalar2=off, op0=ALU.mult, op1=ALU.add),
                     reads=[ang_r], writes=[ang_r])
                P.op("dve", lambda e: e.tensor_scalar(out=ang2[:], in0=ang2[:], scalar1=MAGIC, scalar2=None,
                                                      op0=ALU.add), reads=[ang_r], writes=[ang_r])
                P.op("dve", lambda e: e.tensor_scalar(out=ang2[:], in0=ang2[:], scalar1=-MAGIC, scalar2=None,
                                                      op0=ALU.add), reads=[ang_r], writes=[ang_r])
                for cw in (CW1, CW2):
                    P.op("dve", lambda e, cw=cw: e.scalar_tensor_tensor(out=ang[:], in0=ang2[:], scalar=-cw, in1=ang[:],
                                                                        op0=ALU.mult, op1=ALU.add),
                         reads=[ang_r], writes=[ang_r])
                P.op("dve", lambda e, off=off: e.tensor_scalar(out=ang[:], in0=ang[:], scalar1=off * TWO_PI,
                                                               scalar2=PI_LO, op0=ALU.add, op1=ALU.min),
                     reads=[ang_r], writes=[ang_r])
                P.op("dve", lambda e: e.tensor_scalar(out=ang[:], in0=ang[:], scalar1=-PI_LO, scalar2=None,
                                                      op0=ALU.max), reads=[ang_r], writes=[ang_r])
                P.op("act", lambda e, tab=tab: e.activation(out=tab, in_=ang[:], func=AF.Sin),
                     reads=[ang_r], writes=[out_r])

        pre_tabs = pair and layers[0][0] == "conv" and any(k == "ret" for k, _ in layers)
        if pre_tabs:
            tabs_d = nc.dram_tensor("tabs", [NT, 128, 256], F32, kind="Internal").ap()
            tabs_res = [Res(f"tabs{t}") for t in range(NT)]

        for li, (kind, slot) in enumerate(layers):
            is_first = li == 0
            is_last = li == len(layers) - 1
            x_src = x_ext if (is_first and first_ext) else xs
            x_dst = out_ext if (is_last and last_ext) else xs
            W = lw[li]

            with ExitStack() as st:
                gfm = sb(st, "gfm", [128, 8], F32)
                gfm_r = Res("gfm")
                P.dma("sp", "lc_g", gfm[:], W["gfm"], writes=[gfm_r])
                xin = [sb(st, f"xin{i}", [128, D], F32) for i in range(2)]
                xin_r = [Res(f"xin{i}") for i in range(2)]
                junk = sb(st, "junk", [128, D], BF16) if kind == "ret" else None
                ss = sb(st, "ss", [128, 1], F32)
                rstd = sb(st, "rstd", [128, 1], F32)
                small_r = Res("small")
                xn = sb(st, "xn", [128, D], BF16)
                xn_r = Res("xn")
                PT = [ps(st, f"PT{i}", [128, 8, 128], BF16) for i in range(2)]
                PT_r = [Res(f"PT{i}") for i in range(2)]

                if kind == "ret":
                    NF = 6
                    Fb = [ps(st, f"F{i}", [128, 512], F32) for i in range(NF)]
                    Fb_r = [Res(f"F{i}") for i in range(NF)]
                    hT = [sb(st, f"hT{i}", [128, 8, 128], BF16) for i in range(2)]
                    hT_r = [Res(f"hT{i}") for i in range(2)]
                    ang = sb(st, "ang", [128, 128], F32)
                    ang2 = sb(st, "ang2", [128, 128], F32)
                    ang3 = sb(st, "ang3", [128, 128], F32)
                    cstab = [sb(st, f"cstab{i}", [128, 2, 128], F32) for i in range(2)]
                    sint = [cstab[i][:, 0, :] for i in range(2)]
                    cost = [cstab[i][:, 1, :] for i in range(2)]
                    ang_r = Res("ang")
                    cs_r = [Res("cs0"), Res("cs1")]
                    tA = sb(st, "tA", [128, 2, 128], F32)
                    tB = sb(st, "tB", [128, 2, 128], F32)
                    tmp_r = Res("tmp")
                    qrot = sb(st, "qrot", [128, 1024], BF16)
                    krot = sb(st, "krot", [128, 1024], BF16)
                    qrot_r, krot_r = Res("qrot"), Res("krot")
                    khat = [sb(st, f"khat{i}", [128, H, DK], BF16) for i in range(2)]
                    khat_r = [Res("khat0"), Res("khat1")]
                    qT = [sb(st, f"qT{i}", [128, 8, 128], BF16) for i in range(2)]
                    kT = [sb(st, f"kT{i}", [128, 8, 128], BF16) for i in range(2)]
                    qT_r, kT_r = [Res("qT0"), Res("qT1")], [Res("kT0"), Res("kT1")]
                    v = [sb(st, f"v{i}", [128, E], BF16) for i in range(2)]
                    sg = [sb(st, f"sg{i}", [128, E], BF16) for i in range(2)]
                    v_r = [[Res(f"v{i}_{h}") for h in range(H)] for i in range(2)]
                    sg_r = [[Res(f"sg{i}_{h}") for h in range(H)] for i in range(2)]
                    Pm = sb(st, "Pm", [128, H * 128], BF16)
                    Pm_r = Res("Pm")
                    S = sb(st, "S", [128, 8, 512], F32)
                    Sbf = sb(st, "Sbf", [128, 8, 512], BF16)
                    S_r = [Res(f"S{i}") for i in range(8)]
                    Sbf_r = [Res(f"Sbf{i}") for i in range(8)]
                    ssq = sb(st, "ssq", [128, H], F32)
                    rs4 = sb(st, "rs4", [128, H], F32)
                    ssq_r = Res("ssq")
                    if not pair:
                        og = sb(st, "og", [128, E], BF16)
                        og_r = Res("og")
                        ogT = [sb(st, f"ogT{i}", [128, 16, 128], BF16) for i in range(2)]
                        ogT_r = [Res(f"ogT{i}") for i in range(2)]
                    else:
                        olb = [sb(st, f"olb{i}", [128, E], F32) for i in range(2)]
                        olb_r = [Res(f"olb{i}") for i in range(2)]

                    P.op("dve", lambda e: e.memset(S[:], 0.0), writes=S_r)
                    P.op("pool", lambda e: e.memset(Sbf[:], 0.0), writes=Sbf_r)
                    fctr = [0]

                    def load_x(t):
                        nb = t % 2
                        P.dma("sp", f"xin{nb}", xin[nb][:], x_src[t * 128:(t + 1) * 128, :],
                              reads=[xs_res[t]], writes=[xin_r[nb]])

                    def tables(t):
                        b = t % 2
                        if pre_tabs:
                            P.dma("sp", f"tab_ld{b}", cstab[b][:].rearrange("p a f -> p (a f)"), tabs_d[t],
                                  reads=[tabs_res[t]], writes=[cs_r[b]])
                        else:
                            emit_tables(t, ang, ang2, ang_r, (sint[b], cost[b]), cs_r[b])

                    def prep(t):
                        b = t % 2
                        tables(t)
                        rms_to_hT(st, xin[b], xin_r[b], junk, ss, rstd, xn, small_r, xn_r,
                                  PT[0], PT_r[0], hT[b][:], hT_r[b], gfm, gfm_r)
                        if t + 2 < NT:
                            load_x(t + 2)

                    def stage1(t):
                        b = t % 2
                        for n in (0, 4, 8, 1, 5, 9, 2, 6, 10, 3, 7, 11):
                            fi = fctr[0] % 4
                            fctr[0] += 1
                            F_, F_r = Fb[fi], Fb_r[fi]

                            def mm(e, n=n, F_=F_, b=b):
                                ins = None
                                for kc in range(8):
                                    ins = e.matmul(F_[:], lhsT=hT[b][:, kc, :], rhs=Win[:, kc, n * 512:(n + 1) * 512],
                                                   start=(kc == 0), stop=(kc == 7))
                                return ins
                            P.op("pe", mm, reads=[hT_r[b]] + Win_rs, writes=[F_r])
                            if n < 4:
                                dst, dst_r = (qrot, qrot_r) if n < 2 else (krot, krot_r)
                                c0 = (n % 2) * 512
                                Fv = F_[:].rearrange("p (h two f) -> p h two f", h=2, two=2)
                                t1, t2 = Fv[:, :, 0, :], Fv[:, :, 1, :]
                                dv_ = dst[:, c0:c0 + 512].rearrange("p (h two f) -> p h two f", h=2, two=2)
                                o1, o2 = dv_[:, :, 0, :], dv_[:, :, 1, :]
                                cb_ = cost[b].unsqueeze(1).to_broadcast([128, 2, 128])
                                sb_ = sint[b].unsqueeze(1).to_broadcast([128, 2, 128])
                                rd = [F_r, cs_r[b]]
                                P.op("dve", lambda e, t1=t1, cb_=cb_: e.tensor_tensor(out=tA[:], in0=t1, in1=cb_, op=ALU.mult),
                                     reads=rd, writes=[tmp_r])
                                P.op("dve", lambda e, t2=t2, sb_=sb_: e.tensor_tensor(out=tB[:], in0=t2, in1=sb_, op=ALU.mult),
                                     reads=rd, writes=[tmp_r])
                                P.op("dve", lambda e, o1=o1: e.tensor_tensor(out=o1, in0=tA[:], in1=tB[:], op=ALU.subtract),
                                     reads=[tmp_r], writes=[dst_r])
                                P.op("dve", lambda e, t1=t1, sb_=sb_: e.tensor_tensor(out=tA[:], in0=t1, in1=sb_, op=ALU.mult),
                                     reads=rd, writes=[tmp_r])
                                P.op("dve", lambda e, t2=t2, cb_=cb_: e.tensor_tensor(out=tB[:], in0=t2, in1=cb_, op=ALU.mult),
                                     reads=rd, writes=[tmp_r])
                                P.op("dve", lambda e, o2=o2: e.tensor_tensor(out=o2, in0=tA[:], in1=tB[:], op=ALU.add),
                                     reads=[tmp_r, F_r], writes=[dst_r])
                            elif n < 8:
                                h = n - 4
                                P.op("act", lambda e, h=h, F_=F_, b=b: e.activation(out=v[b][:, h * 512:(h + 1) * 512], in_=F_[:], func=AF.Copy),
                                     reads=[F_r], writes=[v_r[b][h]])
                            else:
                                h = n - 8
                                P.op("act", lambda e, h=h, F_=F_, b=b: e.activation(out=sg[b][:, h * 512:(h + 1) * 512], in_=F_[:], func=AF.Silu),
                                     reads=[F_r], writes=[sg_r[b][h]])

                    def stage1b(t):
                        b = t % 2

                        def trq(e):
                            ins = None
                            for c in range(8):
                                ins = e.transpose(out=PT[0][:, c, :], in_=qrot[:, c * 128:(c + 1) * 128], identity=ident[:])
                            return ins
                        P.op("pe", trq, reads=[qrot_r, const_r], writes=[PT_r[0]])
                        P.op("dve", lambda e, b=b: e.tensor_tensor(
                            out=qT[b][:].rearrange("p (h two) t -> p h two t", two=2),
                            in0=PT[0][:].rearrange("p (h two) t -> p h two t", two=2),
                            in1=qdT[:].unsqueeze(2).to_broadcast([128, H, 2, 128]), op=ALU.mult),
                            reads=[PT_r[0], const_r], writes=[qT_r[b]])

                        def trk(e):
                            ins = None
                            for c in range(8):
                                ins = e.transpose(out=PT[1][:, c, :], in_=krot[:, c * 128:(c + 1) * 128], identity=ident[:])
                            return ins
                        P.op("pe", trk, reads=[krot_r, const_r], writes=[PT_r[1]])
                        P.op("act", lambda e, b=b: e.activation(out=kT[b][:], in_=PT[1][:], func=AF.Copy),
                             reads=[PT_r[1]], writes=[kT_r[b]])
                        P.op("pool", lambda e, b=b: e.tensor_tensor(
                            out=khat[b][:], in0=krot[:].rearrange("p (h d) -> p h d", h=H),
                            in1=kd[:].unsqueeze(2).to_broadcast([128, H, DK]), op=ALU.mult),
                            reads=[krot_r, const_r], writes=[khat_r[b]])

                    def stage2a(t):
                        b = t % 2
                        Fs, Fs_r = Fb[4], Fb_r[4]

                        def sc(e, Fs=Fs, b=b):
                            ins = None
                            for h in range(H):
                                for dc in range(2):
                                    ins = e.matmul(Fs[:, h * 128:(h + 1) * 128], lhsT=kT[b][:, 2 * h + dc, :],
                                                   rhs=qT[b][:, 2 * h + dc, :], start=(dc == 0), stop=(dc == 1))
                            return ins
                        P.op("pe", sc, reads=[kT_r[b], qT_r[b]], writes=[Fs_r])
                        P.op("dve", lambda e, Fs=Fs: e.tensor_tensor(out=Pm[:], in0=Fs[:], in1=maskT[:], op=ALU.mult),
                             reads=[Fs_r, const_r], writes=[Pm_r])

                    def stage2b(t):
                        b = ob = t % 2
                        for h in range(H):
                            fi = fctr[0] % 4
                            fctr[0] += 1
                            Fo, Fo_r = Fb[fi], Fb_r[fi]

                            def om(e, h=h, Fo=Fo, b=b):
                                e.matmul(Fo[:], lhsT=Pm[:, h * 128:(h + 1) * 128], rhs=v[b][:, h * 512:(h + 1) * 512],
                                         start=True, stop=False)
                                e.matmul(Fo[:], lhsT=qT[b][:, 2 * h, :], rhs=Sbf[:, 2 * h, :], start=False, stop=False)
                                return e.matmul(Fo[:], lhsT=qT[b][:, 2 * h + 1, :], rhs=Sbf[:, 2 * h + 1, :],
                                                start=False, stop=True)
                            P.op("pe", om, reads=[Pm_r, v_r[b][h], qT_r[b], Sbf_r[2 * h], Sbf_r[2 * h + 1]], writes=[Fo_r])
                            if pair:
                                if h % 2 == 0:
                                    P.op("act", lambda e, h=h, Fo=Fo, ob=ob: e.activation(
                                        out=olb[ob][:, h * 512:(h + 1) * 512], in_=Fo[:], func=AF.Copy),
                                        reads=[Fo_r], writes=[olb_r[ob]])
                                else:
                                    P.op("dve", lambda e, h=h, Fo=Fo, ob=ob: e.tensor_copy(
                                        out=olb[ob][:, h * 512:(h + 1) * 512], in_=Fo[:]),
                                        reads=[Fo_r], writes=[olb_r[ob]])
                                continue
                            P.op("dve", lambda e, h=h: e.memset(ssq[:, h:h + 1], 0.0), writes=[ssq_r])
                            P.op("act", lambda e, h=h, Fo=Fo: e.activation(out=junk[:, 0:512], in_=Fo[:], func=AF.Square,
                                                                          accum_out=ssq[:, h:h + 1]),
                                 reads=[Fo_r], writes=[ssq_r, small_r])
                            P.op("act", lambda e, h=h: e.activation(out=rs4[:, h:h + 1], in_=ssq[:, h:h + 1], func=AF.Sqrt,
                                                                    bias=epsc[:, 0:1], scale=1.0 / DV),
                                 reads=[ssq_r, pos_r], writes=[ssq_r])
                            P.op("dve", lambda e, h=h: e.reciprocal(out=rs4[:, h:h + 1], in_=rs4[:, h:h + 1]),
                                 reads=[ssq_r], writes=[ssq_r])
                            P.op("dve", lambda e, h=h, Fo=Fo, b=b: e.scalar_tensor_tensor(
                                out=og[:, h * 512:(h + 1) * 512], in0=Fo[:], scalar=rs4[:, h:h + 1],
                                in1=sg[b][:, h * 512:(h + 1) * 512], op0=ALU.mult, op1=ALU.mult),
                                reads=[Fo_r, ssq_r, sg_r[b][h]], writes=[og_r])
                        if pair:
                            P.dma("pool", f"ol_st{ob}", ol[t], olb[ob][:], reads=[olb_r[ob]], writes=[ol_res[t]])
                            P.dma("pool", f"qs_st{b}", qs[t], qT[b][:].rearrange("p k t -> p (k t)"), reads=[qT_r[b]],
                                  writes=[qs_res[t]])
                            P.dma("pool", f"sg_st{b}", sgs[t], sg[b][:], reads=sg_r[b], writes=[sgs_res[t]])
                        for h in range(H):
                            for dc in range(2):
                                i = 2 * h + dc
                                fi = 4 + (fctr[0] % 2)
                                fctr[0] += 1
                                Fk, Fk_r = Fb[fi], Fb_r[fi]
                                P.op("pe", lambda e, h=h, dc=dc, Fk=Fk, b=b: e.matmul(
                                    Fk[:], lhsT=khat[b][:, h, dc * 128:(dc + 1) * 128], rhs=v[b][:, h * 512:(h + 1) * 512],
                                    start=True, stop=True), reads=[khat_r[b], v_r[b][h]], writes=[Fk_r])
                                P.op("dve", lambda e, i=i, h=h, Fk=Fk: e.scalar_tensor_tensor(
                                    out=S[:, i, :], in0=S[:, i, :], scalar=float(CDEC[h]), in1=Fk[:],
                                    op0=ALU.mult, op1=ALU.add), reads=[Fk_r, S_r[i]], writes=[S_r[i]])
                                P.op("act", lambda e, i=i: e.activation(out=Sbf[:, i, :], in_=S[:, i, :], func=AF.Copy),
                                     reads=[S_r[i]], writes=[Sbf_r[i]])
                        if not pair:
                            for half in range(2):
                                def tro(e, half=half):
                                    ins = None
                                    for c in range(8):
                                        cc = half * 8 + c
                                        ins = e.transpose(out=PT[half][:, c, :], in_=og[:, cc * 128:(cc + 1) * 128],
                                                          identity=ident[:])
                                    return ins
                                P.op("pe", tro, reads=[og_r, const_r], writes=[PT_r[half]])
                                if half == 0:
                                    P.op("act", lambda e, ob=ob: e.activation(out=ogT[ob][:, 0:8, :], in_=PT[0][:], func=AF.Copy),
                                         reads=[PT_r[0]], writes=[ogT_r[ob]])
                                else:
                                    P.op("dve", lambda e, ob=ob: e.tensor_copy(out=ogT[ob][:, 8:16, :], in_=PT[1][:]),
                                         reads=[PT_r[1]], writes=[ogT_r[ob]])
                            P.dma("pool", f"ys_st{ob}", ys[t], ogT[ob][:].rearrange("p k t -> p (k t)"),
                                  reads=[ogT_r[ob]], writes=[ys_res[t]])

                    load_x(0)
                    if NT > 1:
                        load_x(1)
                    prep(0)
                    if NT > 1:
                        prep(1)
                    stage1(0)
                    stage1b(0)
                    for t in range(NT):
                        if t + 2 < NT:
                            prep(t + 2)
                        if t + 1 < NT:
                            stage1(t + 1)
                        stage2a(t)
                        if t + 1 < NT:
                            stage1b(t + 1)
                        stage2b(t)
                    if pair:
                        P.dma("pool", "ccs_st", ccs_in[li].ap(), S[:].rearrange("p i e -> p (i e)"), reads=S_r, writes=[ccs_r])
                        P.collective("ccs", "AllGather", RG, ccs_in[li].ap().opt(), ccs_out[li].ap().opt(),
                                     reads=[ccs_r], writes=[ccs_r])
                else:
                    G = 512 if T >= 512 else T
                    NG = T // G
                    TPG = G // 128
                    NF = 6
                    Fb = [ps(st, f"F{i}", [128, 512], F32) for i in range(NF)]
                    Fb_r = [Res(f"F{i}") for i in range(NF)]
                    hT = sb(st, "hTg", [128, 8, G], BF16)
                    hT_r = Res("hTg")
                    U = sb(st, "U", [128, 16, HALO + G], BF16)
                    U_r = [Res(f"U{c}") for c in range(16)]
                    SZ = sb(st, "SZ", [128, 16, G], BF16)
                    SZ_r = [Res(f"SZ{c}") for c in range(16)]
                    csb = sb(st, "csb", [128, 16, G], F32)
                    csb_r = [Res(f"csb{c}") for c in range(16)]
                    sigb = [sb(st, f"sigb{i}", [128, G], F32) for i in range(2)]
                    sigb_r = [Res(f"sigb{i}") for i in range(2)]
                    csq = [sb(st, f"csq{i}", [128, G], BF16) for i in range(2)]
                    chi = [sb(st, f"chi{i}", [128, G], BF16) for i in range(2)]
                    cst_r = [Res(f"cst{i}") for i in range(2)]
                    y1 = [sb(st, f"y1{i}", [128, G], BF16) for i in range(2)]
                    y1_r = [Res(f"y1{i}") for i in range(2)]
                    ND = 3
                    NDT = 8
                    Dg = [sb(st, f"Dg{i}", [128, NDT, 128], BF16) for i in range(ND)]
                    Dg_r = [Res(f"Dg{i}") for i in range(ND)]
                    mean = sb(st, "mean", [128, G], F32)
                    msq = sb(st, "msq", [128, G], F32)
                    rstdl = msq
                    stat_r = Res("stat")
                    dww = sb(st, "dww", [128, 16, KW], F32)
                    dwb = sb(st, "dwb", [128, 16], F32)
                    lng = sb(st, "lng", [128, 16], F32)
                    lnb = sb(st, "lnb", [128, 16], F32)
                    cw_r = Res("cw")
                    P.dma("sp", "lc_c", dww[:].rearrange("p c j -> p (c j)"), W["dww"], writes=[cw_r])
                    P.dma("sp", "lc_c", dwb[:], W["dwb"], writes=[cw_r])
                    P.dma("sp", "lc_c", lng[:], W["lng"], writes=[cw_r])
                    P.dma("sp", "lc_c", lnb[:], W["lnb"], writes=[cw_r])
                    P.dma("sp", "xin0", xin[0][:], x_src[0:128, :], reads=[xs_res[0]], writes=[xin_r[0]])
                    if not pair:
                        P.op("pool", lambda e: e.memset(U[:, :, 0:HALO], 0.0), writes=U_r)
                    else:
                        xp_src = xprev0_d if is_first else ccx_out.ap()[0:128, :]
                        P.dma("sp", "xin1", xin[1][:], xp_src, reads=[ccx_r], writes=[xin_r[1]])
                        P.op("dve", lambda e: e.tensor_scalar(out=xin[1][:], in0=xin[1][:], scalar1=flag[:, 0:1], scalar2=None,
                                                              op0=ALU.mult), reads=[xin_r[1], const_r], writes=[xin_r[1]])
                        rms_to_hT(st, xin[1], xin_r[1], xn, ss, rstd, xn, small_r, xn_r,
                                  PT[0], PT_r[0], hT[:, :, 0:128], hT_r, gfm, gfm_r, junk_r=xn_r)
                        for cb in range(16):
                            s3 = (cb % 2) * 3
                            Fa, Fg = Fb[s3], Fb[s3 + 1]
                            Fa_r, Fg_r = Fb_r[s3], Fb_r[s3 + 1]
                            for F_, F_r, c0 in ((Fa, Fa_r, cb * 128), (Fg, Fg_r, E + cb * 128)):
                                def mmh(e, F_=F_, c0=c0):
                                    ins = None
                                    for kc in range(8):
                                        ins = e.matmul(F_[:, 0:128], lhsT=Win[:, kc, c0:c0 + 128], rhs=hT[:, kc, 0:128],
                                                       start=(kc == 0), stop=(kc == 7))
                                    return ins
                                P.op("pe", mmh, reads=[hT_r] + Win_rs, writes=[F_r])
                            sbi = cb % 2
                            P.op("act", lambda e, Fg=Fg, sbi=sbi: e.activation(out=sigb[sbi][:, 0:128], in_=Fg[:, 0:128], func=AF.Sigmoid),
                                 reads=[Fg_r], writes=[sigb_r[sbi]])
                            P.op("dve", lambda e, Fa=Fa, sbi=sbi, cb=cb: e.tensor_tensor(
                                out=U[:, cb, 0:HALO], in0=Fa[:, 128 - HALO:128], in1=sigb[sbi][:, 128 - HALO:128], op=ALU.mult),
                                reads=[Fa_r, sigb_r[sbi]], writes=[U_r[cb]])
                    dctr = [0]
                    Fsum, Fsq = Fb[4], Fb[5]
                    Fsum_r, Fsq_r = Fb_r[4], Fb_r[5]

                    def prep_tile(g, tt, part=3):
                        t = g * TPG + tt
                        b = t % 2
                        if (part & 1) and t + 1 < NT:
                            nb = (t + 1) % 2
                            P.dma("sp", f"xin{nb}", xin[nb][:], x_src[(t + 1) * 128:(t + 2) * 128, :],
                                  reads=[xs_res[t + 1]], writes=[xin_r[nb]])
                        rms_to_hT(st, xin[b], xin_r[b], xn, ss, rstd, xn, small_r, xn_r,
                                  PT[0], PT_r[0], hT[:, :, tt * 128:(tt + 1) * 128], hT_r, gfm, gfm_r, junk_r=xn_r, part=part)

                    def prep_g(g):
                        for tt in range(TPG):
                            prep_tile(g, tt)

                    def proj(F_, F_r, c0):
                        def mm(e, F_=F_, c0=c0):
                            ins = None
                            for kc in range(8):
                                ins = e.matmul(F_[:, 0:G], lhsT=Win[:, kc, c0:c0 + 128], rhs=hT[:, kc, :],
                                               start=(kc == 0), stop=(kc == 7))
                            return ins
                        P.op("pe", mm, reads=[hT_r] + Win_rs, writes=[F_r])

                    def ab(g, cb):
                        s2 = (cb % 2) * 2
                        Fa, Fg = Fb[s2], Fb[s2 + 1]
                        Fa_r, Fg_r = Fb_r[s2], Fb_r[s2 + 1]
                        proj(Fa, Fa_r, cb * 128)
                        proj(Fg, Fg_r, E + cb * 128)
                        sbi = cb % 2
                        P.op("act", lambda e, Fg=Fg, sbi=sbi: e.activation(out=sigb[sbi][:], in_=Fg[:, 0:G], func=AF.Sigmoid),
                             reads=[Fg_r], writes=[sigb_r[sbi]])
                        P.op("dve", lambda e, Fa=Fa, sbi=sbi, cb=cb: e.tensor_tensor(
                            out=U[:, cb, HALO:HALO + G], in0=Fa[:, 0:G], in1=sigb[sbi][:], op=ALU.mult),
                            reads=[Fa_r, sigb_r[sbi]], writes=[U_r[cb]])

                    def zph(g, cb):
                        Fz, Fz_r = Fb[cb % 4], Fb_r[cb % 4]
                        proj(Fz, Fz_r, 2 * E + cb * 128)
                        P.op("act", lambda e, Fz=Fz, cb=cb: e.activation(out=SZ[:, cb, :], in_=Fz[:, 0:G], func=AF.Silu),
                             reads=[Fz_r], writes=[SZ_r[cb]])

                    def conv_phase(g):
                        pend_stm = []
                        for cb in range(16):
                            Fc, Fc_r = Fb[cb % 2], Fb_r[cb % 2]
                            for j0 in range(0, KW, NDT):
                                nj = min(NDT, KW - j0)
                                di = dctr[0] % ND
                                dctr[0] += 1
                                P.op("dve", lambda e, di=di, cb=cb, j0=j0, nj=nj: e.tensor_tensor(
                                    out=Dg[di][:, 0:nj, :], in0=ident[:].unsqueeze(1).to_broadcast([128, nj, 128]),
                                    in1=dww[:, cb, j0:j0 + nj].unsqueeze(2).to_broadcast([128, nj, 128]), op=ALU.mult),
                                    reads=[const_r, cw_r], writes=[Dg_r[di]])

                                def cv(e, di=di, cb=cb, j0=j0, nj=nj, Fc=Fc):
                                    ins = None
                                    for n in range(nj):
                                        j = j0 + n
                                        ins = e.matmul(Fc[:, 0:G], lhsT=Dg[di][:, n, :], rhs=U[:, cb, j:j + G],
                                                       start=(j == 0), stop=(j == KW - 1))
                                    return ins
                                P.op("pe", cv, reads=[Dg_r[di], U_r[cb]], writes=[Fc_r])
                            ci = cb % 2
                            P.op("act", lambda e, cb=cb, Fc=Fc: e.activation(out=csb[:, cb, :], in_=Fc[:, 0:G], func=AF.Identity,
                                                                            bias=dwb[:, cb:cb + 1], scale=1.0),
                                 reads=[Fc_r, cw_r], writes=[csb_r[cb]])
                            P.op("act", lambda e, cb=cb, ci=ci: e.activation(out=csq[ci][:], in_=csb[:, cb, :], func=AF.Square),
                                 reads=[csb_r[cb]], writes=[cst_r[ci]])
                            P.op("act", lambda e, cb=cb, ci=ci: e.activation(out=chi[ci][:], in_=csb[:, cb, :], func=AF.Identity),
                                 reads=[csb_r[cb]], writes=[cst_r[ci]])

                            def stm(e, cb=cb, ci=ci):
                                e.matmul(Fsum[:, 0:G], lhsT=ones[:], rhs=chi[ci][:], start=(cb == 0), stop=(cb == 15))
                                return e.matmul(Fsq[:, 0:G], lhsT=ones[:], rhs=csq[ci][:], start=(cb == 0), stop=(cb == 15))
                            pend_stm.append((stm, ci))
                            if len(pend_stm) > 1:
                                f_, ci_ = pend_stm.pop(0)
                                P.op("pe", f_, reads=[cst_r[ci_], const_r], writes=[Fsum_r, Fsq_r])
                            if g + 1 < NG and cb // 4 < TPG:
                                if cb % 4 == 0:
                                    prep_tile(g + 1, cb // 4, part=1)
                                elif cb % 4 == 2:
                                    prep_tile(g + 1, cb // 4, part=2)
                        while pend_stm:
                            f_, ci_ = pend_stm.pop(0)
                            P.op("pe", f_, reads=[cst_r[ci_], const_r], writes=[Fsum_r, Fsq_r])
                        if g + 1 < NG:
                            P.op("pool", lambda e: e.tensor_copy(out=U[:, :, 0:HALO], in_=U[:, :, G:G + HALO]),
                                 reads=U_r, writes=U_r)
                        P.op("dve", lambda e: e.tensor_scalar(out=mean[:], in0=Fsum[:, 0:G], scalar1=1.0 / E, scalar2=None,
                                                              op0=ALU.mult), reads=[Fsum_r], writes=[stat_r])
                        P.op("dve", lambda e: e.tensor_tensor(out=msq[:], in0=mean[:], in1=mean[:], op=ALU.mult),
                             reads=[stat_r], writes=[stat_r])
                        P.op("dve", lambda e: e.scalar_tensor_tensor(out=rstdl[:], in0=Fsq[:, 0:G], scalar=1.0 / E, in1=msq[:],
                                                                     op0=ALU.mult, op1=ALU.subtract),
                             reads=[Fsq_r, stat_r], writes=[stat_r])
                        P.op("act", lambda e: e.activation(out=rstdl[:], in_=rstdl[:], func=AF.Sqrt, bias=epsc[:, 0:1], scale=1.0),
                             reads=[stat_r, pos_r], writes=[stat_r])
                        P.op("dve", lambda e: e.reciprocal(out=rstdl[:], in_=rstdl[:]), reads=[stat_r], writes=[stat_r])

                    def tail1(g, cb):
                        yi = cb % 2
                        P.op("dve", lambda e, cb=cb: e.tensor_tensor(out=csb[:, cb, :], in0=csb[:, cb, :], in1=mean[:],
                                                                     op=ALU.subtract),
                             reads=[csb_r[cb], stat_r], writes=[csb_r[cb]])
                        P.op("dve", lambda e, cb=cb: e.tensor_tensor(out=csb[:, cb, :], in0=csb[:, cb, :], in1=rstdl[:],
                                                                     op=ALU.mult),
                             reads=[csb_r[cb], stat_r], writes=[csb_r[cb]])
                        P.op("act", lambda e, cb=cb, yi=yi: e.activation(out=y1[yi][:], in_=csb[:, cb, :], func=AF.Sigmoid,
                                                                        bias=lnb[:, cb:cb + 1], scale=lng[:, cb:cb + 1]),
                             reads=[csb_r[cb], cw_r], writes=[y1_r[yi]])
                        P.op("act", lambda e, cb=cb: e.activation(out=csb[:, cb, :], in_=csb[:, cb, :], func=AF.Identity,
                                                                  bias=lnb[:, cb:cb + 1], scale=lng[:, cb:cb + 1]),
                             reads=[csb_r[cb], cw_r], writes=[csb_r[cb]])

                    def tail2(g, cb):
                        yi = cb % 2
                        P.op("dve", lambda e, cb=cb, yi=yi: e.tensor_tensor(out=csb[:, cb, :], in0=csb[:, cb, :], in1=y1[yi][:],
                                                                            op=ALU.mult),
                             reads=[csb_r[cb], y1_r[yi]], writes=[csb_r[cb]])
                        P.op("dve", lambda e, cb=cb: e.tensor_tensor(out=SZ[:, cb, :], in0=SZ[:, cb, :], in1=csb[:, cb, :],
                                                                     op=ALU.mult),
                             reads=[csb_r[cb], SZ_r[cb]], writes=[SZ_r[cb]])

                    def store_q(g, q):
                        for tt in range(TPG):
                            t = g * TPG + tt
                            P.dma("pool", f"ys_st{q}", ys[t].rearrange("p (k t) -> p k t", k=16)[:, 4 * q:4 * q + 4, :],
                                  SZ[:, 4 * q:4 * q + 4, tt * 128:(tt + 1) * 128], reads=SZ_r[4 * q:4 * q + 4],
                                  writes=[ysq_res[t][q]])

                    prep_g(0)
                    for cb in range(16):
                        ab(0, cb)
                    for g in range(NG):
                        for cb in range(16):
                            zph(g, cb)
                        conv_phase(g)
                        tail1(g, 0)
                        for cb in range(16):
                            if g + 1 < NG:
                                ab(g + 1, cb)
                            if cb + 1 < 16:
                                tail1(g, cb + 1)
                            tail2(g, cb)
                            if cb % 4 == 3:
                                store_q(g, cb // 4)
                P.barrier()
                P.flush()

            with ExitStack() as st:
                rp = pair and kind == "ret"
                send_x = pair and (li + 1 < len(layers)) and layers[li + 1][0] == "conv"
                Wout = sb(st, "Wout", [128, 16, D], BF16)
                Wout_r = Res("Wout")
                P.dma("pool", "wout", Wout[:], W["w_out"].rearrange("(kc p) n -> p kc n", p=128), writes=[Wout_r])
                if li + 1 < len(layers):
                    load_win(li + 1, after=[Wout_r])
                NX = 3
                xr = [sb(st, f"xr{i}", [128, D], F32) for i in range(NX)]
                xr_r = [Res(f"xr{i}") for i in range(NX)]
                fin = do_final and is_last
                if not rp:
                    yt = [sb(st, f"yt{i}", [128, 16, 128], BF16) for i in range(2)]
                    yt_r = [Res(f"yt{i}") for i in range(2)]
                    Gb = [ps(st, f"G{i}", [128, 2, 512], F32) for i in range(2)]
                    Gb_r = [Res(f"G{i}") for i in range(2)]
                else:
                    olbB = sb(st, "olbB", [128, 2, E], F32)
                    olbB_r = [Res("olbB0"), Res("olbB1")]
                    qsb = [sb(st, f"qsb{i}", [128, 8, 128], BF16) for i in range(2)]
                    qsb_r = [Res(f"qsb{i}") for i in range(2)]
                    sgb = [sb(st, f"sgb{i}", [128, E], BF16) for i in range(2)]
                    sgb_r = [Res(f"sgb{i}") for i in range(2)]
                    ogBs = [sb(st, f"ogB{i}", [128, E], BF16) for i in range(2)]
                    ogBs_r = [Res("ogB0"), Res("ogB1")]
                    ogB, ogB_r = ogBs[0], ogBs_r[0]
                    ogTB = sb(st, "ogTB", [128, 16, 128], BF16)
                    ogTB_r = Res("ogTB")
                    SinBf = sb(st, "SinBf", [128, 8, 512], BF16)
                    SinBf_r = Res("SinBf")
                    ssqB = sb(st, "ssqB", [128, H], F32)
                    rs4B = sb(st, "rs4B", [128, H], F32)
                    ssqB_r = Res("ssqB")
                    Cb = [ps(st, f"C{i}", [128, 512], F32) for i in range(H)]
                    Cb_r = [Res(f"C{i}") for i in range(H)]
                    PTB = [ps(st, f"PTB{i}", [128, 8, 128], BF16) for i in range(2)]
                    PTB_r = [Res(f"PTB{i}") for i in range(2)]
                    Gb = [ps(st, "G0", [128, 2, 512], F32)]
                    Gb_r = [Res("G0")]
                    P.dma("sp", "sin_ld", olbB[:].rearrange("p a e -> p (a e)"), ccs_out[li].ap()[0:128, :],
                          reads=[ccs_r], writes=olbB_r)
                    P.op("dve", lambda e: e.tensor_scalar(out=SinBf[:].rearrange("p i e -> p (i e)"),
                                                          in0=olbB[:].rearrange("p a e -> p (a e)"),
                                                          scalar1=flag[:, 0:1], scalar2=None, op0=ALU.mult),
                         reads=olbB_r + [const_r], writes=[SinBf_r])
                if fin:
                    gfin = sb(st, "gfinS", [128, D], F32)
                    gfin_r = Res("gfin")
                    P.dma("sp", "lc_f", gfin[:], gfin_d, writes=[gfin_r])
                    ss = sb(st, "ssB", [128, 1], F32)
                    rstd = sb(st, "rstdB", [128, 1], F32)
                    small_r = Res("smallB")
                    junk = sb(st, "junkB", [128, D], BF16)
                    fj, fj_r = junk[:], small_r

                def loads(t):
                    b, b3 = t % 2, t % NX
                    if not rp:
                        P.dma("sp", f"yt{b}", yt[b][:].rearrange("p k t -> p (k t)"), ys[t], reads=[ys_res[t]] + ysq_res[t], writes=[yt_r[b]])
                    else:
                        P.dma("sp", f"olB{b}", olbB[:, b, :], ol[t], reads=[ol_res[t]], writes=[olbB_r[b]])
                        P.dma("sp", f"qsB{b}", qsb[b][:].rearrange("p k t -> p (k t)"), qs[t], reads=[qs_res[t]], writes=[qsb_r[b]])
                        P.dma("sp", f"sgB{b}", sgb[b][:], sgs[t], reads=[sgs_res[t]], writes=[sgb_r[b]])
                    P.dma("sp", f"xr{b3}", xr[b3][:], x_src[t * 128:(t + 1) * 128, :], reads=[xs_res[t]], writes=[xr_r[b3]])
                def B1pe(t):
                    b = t % 2
                    for h in range(H):
                        def cm(e, h=h, b=b):
                            e.matmul(Cb[h][:], lhsT=qsb[b][:, 2 * h, :], rhs=SinBf[:, 2 * h, :], start=True, stop=False)
                            return e.matmul(Cb[h][:], lhsT=qsb[b][:, 2 * h + 1, :], rhs=SinBf[:, 2 * h + 1, :],
                                            start=False, stop=True)
                        P.op("pe", cm, reads=[qsb_r[b], SinBf_r], writes=[Cb_r[h]])

                def B1(t):
                    b = t % 2
                    ogx, ogx_r = ogBs[b], ogBs_r[b]
                    P.op("dve", lambda e: e.memset(ssqB[:], 0.0), writes=[ssqB_r])
                    for h in range(H):
                        osl = olbB[:, b, h * 512:(h + 1) * 512]
                        P.op("dve", lambda e, h=h, osl=osl, t=t: e.scalar_tensor_tensor(
                            out=osl, in0=Cb[h][:], scalar=float(CDEC[h] ** t), in1=osl, op0=ALU.mult, op1=ALU.add),
                            reads=[Cb_r[h], olbB_r[b]], writes=[olbB_r[b]])
                        P.op("act", lambda e, h=h, osl=osl, ogx=ogx: e.activation(out=ogx[:, h * 512:(h + 1) * 512], in_=osl, func=AF.Square,
                                                                                  accum_out=ssqB[:, h:h + 1]),
                             reads=[olbB_r[b]], writes=[ssqB_r, ogx_r])
                    P.op("act", lambda e: e.activation(out=rs4B[:], in_=ssqB[:], func=AF.Sqrt,
                                                       bias=epsc[:, 0:1], scale=1.0 / DV),
                         reads=[ssqB_r, pos_r], writes=[ssqB_r])
                    P.op("dve", lambda e: e.reciprocal(out=rs4B[:], in_=rs4B[:]), reads=[ssqB_r], writes=[ssqB_r])
                    for h in range(H):
                        osl = olbB[:, b, h * 512:(h + 1) * 512]
                        P.op("dve", lambda e, h=h, osl=osl, b=b, ogx=ogx: e.scalar_tensor_tensor(
                            out=ogx[:, h * 512:(h + 1) * 512], in0=osl, scalar=rs4B[:, h:h + 1],
                            in1=sgb[b][:, h * 512:(h + 1) * 512], op0=ALU.mult, op1=ALU.mult),
                            reads=[olbB_r[b], ssqB_r, sgb_r[b]], writes=[ogx_r])

                mk_tabs = pre_tabs and li == 0
                if mk_tabs:
                    angB = sb(st, "angB", [128, 128], F32)
                    ang2B = sb(st, "ang2B", [128, 128], F32)
                    angB_r = Res("angB")
                    tabB = [sb(st, f"tabB{i}", [128, 2, 128], F32) for i in range(2)]
                    tabB_r = [Res("tabB0"), Res("tabB1")]
                loads(0)
                if NT > 1:
                    loads(1)
                if rp:
                    B1pe(0)
                    B1(0)
                for t in range(NT):
                    b, b3 = t % 2, t % NX
                    if rp:
                        if t + 2 < NT:
                            loads(t + 2)
                        if t + 1 < NT:
                            B1pe(t + 1)
                    if not rp:
                        Gt, Gt_r = Gb[b], Gb_r[b]

                        def mm(e, b=b, Gt=Gt):
                            ins = None
                            for n in range(2):
                                for kc in range(16):
                                    ins = e.matmul(Gt[:, n, :], lhsT=yt[b][:, kc, :], rhs=Wout[:, kc, n * 512:(n + 1) * 512],
                                                   start=(kc == 0), stop=(kc == 15))
                            return ins
                        P.op("pe", mm, reads=[yt_r[b], Wout_r], writes=[Gt_r])
                        if t + 2 < NT:
                            loads(t + 2)
                    else:
                        ogx, ogx_r = ogBs[b], ogBs_r[b]
                        for half in range(2):
                            def tro(e, half=half, ogx=ogx):
                                ins = None
                                for c in range(8):
                                    cc = half * 8 + c
                                    ins = e.transpose(out=PTB[half][:, c, :], in_=ogx[:, cc * 128:(cc + 1) * 128],
                                                      identity=ident[:])
                                return ins
                            P.op("pe", tro, reads=[ogx_r, const_r], writes=[PTB_r[half]])
                            if half == 0:
                                P.op("act", lambda e: e.activation(out=ogTB[:, 0:8, :], in_=PTB[0][:], func=AF.Copy),
                                     reads=[PTB_r[0]], writes=[ogTB_r])
                            else:
                                P.op("dve", lambda e: e.tensor_copy(out=ogTB[:, 8:16, :], in_=PTB[1][:]),
                                     reads=[PTB_r[1]], writes=[ogTB_r])
                        Gt, Gt_r = Gb[0], Gb_r[0]

                        def mm(e, Gt=Gt):
                            ins = None
                            for n in range(2):
                                for kc in range(16):
                                    ins = e.matmul(Gt[:, n, :], lhsT=ogTB[:, kc, :], rhs=Wout[:, kc, n * 512:(n + 1) * 512],
                                                   start=(kc == 0), stop=(kc == 15))
                            return ins
                        P.op("pe", mm, reads=[ogTB_r, Wout_r], writes=[Gt_r])
                        if t + 1 < NT:
                            B1(t + 1)
                    P.op("dve", lambda e, b3=b3, Gt=Gt: e.tensor_tensor(
                        out=xr[b3][:], in0=xr[b3][:], in1=Gt[:].rearrange("p n c -> p (n c)"), op=ALU.add),
                        reads=[Gt_r, xr_r[b3]], writes=[xr_r[b3]])
                    if mk_tabs:
                        emit_tables(t, angB, ang2B, angB_r, (tabB[b][:, 0, :], tabB[b][:, 1, :]), tabB_r[b])
                        P.dma("pool", f"tab_st{b}", tabs_d[t], tabB[b][:].rearrange("p a f -> p (a f)"),
                              reads=[tabB_r[b]], writes=[tabs_res[t]])
                    if send_x and t == NT - 1:
                        P.dma("pool", "ccx_st", ccx_in.ap(), xr[b3][:], reads=[xr_r[b3]], writes=[ccx_r])
                        P.collective("ccx", "AllGather", RG, ccx_in.ap().opt(), ccx_out.ap().opt(), reads=[ccx_r], writes=[ccx_r])
                    if fin:
                        P.op("dve", lambda e: e.memset(ss[:], 0.0), writes=[small_r])
                        P.op("act", lambda e, b3=b3: e.activation(out=fj, in_=xr[b3][:], func=AF.Square, accum_out=ss[:, 0:1]),
                             reads=[xr_r[b3]], writes=[small_r, fj_r])
                        P.op("act", lambda e: e.activation(out=rstd[:], in_=ss[:], func=AF.Sqrt, bias=epsc[:, 0:1], scale=1.0 / D),
                             reads=[small_r, pos_r], writes=[small_r])
                        P.op("dve", lambda e: e.reciprocal(out=rstd[:], in_=rstd[:]), reads=[small_r], writes=[small_r])
                        P.op("dve", lambda e, b3=b3: e.scalar_tensor_tensor(
                            out=xr[b3][:], in0=xr[b3][:], scalar=rstd[:, 0:1], in1=gfin[:], op0=ALU.mult, op1=ALU.mult),
                            reads=[xr_r[b3], small_r, gfin_r], writes=[xr_r[b3]])
                    P.dma("pool", f"xst{b3}", x_dst[t * 128:(t + 1) * 128, :], xr[b3][:], reads=[xr_r[b3]],
                          writes=[xs_res[t]])
                if is_last:
                    P.final_wait()
                else:
                    P.barrier()
                P.flush()
    return nc


def _consts():
    bf = ml_dtypes.bfloat16
    ident = np.eye(128, dtype=np.float32).astype(bf)
    ones = np.ones((128, 128), dtype=np.float32).astype(bf)
    inv_freq = (10000.0 ** (-np.arange(128, dtype=np.float32) / 128.0)).astype(np.float32)
    invf = np.broadcast_to(inv_freq[None, :], (128, 128)).astype(np.float32).copy()
    idx = np.arange(128, dtype=np.float64)
    maskT = np.zeros((128, H, 128), np.float64)
    qdT = np.zeros((128, H, 128), np.float64)
    kd = np.zeros((128, H), np.float64)
    for h in range(H):
        g = GAMMA[h]
        causal = (idx[None, :] >= idx[:, None]).astype(np.float64)
        maskT[:, h, :] = (g ** (-(idx[:, None] + 1.0))) * causal * (DK ** -0.5)
        qdT[:, h, :] = (g ** (idx[None, :] + 1.0))
        kd[:, h] = (g ** (127.0 - idx)) * (DK ** -0.5)
    return dict(ident=ident, ones=ones, invf=invf,
                maskT=maskT.reshape(128, H * 128).astype(np.float32),
                qdT=qdT.reshape(128, H * 128).astype(np.float32), kd=kd.astype(np.float32))


def _fm(vec, nblk):
    return np.ascontiguousarray(np.asarray(vec, np.float32).reshape(nblk, 128).T)


def _layer_inputs(li, kind, slot, inp):
    d = {}
    if kind == "conv":
        d[f"gfm{li}"] = _fm(inp["conv_norm"][slot], 8)
        d[f"w_in{li}"] = np.ascontiguousarray(inp["conv_w_in"][slot], dtype=np.float32)
        d[f"w_out{li}"] = np.ascontiguousarray(inp["conv_w_out"][slot], dtype=np.float32)
        dw = np.asarray(inp["conv_dw_w"][slot], np.float32)
        d[f"dww{li}"] = np.ascontiguousarray(dw.reshape(KW, 16, 128).transpose(2, 1, 0).reshape(128, 16 * KW))
        d[f"dwb{li}"] = _fm(inp["conv_dw_b"][slot], 16)
        d[f"lng{li}"] = _fm(inp["conv_ln_g"][slot], 16)
        d[f"lnb{li}"] = _fm(inp["conv_ln_b"][slot], 16)
    else:
        d[f"gfm{li}"] = _fm(inp["ret_norm"][slot], 8)
        d[f"w_in{li}"] = np.ascontiguousarray(inp["ret_w_in"][slot], dtype=np.float32)
        d[f"w_out{li}"] = np.ascontiguousarray(inp["ret_w_out"][slot], dtype=np.float32)
    return d


_PROG_CACHE = {}


def run_layers(x, positions, inp, layer_ids, do_final, n_cores, pair=False):
    B, S, _ = x.shape
    T = S // 2 if pair else S
    layers = [("conv" if i % 2 == 0 else "ret", i // 2) for i in layer_ids]
    key = (T, tuple(layers), do_final, pair, n_cores)
    if key not in _PROG_CACHE:
        _PROG_CACHE[key] = build_program(T, layers, True, True, do_final, pair=pair, n_cores=n_cores)
    nc = _PROG_CACHE[key]
    shared = dict(_consts())
    shared["gfin"] = np.ascontiguousarray(np.broadcast_to(np.asarray(inp["final_norm"], np.float32)[None, :], (128, D)))
    for li, (kind, slot) in enumerate(layers):
        shared.update(_layer_inputs(li, kind, slot, inp))
    in_maps = []
    for c in range(n_cores):
        m = dict(shared)
        if pair:
            b, hf = (c // 2) % B, c % 2
            lo = hf * T
            m["xprev0"] = (np.ascontiguousarray(x[b, lo - 128:lo], dtype=np.float32) if hf == 1
                           else np.zeros((128, D), np.float32))
            m["flag"] = np.full((128, 1), float(hf), np.float32)
        else:
            b, lo = c % B, 0
        m["x"] = np.ascontiguousarray(x[b, lo:lo + T], dtype=np.float32)
        m["posT"] = np.ascontiguousarray(np.asarray(positions[b, lo:lo + T], np.int32).reshape(T // 128, 128).T)
        in_maps.append(m)
    res = run_bass_kernel_spmd(nc, in_maps, core_ids=list(range(n_cores)))
    if pair:
        return np.stack([np.concatenate([res.results[2 * b]["out"], res.results[2 * b + 1]["out"]], axis=0)
                         for b in range(B)], axis=0)
    return np.stack([res.results[b]["out"] for b in range(B)], axis=0)


def kernel(x, positions, conv_norm, conv_w_in, conv_dw_w, conv_dw_b, conv_ln_g, conv_ln_b, conv_w_out,
           ret_norm, ret_w_in, ret_w_out, final_norm):
    inp = dict(conv_norm=np.asarray(conv_norm), conv_w_in=np.asarray(conv_w_in), conv_dw_w=np.asarray(conv_dw_w),
               conv_dw_b=np.asarray(conv_dw_b), conv_ln_g=np.asarray(conv_ln_g), conv_ln_b=np.asarray(conv_ln_b),
               conv_w_out=np.asarray(conv_w_out), ret_norm=np.asarray(ret_norm), ret_w_in=np.asarray(ret_w_in),
               ret_w_out=np.asarray(ret_w_out), final_norm=np.asarray(final_norm))
    x = np.asarray(x, dtype=np.float32)
    positions = np.asarray(positions)
    out = run_layers(x, positions, inp, [0, 1, 2, 3], True, 8, pair=True)
    return out.astype(np.float32)
```

```python
import math
from contextlib import ExitStack

import numpy as np
import ml_dtypes

import concourse.bass as bass
import concourse.mybir as mybir
from concourse.bass_utils import run_bass_kernel_spmd

F32 = mybir.dt.float32
BF16 = mybir.dt.bfloat16
I32 = mybir.dt.int32
AF = mybir.ActivationFunctionType
ALU = mybir.AluOpType

D = 1024
E = 2048
H = 4
DK = 256
DV = 512
KW = 31
HALO = KW - 1
EPS = 1e-6
TWO_PI = 2.0 * math.pi
MAGIC = 12582912.0
CW1 = 6.28125
CW2 = TWO_PI - 6.28125
PI_LO = 3.1415925
GAMMA = [1.0 - 2.0 ** (-5.0 - h) for h in range(H)]
CDEC = [g ** 128 for g in GAMMA]

ENGS = ("pe", "act", "dve", "pool", "sp")
BLK = dict(pe="tensor", act="scalar", dve="vector", pool="gpsimd", sp="sync")
import os as _os
SAME_ENGINE_SYNC = _os.environ.get("K_SES", "1") == "1"


class Res:
    __slots__ = ("name", "w", "r")

    def __init__(self, name):
        self.name = name
        self.w = None
        self.r = []


class Prog:
    def __init__(self, nc, stack):
        self.nc = nc
        self.stack = stack
        self.sem = {e: stack.enter_context(nc.semaphore("sem_" + e)) for e in ENGS}
        self.cnt = {e: 0 for e in ENGS}
        self.dsem = {}
        self.dcnt = {}
        self.persist = set()
        self.cc_sems = set()
        self.waited = {e: {} for e in ENGS}
        self.streams = {e: [] for e in ENGS}
        self.extra = {e: set() for e in ENGS}

    def _semh(self, k):
        return self.sem[k] if k in self.sem else self.dsem[k]

    def _emit(self, eng, fn, reads, writes, ev):
        deps = set(self.extra[eng])
        self.extra[eng] = set()
        for r in reads:
            if r.w is not None:
                deps.add(r.w)
        for w in writes:
            if w.w is not None:
                deps.add(w.w)
            deps.update(w.r)
        best = {}
        for k, v in deps:
            if v > best.get(k, 0):
                best[k] = v
        waits = []
        wd = self.waited[eng]
        for k, v in best.items():
            if k == eng and (eng == "pe" or not SAME_ENGINE_SYNC):
                continue
            if wd.get(k, 0) >= v:
                continue
            wd[k] = v
            waits.append((k, v))
        self.streams[eng].append((waits, fn, ev))
        if ev is not None:
            for r in reads:
                r.r.append(ev)
            for w in writes:
                w.w = ev
                w.r = []

    def op(self, eng, fn, reads=(), writes=()):
        self.cnt[eng] += 1
        ev = (eng, self.cnt[eng])
        self._emit(eng, fn, reads, writes, ev)
        return ev

    def dma(self, q, semname, out, in_, reads=(), writes=(), persist=False):
        if semname not in self.dsem:
            self.dsem[semname] = self.stack.enter_context(self.nc.semaphore("d_" + semname))
            self.dcnt[semname] = 0
            if persist:
                self.persist.add(semname)
        self.dcnt[semname] += 16
        ev = (semname, self.dcnt[semname])
        self._emit(q, lambda e: e.dma_start(out=out, in_=in_), reads, writes, ev)
        return ev

    def collective(self, semname, kind, rg, in_ap, out_ap, reads=(), writes=()):
        if semname not in self.dsem:
            self.dsem[semname] = self.stack.enter_context(self.nc.semaphore("c_" + semname))
            self.dcnt[semname] = 0
            self.cc_sems.add(semname)
        self.dcnt[semname] += 1
        ev = (semname, self.dcnt[semname])
        self._emit("pool", lambda e: e.collective_compute(kind, ALU.bypass, replica_groups=rg, ins=[in_ap], outs=[out_ap]),
                   reads, writes, ev)
        return ev

    def barrier(self, include_persist=False):
        tg = set((e, self.cnt[e]) for e in ENGS if self.cnt[e] > 0)
        for k, v in self.dcnt.items():
            if v > 0 and (include_persist or k not in self.persist):
                tg.add((k, v))
        for e in ENGS:
            self.extra[e] |= tg

    def final_wait(self):
        self.barrier(include_persist=True)
        self._emit("sp", None, (), (), None)

    def flush(self):
        with self.nc.Block() as block:
            for e in ENGS:
                stream = self.streams[e]
                if not stream:
                    continue

                def body(engobj, stream=stream):
                    for waits, fn, ev in stream:
                        for k, v in waits:
                            engobj.wait_ge(self._semh(k), v)
                        if fn is None:
                            continue
                        ins = fn(engobj)
                        if ev[0] in self.cc_sems:
                            ins.then_inc(self._semh(ev[0]))
                        else:
                            ins.then_inc(self._semh(ev[0]), 16 if ev[0] in self.dsem else 1)

                getattr(block, BLK[e])(body)
        self.streams = {e: [] for e in ENGS}


def build_program(T, layers, first_ext, last_ext, do_final, pair=False, n_cores=8):
    NT = T // 128
    nc = bass.Bass("TRN2", target_bir_lowering=False)
    dr = {}

    def ext_in(name, shape, dt):
        dr[name] = nc.dram_tensor(name, list(shape), dt, kind="ExternalInput").ap()
        return dr[name]

    x_ext = ext_in("x", [T, D], F32)
    posT = ext_in("posT", [128, NT], I32)
    ident_d = ext_in("ident", [128, 128], BF16)
    ones_d = ext_in("ones", [128, 128], BF16)
    invf_d = ext_in("invf", [128, 128], F32)
    maskT_d = ext_in("maskT", [128, H * 128], F32)
    qdT_d = ext_in("qdT", [128, H * 128], F32)
    kd_d = ext_in("kd", [128, H], F32)
    gfin_d = ext_in("gfin", [128, D], F32)
    lw = []
    for li, (kind, slot) in enumerate(layers):
        d = {}
        d["gfm"] = ext_in(f"gfm{li}", [128, 8], F32)
        d["w_in"] = ext_in(f"w_in{li}", [D, 3 * E], F32)
        d["w_out"] = ext_in(f"w_out{li}", [E, D], F32)
        if kind == "conv":
            d["dww"] = ext_in(f"dww{li}", [128, 16 * KW], F32)
            d["dwb"] = ext_in(f"dwb{li}", [128, 16], F32)
            d["lng"] = ext_in(f"lng{li}", [128, 16], F32)
            d["lnb"] = ext_in(f"lnb{li}", [128, 16], F32)
        lw.append(d)
    out_ext = nc.dram_tensor("out", [T, D], F32, kind="ExternalOutput").ap()
    xs = nc.dram_tensor("xs", [T, D], F32, kind="Internal").ap()
    ys = nc.dram_tensor("ys", [NT, 128, 16 * 128], BF16, kind="Internal").ap()

    if pair:
        RG = [[2 * i, 2 * i + 1] for i in range(n_cores // 2)]
        xprev0_d = ext_in("xprev0", [128, D], F32)
        flag_d = ext_in("flag", [128, 1], F32)
        ol = nc.dram_tensor("ol", [NT, 128, E], F32, kind="Internal").ap()
        qs = nc.dram_tensor("qs", [NT, 128, 8 * 128], BF16, kind="Internal").ap()
        sgs = nc.dram_tensor("sgs", [NT, 128, E], BF16, kind="Internal").ap()
        ol_res = [Res(f"ol{t}") for t in range(NT)]
        qs_res = [Res(f"qs{t}") for t in range(NT)]
        sgs_res = [Res(f"sgs{t}") for t in range(NT)]
        ccx_in = nc.dram_tensor("ccx_in", [128, D], F32)
        ccx_out = nc.dram_tensor("ccx_out", [256, D], F32)
        ccs_in = [nc.dram_tensor(f"ccs_in{i}", [128, 8 * 512], F32) for i in range(len(layers))]
        ccs_out = [nc.dram_tensor(f"ccs_out{i}", [256, 8 * 512], F32) for i in range(len(layers))]
        ccx_r, ccs_r = Res("ccx"), Res("ccs")
    xs_res = [Res(f"xs{t}") for t in range(NT)]
    ys_res = [Res(f"ys{t}") for t in range(NT)]
    ysq_res = [[Res(f"ys{t}_{q}") for q in range(4)] for t in range(NT)]

    with ExitStack() as top:
        P = Prog(nc, top)

        uniq = [0]

        def sb(stack, name, shape, dt):
            uniq[0] += 1
            return stack.enter_context(nc.sbuf_tensor(f"{name}_{uniq[0]}", list(shape), dt))

        def ps(stack, name, shape, dt):
            uniq[0] += 1
            return stack.enter_context(nc.psum_tensor(f"{name}_{uniq[0]}", list(shape), dt))

        Win = sb(top, "Win", [128, 8, 3 * E], BF16)
        Win_rs = [Res(f"Win{k}") for k in range(8)]
        ident = sb(top, "identS", [128, 128], BF16)
        ones = sb(top, "onesS", [128, 128], BF16)
        invf = sb(top, "invfS", [128, 128], F32)
        maskT = sb(top, "maskTS", [128, H * 128], F32)
        qdT = sb(top, "qdTS", [128, H, 128], F32)
        kd = sb(top, "kdS", [128, H], F32)
        posi = sb(top, "posi", [128, NT], I32)
        posf = sb(top, "posf", [128, NT], F32)
        negpi = sb(top, "negpi", [128, 1], F32)
        const_r = Res("consts")
        pos_r = Res("pos")

        for dst, src in ((ident, ident_d), (ones, ones_d), (invf, invf_d), (maskT, maskT_d),
                         (kd, kd_d), (posi, posT)):
            P.dma("sp", "const", dst[:], src, writes=[const_r])
        P.dma("sp", "const", qdT[:].rearrange("p h t -> p (h t)"), qdT_d, writes=[const_r])
        P.op("dve", lambda e: e.tensor_copy(out=posf[:], in_=posi[:]), reads=[const_r], writes=[pos_r])
        P.op("dve", lambda e: e.memset(negpi[:], math.pi), writes=[pos_r])
        epsc = sb(top, "epsc", [128, 1], F32)
        if pair:
            flag = sb(top, "flagS", [128, 1], F32)
            P.dma("sp", "const", flag[:], flag_d, writes=[const_r])
        P.op("dve", lambda e: e.memset(epsc[:], EPS), writes=[pos_r])

        def load_win(li, after=()):
            src = lw[li]["w_in"].rearrange("(kc p) n -> p kc n", p=128)
            for kc in range(8):
                P.dma("pool", "win", Win[:, kc, :], src[:, kc, :], reads=list(after), writes=[Win_rs[kc]], persist=True)

        load_win(0)

        def rms_to_hT(st, xt, xt_r, junk, ss, rstd, xn, small_r, xn_r, PT, PT_r, hT_out, hT_r, gfm, gfm_r, junk_r=None,
                      part=3):
            if part & 1:
                rms_chain(xt, xt_r, junk, ss, rstd, xn, small_r, xn_r, junk_r)
            if part & 2:
                rms_tr(xn, xn_r, PT, PT_r, hT_out, hT_r, gfm, gfm_r)

        def rms_chain(xt, xt_r, junk, ss, rstd, xn, small_r, xn_r, junk_r):
            P.op("dve", lambda e: e.memset(ss[:], 0.0), writes=[small_r])
            P.op("act", lambda e: e.activation(out=junk[:], in_=xt[:], func=AF.Square, accum_out=ss[:, 0:1]),
                 reads=[xt_r], writes=[small_r] + ([junk_r] if junk_r is not None else []))
            P.op("act", lambda e: e.activation(out=rstd[:], in_=ss[:], func=AF.Sqrt, bias=epsc[:, 0:1], scale=1.0 / D),
                 reads=[small_r, pos_r], writes=[small_r])
            P.op("dve", lambda e: e.reciprocal(out=rstd[:], in_=rstd[:]), reads=[small_r], writes=[small_r])
            P.op("act", lambda e: e.activation(out=xn[:], in_=xt[:], func=AF.Identity, scale=rstd[:, 0:1]),
                 reads=[xt_r, small_r], writes=[xn_r])

        def rms_tr(xn, xn_r, PT, PT_r, hT_out, hT_r, gfm, gfm_r):
            def tr(e):
                ins = None
                for kc in range(8):
                    ins = e.transpose(out=PT[:, kc, :], in_=xn[:, kc * 128:(kc + 1) * 128], identity=ident[:])
                return ins
            P.op("pe", tr, reads=[xn_r, const_r], writes=[PT_r])
            P.op("dve", lambda e: e.tensor_tensor(out=hT_out, in0=PT[:],
                                                  in1=gfm[:].unsqueeze(2).to_broadcast([128, 8, 128]),
                                                  op=ALU.mult), reads=[PT_r, gfm_r], writes=[hT_r])

        def emit_tables(t, ang, ang2, ang_r, tabs_out, out_r):
            for tab, off in ((tabs_out[0], 0.0), (tabs_out[1], 0.25)):
                P.op("dve", lambda e, t=t: e.tensor_scalar(out=ang[:], in0=invf[:], scalar1=posf[:, t:t + 1],
                                                           scalar2=None, op0=ALU.mult),
                     reads=[const_r, pos_r], writes=[ang_r])
                P.op("dve", lambda e, off=off: e.tensor_scalar(out=ang2[:], in0=ang[:], scalar1=1.0 / TWO_PI,
                                                               scalar2=off, op0=ALU.mult, op1=ALU.add),
                     reads=[ang_r], writes=[ang_r])
                P.op("dve", lambda e: e.tensor_scalar(out=ang2[:], in0=ang2[:], scalar1=MAGIC, scalar2=None,
                                                      op0=ALU.add), reads=[ang_r], writes=[ang_r])
                P.op("dve", lambda e: e.tensor_scalar(out=ang2[:], in0=ang2[:], scalar1=-MAGIC, scalar2=None,
                                                      op0=ALU.add), reads=[ang_r], writes=[ang_r])
                for cw in (CW1, CW2):
                    P.op("dve", lambda e, cw=cw: e.scalar_tensor_tensor(out=ang[:], in0=ang2[:], scalar=-cw, in1=ang[:],
                                                                        op0=ALU.mult, op1=ALU.add),
                         reads=[ang_r], writes=[ang_r])
                P.op("dve", lambda e, off=off: e.tensor_scalar(out=ang[:], in0=ang[:], scalar1=off * TWO_PI,
                                                               scalar2=PI_LO, op0=ALU.add, op1=ALU.min),
                     reads=[ang_r], writes=[ang_r])
                P.op("dve", lambda e: e.tensor_scalar(out=ang[:], in0=ang[:], scalar1=-PI_LO, scalar2=None,
                                                      op0=ALU.max), reads=[ang_r], writes=[ang_r])
                P.op("act", lambda e, tab=tab: e.activation(out=tab, in_=ang[:], func=AF.Sin),
                     reads=[ang_r], writes=[out_r])

        pre_tabs = pair and layers[0][0] == "conv" and any(k == "ret" for k, _ in layers)
        if pre_tabs:
            tabs_d = nc.dram_tensor("tabs", [NT, 128, 256], F32, kind="Internal").ap()
            tabs_res = [Res(f"tabs{t}") for t in range(NT)]

        for li, (kind, slot) in enumerate(layers):
            is_first = li == 0
            is_last = li == len(layers) - 1
            x_src = x_ext if (is_first and first_ext) else xs
            x_dst = out_ext if (is_last and last_ext) else xs
            W = lw[li]

            with ExitStack() as st:
                gfm = sb(st, "gfm", [128, 8], F32)
                gfm_r = Res("gfm")
                P.dma("sp", "lc_g", gfm[:], W["gfm"], writes=[gfm_r])
                xin = [sb(st, f"xin{i}", [128, D], F32) for i in range(2)]
                xin_r = [Res(f"xin{i}") for i in range(2)]
                junk = sb(st, "junk", [128, D], BF16) if kind == "ret" else None
                ss = sb(st, "ss", [128, 1], F32)
                rstd = sb(st, "rstd", [128, 1], F32)
                small_r = Res("small")
                xn = sb(st, "xn", [128, D], BF16)
                xn_r = Res("xn")
                PT = [ps(st, f"PT{i}", [128, 8, 128], BF16) for i in range(2)]
                PT_r = [Res(f"PT{i}") for i in range(2)]

                if kind == "ret":
                    NF = 6
                    Fb = [ps(st, f"F{i}", [128, 512], F32) for i in range(NF)]
                    Fb_r = [Res(f"F{i}") for i in range(NF)]
                    hT = [sb(st, f"hT{i}", [128, 8, 128], BF16) for i in range(2)]
                    hT_r = [Res(f"hT{i}") for i in range(2)]
                    ang = sb(st, "ang", [128, 128], F32)
                    ang2 = sb(st, "ang2", [128, 128], F32)
                    ang3 = sb(st, "ang3", [128, 128], F32)
                    cstab = [sb(st, f"cstab{i}", [128, 2, 128], F32) for i in range(2)]
                    sint = [cstab[i][:, 0, :] for i in range(2)]
                    cost = [cstab[i][:, 1, :] for i in range(2)]
                    ang_r = Res("ang")
                    cs_r = [Res("cs0"), Res("cs1")]
                    tA = sb(st, "tA", [128, 2, 128], F32)
                    tB = sb(st, "tB", [128, 2, 128], F32)
                    tmp_r = Res("tmp")
                    qrot = sb(st, "qrot", [128, 1024], BF16)
                    krot = sb(st, "krot", [128, 1024], BF16)
                    qrot_r, krot_r = Res("qrot"), Res("krot")
                    khat = [sb(st, f"khat{i}", [128, H, DK], BF16) for i in range(2)]
                    khat_r = [Res("khat0"), Res("khat1")]
                    qT = [sb(st, f"qT{i}", [128, 8, 128], BF16) for i in range(2)]
                    kT = [sb(st, f"kT{i}", [128, 8, 128], BF16) for i in range(2)]
                    qT_r, kT_r = [Res("qT0"), Res("qT1")], [Res("kT0"), Res("kT1")]
                    v = [sb(st, f"v{i}", [128, E], BF16) for i in range(2)]
                    sg = [sb(st, f"sg{i}", [128, E], BF16) for i in range(2)]
                    v_r = [[Res(f"v{i}_{h}") for h in range(H)] for i in range(2)]
                    sg_r = [[Res(f"sg{i}_{h}") for h in range(H)] for i in range(2)]
                    Pm = sb(st, "Pm", [128, H * 128], BF16)
                    Pm_r = Res("Pm")
                    S = sb(st, "S", [128, 8, 512], F32)
                    Sbf = sb(st, "Sbf", [128, 8, 512], BF16)
                    S_r = [Res(f"S{i}") for i in range(8)]
                    Sbf_r = [Res(f"Sbf{i}") for i in range(8)]
                    ssq = sb(st, "ssq", [128, H], F32)
                    rs4 = sb(st, "rs4", [128, H], F32)
                    ssq_r = Res("ssq")
                    if not pair:
                        og = sb(st, "og", [128, E], BF16)
                        og_r = Res("og")
                        ogT = [sb(st, f"ogT{i}", [128, 16, 128], BF16) for i in range(2)]
                        ogT_r = [Res(f"ogT{i}") for i in range(2)]
                    else:
                        olb = [sb(st, f"olb{i}", [128, E], F32) for i in range(2)]
                        olb_r = [Res(f"olb{i}") for i in range(2)]

                    P.op("dve", lambda e: e.memset(S[:], 0.0), writes=S_r)
                    P.op("pool", lambda e: e.memset(Sbf[:], 0.0), writes=Sbf_r)
                    fctr = [0]

                    def load_x(t):
                        nb = t % 2
                        P.dma("sp", f"xin{nb}", xin[nb][:], x_src[t * 128:(t + 1) * 128, :],
                              reads=[xs_res[t]], writes=[xin_r[nb]])

                    def tables(t):
                        b = t % 2
                        if pre_tabs:
                            P.dma("sp", f"tab_ld{b}", cstab[b][:].rearrange("p a f -> p (a f)"), tabs_d[t],
                                  reads=[tabs_res[t]], writes=[cs_r[b]])
                        else:
                            emit_tables(t, ang, ang2, ang_r, (sint[b], cost[b]), cs_r[b])

                    def prep(t):
                        b = t % 2
                        tables(t)
                        rms_to_hT(st, xin[b], xin_r[b], junk, ss, rstd, xn, small_r, xn_r,
                                  PT[0], PT_r[0], hT[b][:], hT_r[b], gfm, gfm_r)
                        if t + 2 < NT:
                            load_x(t + 2)

                    def stage1(t):
                        b = t % 2
                        for n in (0, 4, 8, 1, 5, 9, 2, 6, 10, 3, 7, 11):
                            fi = fctr[0] % 4
                            fctr[0] += 1
                            F_, F_r = Fb[fi], Fb_r[fi]

                            def mm(e, n=n, F_=F_, b=b):
                                ins = None
                                for kc in range(8):
                                    ins = e.matmul(F_[:], lhsT=hT[b][:, kc, :], rhs=Win[:, kc, n * 512:(n + 1) * 512],
                                                   start=(kc == 0), stop=(kc == 7))
                                return ins
                            P.op("pe", mm, reads=[hT_r[b]] + Win_rs, writes=[F_r])
                            if n < 4:
                                dst, dst_r = (qrot, qrot_r) if n < 2 else (krot, krot_r)
                                c0 = (n % 2) * 512
                                Fv = F_[:].rearrange("p (h two f) -> p h two f", h=2, two=2)
                                t1, t2 = Fv[:, :, 0, :], Fv[:, :, 1, :]
                                dv_ = dst[:, c0:c0 + 512].rearrange("p (h two f) -> p h two f", h=2, two=2)
                                o1, o2 = dv_[:, :, 0, :], dv_[:, :, 1, :]
                                cb_ = cost[b].unsqueeze(1).to_broadcast([128, 2, 128])
                                sb_ = sint[b].unsqueeze(1).to_broadcast([128, 2, 128])
                                rd = [F_r, cs_r[b]]
                                P.op("dve", lambda e, t1=t1, cb_=cb_: e.tensor_tensor(out=tA[:], in0=t1, in1=cb_, op=ALU.mult),
                                     reads=rd, writes=[tmp_r])
                                P.op("dve", lambda e, t2=t2, sb_=sb_: e.tensor_tensor(out=tB[:], in0=t2, in1=sb_, op=ALU.mult),
                                     reads=rd, writes=[tmp_r])
                                P.op("dve", lambda e, o1=o1: e.tensor_tensor(out=o1, in0=tA[:], in1=tB[:], op=ALU.subtract),
                                     reads=[tmp_r], writes=[dst_r])
                                P.op("dve", lambda e, t1=t1, sb_=sb_: e.tensor_tensor(out=tA[:], in0=t1, in1=sb_, op=ALU.mult),
                                     reads=rd, writes=[tmp_r])
                                P.op("dve", lambda e, t2=t2, cb_=cb_: e.tensor_tensor(out=tB[:], in0=t2, in1=cb_, op=ALU.mult),
                                     reads=rd, writes=[tmp_r])
                                P.op("dve", lambda e, o2=o2: e.tensor_tensor(out=o2, in0=tA[:], in1=tB[:], op=ALU.add),
                                     reads=[tmp_r, F_r], writes=[dst_r])
                            elif n < 8:
                                h = n - 4
                                P.op("act", lambda e, h=h, F_=F_, b=b: e.activation(out=v[b][:, h * 512:(h + 1) * 512], in_=F_[:], func=AF.Copy),
                                     reads=[F_r], writes=[v_r[b][h]])
                            else:
                                h = n - 8
                                P.op("act", lambda e, h=h, F_=F_, b=b: e.activation(out=sg[b][:, h * 512:(h + 1) * 512], in_=F_[:], func=AF.Silu),
                                     reads=[F_r], writes=[sg_r[b][h]])

                    def stage1b(t):
                        b = t % 2

                        def trq(e):
                            ins = None
                            for c in range(8):
                                ins = e.transpose(out=PT[0][:, c, :], in_=qrot[:, c * 128:(c + 1) * 128], identity=ident[:])
                            return ins
                        P.op("pe", trq, reads=[qrot_r, const_r], writes=[PT_r[0]])
                        P.op("dve", lambda e, b=b: e.tensor_tensor(
                            out=qT[b][:].rearrange("p (h two) t -> p h two t", two=2),
                            in0=PT[0][:].rearrange("p (h two) t -> p h two t", two=2),
                            in1=qdT[:].unsqueeze(2).to_broadcast([128, H, 2, 128]), op=ALU.mult),
                            reads=[PT_r[0], const_r], writes=[qT_r[b]])

                        def trk(e):
                            ins = None
                            for c in range(8):
                                ins = e.transpose(out=PT[1][:, c, :], in_=krot[:, c * 128:(c + 1) * 128], identity=ident[:])
                            return ins
                        P.op("pe", trk, reads=[krot_r, const_r], writes=[PT_r[1]])
                        P.op("act", lambda e, b=b: e.activation(out=kT[b][:], in_=PT[1][:], func=AF.Copy),
                             reads=[PT_r[1]], writes=[kT_r[b]])
                        P.op("pool", lambda e, b=b: e.tensor_tensor(
                            out=khat[b][:], in0=krot[:].rearrange("p (h d) -> p h d", h=H),
                            in1=kd[:].unsqueeze(2).to_broadcast([128, H, DK]), op=ALU.mult),
                            reads=[krot_r, const_r], writes=[khat_r[b]])

                    def stage2a(t):
                        b = t % 2
                        Fs, Fs_r = Fb[4], Fb_r[4]

                        def sc(e, Fs=Fs, b=b):
                            ins = None
                            for h in range(H):
                                for dc in range(2):
                                    ins = e.matmul(Fs[:, h * 128:(h + 1) * 128], lhsT=kT[b][:, 2 * h + dc, :],
                                                   rhs=qT[b][:, 2 * h + dc, :], start=(dc == 0), stop=(dc == 1))
                            return ins
                        P.op("pe", sc, reads=[kT_r[b], qT_r[b]], writes=[Fs_r])
                        P.op("dve", lambda e, Fs=Fs: e.tensor_tensor(out=Pm[:], in0=Fs[:], in1=maskT[:], op=ALU.mult),
                             reads=[Fs_r, const_r], writes=[Pm_r])

                    def stage2b(t):
                        b = ob = t % 2
                        for h in range(H):
                            fi = fctr[0] % 4
                            fctr[0] += 1
                            Fo, Fo_r = Fb[fi], Fb_r[fi]

                            def om(e, h=h, Fo=Fo, b=b):
                                e.matmul(Fo[:], lhsT=Pm[:, h * 128:(h + 1) * 128], rhs=v[b][:, h * 512:(h + 1) * 512],
                                         start=True, stop=False)
                                e.matmul(Fo[:], lhsT=qT[b][:, 2 * h, :], rhs=Sbf[:, 2 * h, :], start=False, stop=False)
                                return e.matmul(Fo[:], lhsT=qT[b][:, 2 * h + 1, :], rhs=Sbf[:, 2 * h + 1, :],
                                                start=False, stop=True)
                            P.op("pe", om, reads=[Pm_r, v_r[b][h], qT_r[b], Sbf_r[2 * h], Sbf_r[2 * h + 1]], writes=[Fo_r])
                            if pair:
                                if h % 2 == 0:
                                    P.op("act", lambda e, h=h, Fo=Fo, ob=ob: e.activation(
                                        out=olb[ob][:, h * 512:(h + 1) * 512], in_=Fo[:], func=AF.Copy),
                                        reads=[Fo_r], writes=[olb_r[ob]])
                                else:
                                    P.op("dve", lambda e, h=h, Fo=Fo, ob=ob: e.tensor_copy(
                                        out=olb[ob][:, h * 512:(h + 1) * 512], in_=Fo[:]),
                                        reads=[Fo_r], writes=[olb_r[ob]])
                            else:
                                P.op("dve", lambda e, h=h: e.memset(ssq[:, h:h + 1], 0.0), writes=[ssq_r])
                                P.op("act", lambda e, h=h, Fo=Fo: e.activation(out=junk[:, 0:512], in_=Fo[:], func=AF.Square,
                                                                              accum_out=ssq[:, h:h + 1]),
                                     reads=[Fo_r], writes=[ssq_r, small_r])
                                P.op("act", lambda e, h=h: e.activation(out=rs4[:, h:h + 1], in_=ssq[:, h:h + 1], func=AF.Sqrt,
                                                                        bias=epsc[:, 0:1], scale=1.0 / DV),
                                     reads=[ssq_r, pos_r], writes=[ssq_r])
                                P.op("dve", lambda e, h=h: e.reciprocal(out=rs4[:, h:h + 1], in_=rs4[:, h:h + 1]),
                                     reads=[ssq_r], writes=[ssq_r])
                                P.op("dve", lambda e, h=h, Fo=Fo, b=b: e.scalar_tensor_tensor(
                                    out=og[:, h * 512:(h + 1) * 512], in0=Fo[:], scalar=rs4[:, h:h + 1],
                                    in1=sg[b][:, h * 512:(h + 1) * 512], op0=ALU.mult, op1=ALU.mult),
                                    reads=[Fo_r, ssq_r, sg_r[b][h]], writes=[og_r])
                            for dc in range(2):
                                i = 2 * h + dc
                                fi = 4 + (fctr[0] % 2)
                                fctr[0] += 1
                                Fk, Fk_r = Fb[fi], Fb_r[fi]
                                P.op("pe", lambda e, h=h, dc=dc, Fk=Fk, b=b: e.matmul(
                                    Fk[:], lhsT=khat[b][:, h, dc * 128:(dc + 1) * 128], rhs=v[b][:, h * 512:(h + 1) * 512],
                                    start=True, stop=True), reads=[khat_r[b], v_r[b][h]], writes=[Fk_r])
                                P.op("dve", lambda e, i=i, h=h, Fk=Fk: e.scalar_tensor_tensor(
                                    out=S[:, i, :], in0=S[:, i, :], scalar=float(CDEC[h]), in1=Fk[:],
                                    op0=ALU.mult, op1=ALU.add), reads=[Fk_r, S_r[i]], writes=[S_r[i]])
                                P.op("act", lambda e, i=i: e.activation(out=Sbf[:, i, :], in_=S[:, i, :], func=AF.Copy),
                                     reads=[S_r[i]], writes=[Sbf_r[i]])
                        if pair:
                            P.dma("pool", f"ol_st{ob}", ol[t], olb[ob][:], reads=[olb_r[ob]], writes=[ol_res[t]])
                            P.dma("pool", f"qs_st{b}", qs[t], qT[b][:].rearrange("p k t -> p (k t)"), reads=[qT_r[b]],
                                  writes=[qs_res[t]])
                            P.dma("pool", f"sg_st{b}", sgs[t], sg[b][:], reads=sg_r[b], writes=[sgs_res[t]])
                        if not pair:
                            for half in range(2):
                                def tro(e, half=half):
                                    ins = None
                                    for c in range(8):
                                        cc = half * 8 + c
                                        ins = e.transpose(out=PT[half][:, c, :], in_=og[:, cc * 128:(cc + 1) * 128],
                                                          identity=ident[:])
                                    return ins
                                P.op("pe", tro, reads=[og_r, const_r], writes=[PT_r[half]])
                                if half == 0:
                                    P.op("act", lambda e, ob=ob: e.activation(out=ogT[ob][:, 0:8, :], in_=PT[0][:], func=AF.Copy),
                                         reads=[PT_r[0]], writes=[ogT_r[ob]])
                                else:
                                    P.op("dve", lambda e, ob=ob: e.tensor_copy(out=ogT[ob][:, 8:16, :], in_=PT[1][:]),
                                         reads=[PT_r[1]], writes=[ogT_r[ob]])
                            P.dma("pool", f"ys_st{ob}", ys[t], ogT[ob][:].rearrange("p k t -> p (k t)"),
                                  reads=[ogT_r[ob]], writes=[ys_res[t]])

                    load_x(0)
                    if NT > 1:
                        load_x(1)
                    prep(0)
                    if NT > 1:
                        prep(1)
                    stage1(0)
                    stage1b(0)
                    for t in range(NT):
                        if t + 2 < NT:
                            prep(t + 2)
                        if t + 1 < NT:
                            stage1(t + 1)
                        stage2a(t)
                        if t + 1 < NT:
                            stage1b(t + 1)
                        stage2b(t)
                    if pair:
                        P.dma("pool", "ccs_st", ccs_in[li].ap(), S[:].rearrange("p i e -> p (i e)"), reads=S_r, writes=[ccs_r])
                        P.collective("ccs", "AllGather", RG, ccs_in[li].ap().opt(), ccs_out[li].ap().opt(),
                                     reads=[ccs_r], writes=[ccs_r])
                else:
                    G = 512 if T >= 512 else T
                    NG = T // G
                    TPG = G // 128
                    NF = 6
                    Fb = [ps(st, f"F{i}", [128, 512], F32) for i in range(NF)]
                    Fb_r = [Res(f"F{i}") for i in range(NF)]
                    hT = sb(st, "hTg", [128, 8, G], BF16)
                    hT_r = Res("hTg")
                    U = sb(st, "U", [128, 16, HALO + G], BF16)
                    U_r = [Res(f"U{c}") for c in range(16)]
                    SZ = sb(st, "SZ", [128, 16, G], BF16)
                    SZ_r = [Res(f"SZ{c}") for c in range(16)]
                    csb = sb(st, "csb", [128, 16, G], F32)
                    csb_r = [Res(f"csb{c}") for c in range(16)]
                    sigb = [sb(st, f"sigb{i}", [128, G], F32) for i in range(2)]
                    sigb_r = [Res(f"sigb{i}") for i in range(2)]
                    csq = [sb(st, f"csq{i}", [128, G], BF16) for i in range(2)]
                    chi = [sb(st, f"chi{i}", [128, G], BF16) for i in range(2)]
                    cst_r = [Res(f"cst{i}") for i in range(2)]
                    y1 = [sb(st, f"y1{i}", [128, G], BF16) for i in range(2)]
                    y1_r = [Res(f"y1{i}") for i in range(2)]
                    ND = 3
                    NDT = 8
                    Dg = [sb(st, f"Dg{i}", [128, NDT, 128], BF16) for i in range(ND)]
                    Dg_r = [Res(f"Dg{i}") for i in range(ND)]
                    mean = sb(st, "mean", [128, G], F32)
                    msq = sb(st, "msq", [128, G], F32)
                    rstdl = msq
                    stat_r = Res("stat")
                    dww = sb(st, "dww", [128, 16, KW], F32)
                    dwb = sb(st, "dwb", [128, 16], F32)
                    lng = sb(st, "lng", [128, 16], F32)
                    lnb = sb(st, "lnb", [128, 16], F32)
                    cw_r = Res("cw")
                    P.dma("sp", "lc_c", dww[:].rearrange("p c j -> p (c j)"), W["dww"], writes=[cw_r])
                    P.dma("sp", "lc_c", dwb[:], W["dwb"], writes=[cw_r])
                    P.dma("sp", "lc_c", lng[:], W["lng"], writes=[cw_r])
                    P.dma("sp", "lc_c", lnb[:], W["lnb"], writes=[cw_r])
                    P.dma("sp", "xin0", xin[0][:], x_src[0:128, :], reads=[xs_res[0]], writes=[xin_r[0]])
                    if not pair:
                        P.op("pool", lambda e: e.memset(U[:, :, 0:HALO], 0.0), writes=U_r)
                    else:
                        xp_src = xprev0_d if is_first else ccx_out.ap()[0:128, :]
                        P.dma("sp", "xin1", xin[1][:], xp_src, reads=[ccx_r], writes=[xin_r[1]])
                        P.op("dve", lambda e: e.tensor_scalar(out=xin[1][:], in0=xin[1][:], scalar1=flag[:, 0:1], scalar2=None,
                                                              op0=ALU.mult), reads=[xin_r[1], const_r], writes=[xin_r[1]])
                        rms_to_hT(st, xin[1], xin_r[1], xn, ss, rstd, xn, small_r, xn_r,
                                  PT[0], PT_r[0], hT[:, :, 0:128], hT_r, gfm, gfm_r, junk_r=xn_r)
                        for cb in range(16):
                            s3 = (cb % 2) * 3
                            Fa, Fg = Fb[s3], Fb[s3 + 1]
                            Fa_r, Fg_r = Fb_r[s3], Fb_r[s3 + 1]
                            for F_, F_r, c0 in ((Fa, Fa_r, cb * 128), (Fg, Fg_r, E + cb * 128)):
                                def mmh(e, F_=F_, c0=c0):
                                    ins = None
                                    for kc in range(8):
                                        ins = e.matmul(F_[:, 0:128], lhsT=Win[:, kc, c0:c0 + 128], rhs=hT[:, kc, 0:128],
                                                       start=(kc == 0), stop=(kc == 7))
                                    return ins
                                P.op("pe", mmh, reads=[hT_r] + Win_rs, writes=[F_r])
                            sbi = cb % 2
                            P.op("act", lambda e, Fg=Fg, sbi=sbi: e.activation(out=sigb[sbi][:, 0:128], in_=Fg[:, 0:128], func=AF.Sigmoid),
                                 reads=[Fg_r], writes=[sigb_r[sbi]])
                            P.op("dve", lambda e, Fa=Fa, sbi=sbi, cb=cb: e.tensor_tensor(
                                out=U[:, cb, 0:HALO], in0=Fa[:, 128 - HALO:128], in1=sigb[sbi][:, 128 - HALO:128], op=ALU.mult),
                                reads=[Fa_r, sigb_r[sbi]], writes=[U_r[cb]])
                    dctr = [0]
                    Fsum, Fsq = Fb[4], Fb[5]
                    Fsum_r, Fsq_r = Fb_r[4], Fb_r[5]

                    def prep_tile(g, tt, part=3):
                        t = g * TPG + tt
                        b = t % 2
                        if (part & 1) and t + 1 < NT:
                            nb = (t + 1) % 2
                            P.dma("sp", f"xin{nb}", xin[nb][:], x_src[(t + 1) * 128:(t + 2) * 128, :],
                                  reads=[xs_res[t + 1]], writes=[xin_r[nb]])
                        rms_to_hT(st, xin[b], xin_r[b], xn, ss, rstd, xn, small_r, xn_r,
                                  PT[0], PT_r[0], hT[:, :, tt * 128:(tt + 1) * 128], hT_r, gfm, gfm_r, junk_r=xn_r, part=part)

                    def prep_g(g):
                        for tt in range(TPG):
                            prep_tile(g, tt)

                    def proj(F_, F_r, c0):
                        def mm(e, F_=F_, c0=c0):
                            ins = None
                            for kc in range(8):
                                ins = e.matmul(F_[:, 0:G], lhsT=Win[:, kc, c0:c0 + 128], rhs=hT[:, kc, :],
                                               start=(kc == 0), stop=(kc == 7))
                            return ins
                        P.op("pe", mm, reads=[hT_r] + Win_rs, writes=[F_r])

                    def ab(g, cb):
                        s2 = (cb % 2) * 2
                        Fa, Fg = Fb[s2], Fb[s2 + 1]
                        Fa_r, Fg_r = Fb_r[s2], Fb_r[s2 + 1]
                        proj(Fa, Fa_r, cb * 128)
                        proj(Fg, Fg_r, E + cb * 128)
                        sbi = cb % 2
                        P.op("act", lambda e, Fg=Fg, sbi=sbi: e.activation(out=sigb[sbi][:], in_=Fg[:, 0:G], func=AF.Sigmoid),
                             reads=[Fg_r], writes=[sigb_r[sbi]])
                        P.op("dve", lambda e, Fa=Fa, sbi=sbi, cb=cb: e.tensor_tensor(
                            out=U[:, cb, HALO:HALO + G], in0=Fa[:, 0:G], in1=sigb[sbi][:], op=ALU.mult),
                            reads=[Fa_r, sigb_r[sbi]], writes=[U_r[cb]])

                    def zph(g, cb):
                        Fz, Fz_r = Fb[cb % 4], Fb_r[cb % 4]
                        proj(Fz, Fz_r, 2 * E + cb * 128)
                        P.op("act", lambda e, Fz=Fz, cb=cb: e.activation(out=SZ[:, cb, :], in_=Fz[:, 0:G], func=AF.Silu),
                             reads=[Fz_r], writes=[SZ_r[cb]])

                    def conv_phase(g):
                        pend_stm = []
                        for cb in range(16):
                            Fc, Fc_r = Fb[cb % 2], Fb_r[cb % 2]
                            for j0 in range(0, KW, NDT):
                                nj = min(NDT, KW - j0)
                                di = dctr[0] % ND
                                dctr[0] += 1
                                P.op("dve", lambda e, di=di, cb=cb, j0=j0, nj=nj: e.tensor_tensor(
                                    out=Dg[di][:, 0:nj, :], in0=ident[:].unsqueeze(1).to_broadcast([128, nj, 128]),
                                    in1=dww[:, cb, j0:j0 + nj].unsqueeze(2).to_broadcast([128, nj, 128]), op=ALU.mult),
                                    reads=[const_r, cw_r], writes=[Dg_r[di]])

                                def cv(e, di=di, cb=cb, j0=j0, nj=nj, Fc=Fc):
                                    ins = None
                                    for n in range(nj):
                                        j = j0 + n
                                        ins = e.matmul(Fc[:, 0:G], lhsT=Dg[di][:, n, :], rhs=U[:, cb, j:j + G],
                                                       start=(j == 0), stop=(j == KW - 1))
                                    return ins
                                P.op("pe", cv, reads=[Dg_r[di], U_r[cb]], writes=[Fc_r])
                            ci = cb % 2
                            P.op("act", lambda e, cb=cb, Fc=Fc: e.activation(out=csb[:, cb, :], in_=Fc[:, 0:G], func=AF.Identity,
                                                                            bias=dwb[:, cb:cb + 1], scale=1.0),
                                 reads=[Fc_r, cw_r], writes=[csb_r[cb]])
                            P.op("act", lambda e, cb=cb, ci=ci: e.activation(out=csq[ci][:], in_=csb[:, cb, :], func=AF.Square),
                                 reads=[csb_r[cb]], writes=[cst_r[ci]])
                            P.op("act", lambda e, cb=cb, ci=ci: e.activation(out=chi[ci][:], in_=csb[:, cb, :], func=AF.Identity),
                                 reads=[csb_r[cb]], writes=[cst_r[ci]])

                            def stm(e, cb=cb, ci=ci):
                                e.matmul(Fsum[:, 0:G], lhsT=ones[:], rhs=chi[ci][:], start=(cb == 0), stop=(cb == 15))
                                return e.matmul(Fsq[:, 0:G], lhsT=ones[:], rhs=csq[ci][:], start=(cb == 0), stop=(cb == 15))
                            pend_stm.append((stm, ci))
                            if len(pend_stm) > 1:
                                f_, ci_ = pend_stm.pop(0)
                                P.op("pe", f_, reads=[cst_r[ci_], const_r], writes=[Fsum_r, Fsq_r])
                            if g + 1 < NG and cb // 4 < TPG:
                                if cb % 4 == 0:
                                    prep_tile(g + 1, cb // 4, part=1)
                                elif cb % 4 == 2:
                                    prep_tile(g + 1, cb // 4, part=2)
                        while pend_stm:
                            f_, ci_ = pend_stm.pop(0)
                            P.op("pe", f_, reads=[cst_r[ci_], const_r], writes=[Fsum_r, Fsq_r])
                        if g + 1 < NG:
                            P.op("pool", lambda e: e.tensor_copy(out=U[:, :, 0:HALO], in_=U[:, :, G:G + HALO]),
                                 reads=U_r, writes=U_r)
                        P.op("dve", lambda e: e.tensor_scalar(out=mean[:], in0=Fsum[:, 0:G], scalar1=1.0 / E, scalar2=None,
                                                              op0=ALU.mult), reads=[Fsum_r], writes=[stat_r])
                        P.op("dve", lambda e: e.tensor_tensor(out=msq[:], in0=mean[:], in1=mean[:], op=ALU.mult),
                             reads=[stat_r], writes=[stat_r])
                        P.op("dve", lambda e: e.scalar_tensor_tensor(out=rstdl[:], in0=Fsq[:, 0:G], scalar=1.0 / E, in1=msq[:],
                                                                     op0=ALU.mult, op1=ALU.subtract),
                             reads=[Fsq_r, stat_r], writes=[stat_r])
                        P.op("act", lambda e: e.activation(out=rstdl[:], in_=rstdl[:], func=AF.Sqrt, bias=epsc[:, 0:1], scale=1.0),
                             reads=[stat_r, pos_r], writes=[stat_r])
                        P.op("dve", lambda e: e.reciprocal(out=rstdl[:], in_=rstdl[:]), reads=[stat_r], writes=[stat_r])

                    def tail1(g, cb):
                        yi = cb % 2
                        P.op("dve", lambda e, cb=cb: e.tensor_tensor(out=csb[:, cb, :], in0=csb[:, cb, :], in1=mean[:],
                                                                     op=ALU.subtract),
                             reads=[csb_r[cb], stat_r], writes=[csb_r[cb]])
                        P.op("dve", lambda e, cb=cb: e.tensor_tensor(out=csb[:, cb, :], in0=csb[:, cb, :], in1=rstdl[:],
                                                                     op=ALU.mult),
                             reads=[csb_r[cb], stat_r], writes=[csb_r[cb]])
                        P.op("act", lambda e, cb=cb, yi=yi: e.activation(out=y1[yi][:], in_=csb[:, cb, :], func=AF.Sigmoid,
                                                                        bias=lnb[:, cb:cb + 1], scale=lng[:, cb:cb + 1]),
                             reads=[csb_r[cb], cw_r], writes=[y1_r[yi]])
                        P.op("act", lambda e, cb=cb: e.activation(out=csb[:, cb, :], in_=csb[:, cb, :], func=AF.Identity,
                                                                  bias=lnb[:, cb:cb + 1], scale=lng[:, cb:cb + 1]),
                             reads=[csb_r[cb], cw_r], writes=[csb_r[cb]])

                    def tail2(g, cb):
                        yi = cb % 2
                        P.op("dve", lambda e, cb=cb, yi=yi: e.tensor_tensor(out=csb[:, cb, :], in0=csb[:, cb, :], in1=y1[yi][:],
                                                                            op=ALU.mult),
                             reads=[csb_r[cb], y1_r[yi]], writes=[csb_r[cb]])
                        P.op("dve", lambda e, cb=cb: e.tensor_tensor(out=SZ[:, cb, :], in0=SZ[:, cb, :], in1=csb[:, cb, :],
                                                                     op=ALU.mult),
                             reads=[csb_r[cb], SZ_r[cb]], writes=[SZ_r[cb]])

                    def store_q(g, q):
                        for tt in range(TPG):
                            t = g * TPG + tt
                            P.dma("pool", f"ys_st{q}", ys[t].rearrange("p (k t) -> p k t", k=16)[:, 4 * q:4 * q + 4, :],
                                  SZ[:, 4 * q:4 * q + 4, tt * 128:(tt + 1) * 128], reads=SZ_r[4 * q:4 * q + 4],
                                  writes=[ysq_res[t][q]])

                    prep_g(0)
                    for cb in range(16):
                        ab(0, cb)
                    for g in range(NG):
                        for cb in range(16):
                            zph(g, cb)
                        conv_phase(g)
                        tail1(g, 0)
                        for cb in range(16):
                            if g + 1 < NG:
                                ab(g + 1, cb)
                            if cb + 1 < 16:
                                tail1(g, cb + 1)
                            tail2(g, cb)
                            if cb % 4 == 3:
                                store_q(g, cb // 4)
                P.barrier()
                P.flush()

            with ExitStack() as st:
                rp = pair and kind == "ret"
                send_x = pair and (li + 1 < len(layers)) and layers[li + 1][0] == "conv"
                Wout = sb(st, "Wout", [128, 16, D], BF16)
                Wout_r = Res("Wout")
                P.dma("pool", "wout", Wout[:], W["w_out"].rearrange("(kc p) n -> p kc n", p=128), writes=[Wout_r])
                if li + 1 < len(layers):
                    load_win(li + 1, after=[Wout_r])
                NX = 3
                xr = [sb(st, f"xr{i}", [128, D], F32) for i in range(NX)]
                xr_r = [Res(f"xr{i}") for i in range(NX)]
                fin = do_final and is_last
                if not rp:
                    yt = [sb(st, f"yt{i}", [128, 16, 128], BF16) for i in range(2)]
                    yt_r = [Res(f"yt{i}") for i in range(2)]
                    Gb = [ps(st, f"G{i}", [128, 2, 512], F32) for i in range(2)]
                    Gb_r = [Res(f"G{i}") for i in range(2)]
                else:
                    olbB = sb(st, "olbB", [128, 2, E], F32)
                    olbB_r = [Res("olbB0"), Res("olbB1")]
                    qsb = [sb(st, f"qsb{i}", [128, 8, 128], BF16) for i in range(2)]
                    qsb_r = [Res(f"qsb{i}") for i in range(2)]
                    sgb = [sb(st, f"sgb{i}", [128, E], BF16) for i in range(2)]
                    sgb_r = [Res(f"sgb{i}") for i in range(2)]
                    ogBs = [sb(st, f"ogB{i}", [128, E], BF16) for i in range(2)]
                    ogBs_r = [Res("ogB0"), Res("ogB1")]
                    ogB, ogB_r = ogBs[0], ogBs_r[0]
                    ogTB = sb(st, "ogTB", [128, 16, 128], BF16)
                    ogTB_r = Res("ogTB")
                    SinBf = sb(st, "SinBf", [128, 8, 512], BF16)
                    SinBf_r = Res("SinBf")
                    ssqB = sb(st, "ssqB", [128, H], F32)
                    rs4B = sb(st, "rs4B", [128, H], F32)
                    ssqB_r = Res("ssqB")
                    Cb = [ps(st, f"C{i}", [128, 512], F32) for i in range(H)]
                    Cb_r = [Res(f"C{i}") for i in range(H)]
                    PTB = [ps(st, f"PTB{i}", [128, 8, 128], BF16) for i in range(2)]
                    PTB_r = [Res(f"PTB{i}") for i in range(2)]
                    Gb = [ps(st, "G0", [128, 2, 512], F32)]
                    Gb_r = [Res("G0")]
                    P.dma("sp", "sin_ld", olbB[:].rearrange("p a e -> p (a e)"), ccs_out[li].ap()[0:128, :],
                          reads=[ccs_r], writes=olbB_r)
                    P.op("dve", lambda e: e.tensor_scalar(out=SinBf[:].rearrange("p i e -> p (i e)"),
                                                          in0=olbB[:].rearrange("p a e -> p (a e)"),
                                                          scalar1=flag[:, 0:1], scalar2=None, op0=ALU.mult),
                         reads=olbB_r + [const_r], writes=[SinBf_r])
                if fin:
                    gfin = sb(st, "gfinS", [128, D], F32)
                    gfin_r = Res("gfin")
                    P.dma("sp", "lc_f", gfin[:], gfin_d, writes=[gfin_r])
                    ss = sb(st, "ssB", [128, 1], F32)
                    rstd = sb(st, "rstdB", [128, 1], F32)
                    small_r = Res("smallB")
                    junk = sb(st, "junkB", [128, D], BF16)
                    fj, fj_r = junk[:], small_r

                def loads(t):
                    b, b3 = t % 2, t % NX
                    if not rp:
                        P.dma("sp", f"yt{b}", yt[b][:].rearrange("p k t -> p (k t)"), ys[t], reads=[ys_res[t]] + ysq_res[t], writes=[yt_r[b]])
                    else:
                        P.dma("sp", f"olB{b}", olbB[:, b, :], ol[t], reads=[ol_res[t]], writes=[olbB_r[b]])
                        P.dma("sp", f"qsB{b}", qsb[b][:].rearrange("p k t -> p (k t)"), qs[t], reads=[qs_res[t]], writes=[qsb_r[b]])
                        P.dma("sp", f"sgB{b}", sgb[b][:], sgs[t], reads=[sgs_res[t]], writes=[sgb_r[b]])
                    P.dma("sp", f"xr{b3}", xr[b3][:], x_src[t * 128:(t + 1) * 128, :], reads=[xs_res[t]], writes=[xr_r[b3]])
                def B1pe(t):
                    b = t % 2
                    for h in range(H):
                        def cm(e, h=h, b=b):
                            e.matmul(Cb[h][:], lhsT=qsb[b][:, 2 * h, :], rhs=SinBf[:, 2 * h, :], start=True, stop=False)
                            return e.matmul(Cb[h][:], lhsT=qsb[b][:, 2 * h + 1, :], rhs=SinBf[:, 2 * h + 1, :],
                                            start=False, stop=True)
                        P.op("pe", cm, reads=[qsb_r[b], SinBf_r], writes=[Cb_r[h]])

                def B1(t):
                    b = t % 2
                    ogx, ogx_r = ogBs[b], ogBs_r[b]
                    P.op("dve", lambda e: e.memset(ssqB[:], 0.0), writes=[ssqB_r])
                    for h in range(H):
                        osl = olbB[:, b, h * 512:(h + 1) * 512]
                        P.op("dve", lambda e, h=h, osl=osl, t=t: e.scalar_tensor_tensor(
                            out=osl, in0=Cb[h][:], scalar=float(CDEC[h] ** t), in1=osl, op0=ALU.mult, op1=ALU.add),
                            reads=[Cb_r[h], olbB_r[b]], writes=[olbB_r[b]])
                        P.op("act", lambda e, h=h, osl=osl, ogx=ogx: e.activation(out=ogx[:, h * 512:(h + 1) * 512], in_=osl, func=AF.Square,
                                                                                  accum_out=ssqB[:, h:h + 1]),
                             reads=[olbB_r[b]], writes=[ssqB_r, ogx_r])
                    P.op("act", lambda e: e.activation(out=rs4B[:], in_=ssqB[:], func=AF.Sqrt,
                                                       bias=epsc[:, 0:1], scale=1.0 / DV),
                         reads=[ssqB_r, pos_r], writes=[ssqB_r])
                    P.op("dve", lambda e: e.reciprocal(out=rs4B[:], in_=rs4B[:]), reads=[ssqB_r], writes=[ssqB_r])
                    for h in range(H):
                        osl = olbB[:, b, h * 512:(h + 1) * 512]
                        P.op("dve", lambda e, h=h, osl=osl, b=b, ogx=ogx: e.scalar_tensor_tensor(
                            out=ogx[:, h * 512:(h + 1) * 512], in0=osl, scalar=rs4B[:, h:h + 1],
                            in1=sgb[b][:, h * 512:(h + 1) * 512], op0=ALU.mult, op1=ALU.mult),
                            reads=[olbB_r[b], ssqB_r, sgb_r[b]], writes=[ogx_r])

                mk_tabs = pre_tabs and li == 0
                if mk_tabs:
                    angB = sb(st, "angB", [128, 128], F32)
                    ang2B = sb(st, "ang2B", [128, 128], F32)
                    angB_r = Res("angB")
                    tabB = [sb(st, f"tabB{i}", [128, 2, 128], F32) for i in range(2)]
                    tabB_r = [Res("tabB0"), Res("tabB1")]
                loads(0)
                if NT > 1:
                    loads(1)
                if rp:
                    B1pe(0)
                    B1(0)
                for t in range(NT):
                    b, b3 = t % 2, t % NX
                    if rp:
                        if t + 2 < NT:
                            loads(t + 2)
                        if t + 1 < NT:
                            B1pe(t + 1)
                    if not rp:
                        Gt, Gt_r = Gb[b], Gb_r[b]

                        def mm(e, b=b, Gt=Gt):
                            ins = None
                            for n in range(2):
                                for kc in range(16):
                                    ins = e.matmul(Gt[:, n, :], lhsT=yt[b][:, kc, :], rhs=Wout[:, kc, n * 512:(n + 1) * 512],
                                                   start=(kc == 0), stop=(kc == 15))
                            return ins
                        P.op("pe", mm, reads=[yt_r[b], Wout_r], writes=[Gt_r])
                        if t + 2 < NT:
                            loads(t + 2)
                    else:
                        ogx, ogx_r = ogBs[b], ogBs_r[b]
                        for half in range(2):
                            def tro(e, half=half, ogx=ogx):
                                ins = None
                                for c in range(8):
                                    cc = half * 8 + c
                                    ins = e.transpose(out=PTB[half][:, c, :], in_=ogx[:, cc * 128:(cc + 1) * 128],
                                                      identity=ident[:])
                                return ins
                            P.op("pe", tro, reads=[ogx_r, const_r], writes=[PTB_r[half]])
                            if half == 0:
                                P.op("act", lambda e: e.activation(out=ogTB[:, 0:8, :], in_=PTB[0][:], func=AF.Copy),
                                     reads=[PTB_r[0]], writes=[ogTB_r])
                            else:
                                P.op("dve", lambda e: e.tensor_copy(out=ogTB[:, 8:16, :], in_=PTB[1][:]),
                                     reads=[PTB_r[1]], writes=[ogTB_r])
                        Gt, Gt_r = Gb[0], Gb_r[0]

                        def mm(e, Gt=Gt):
                            ins = None
                            for n in range(2):
                                for kc in range(16):
                                    ins = e.matmul(Gt[:, n, :], lhsT=ogTB[:, kc, :], rhs=Wout[:, kc, n * 512:(n + 1) * 512],
                                                   start=(kc == 0), stop=(kc == 15))
                            return ins
                        P.op("pe", mm, reads=[ogTB_r, Wout_r], writes=[Gt_r])
                        if t + 1 < NT:
                            B1(t + 1)
                    P.op("dve", lambda e, b3=b3, Gt=Gt: e.tensor_tensor(
                        out=xr[b3][:], in0=xr[b3][:], in1=Gt[:].rearrange("p n c -> p (n c)"), op=ALU.add),
                        reads=[Gt_r, xr_r[b3]], writes=[xr_r[b3]])
                    if mk_tabs:
                        emit_tables(t, angB, ang2B, angB_r, (tabB[b][:, 0, :], tabB[b][:, 1, :]), tabB_r[b])
                        P.dma("pool", f"tab_st{b}", tabs_d[t], tabB[b][:].rearrange("p a f -> p (a f)"),
                              reads=[tabB_r[b]], writes=[tabs_res[t]])
                    if send_x and t == NT - 1:
                        P.dma("pool", "ccx_st", ccx_in.ap(), xr[b3][:], reads=[xr_r[b3]], writes=[ccx_r])
                        P.collective("ccx", "AllGather", RG, ccx_in.ap().opt(), ccx_out.ap().opt(), reads=[ccx_r], writes=[ccx_r])
                    if fin:
                        P.op("dve", lambda e: e.memset(ss[:], 0.0), writes=[small_r])
                        P.op("act", lambda e, b3=b3: e.activation(out=fj, in_=xr[b3][:], func=AF.Square, accum_out=ss[:, 0:1]),
                             reads=[xr_r[b3]], writes=[small_r, fj_r])
                        P.op("act", lambda e: e.activation(out=rstd[:], in_=ss[:], func=AF.Sqrt, bias=epsc[:, 0:1], scale=1.0 / D),
                             reads=[small_r, pos_r], writes=[small_r])
                        P.op("dve", lambda e: e.reciprocal(out=rstd[:], in_=rstd[:]), reads=[small_r], writes=[small_r])
                        P.op("dve", lambda e, b3=b3: e.scalar_tensor_tensor(
                            out=xr[b3][:], in0=xr[b3][:], scalar=rstd[:, 0:1], in1=gfin[:], op0=ALU.mult, op1=ALU.mult),
                            reads=[xr_r[b3], small_r, gfin_r], writes=[xr_r[b3]])
                    P.dma("pool", f"xst{b3}", x_dst[t * 128:(t + 1) * 128, :], xr[b3][:], reads=[xr_r[b3]],
                          writes=[xs_res[t]])
                if is_last:
                    P.final_wait()
                else:
                    P.barrier()
                P.flush()
    return nc


def _consts():
    bf = ml_dtypes.bfloat16
    ident = np.eye(128, dtype=np.float32).astype(bf)
    ones = np.ones((128, 128), dtype=np.float32).astype(bf)
    inv_freq = (10000.0 ** (-np.arange(128, dtype=np.float32) / 128.0)).astype(np.float32)
    invf = np.broadcast_to(inv_freq[None, :], (128, 128)).astype(np.float32).copy()
    idx = np.arange(128, dtype=np.float64)
    maskT = np.zeros((128, H, 128), np.float64)
    qdT = np.zeros((128, H, 128), np.float64)
    kd = np.zeros((128, H), np.float64)
    for h in range(H):
        g = GAMMA[h]
        causal = (idx[None, :] >= idx[:, None]).astype(np.float64)
        maskT[:, h, :] = (g ** (-(idx[:, None] + 1.0))) * causal * (DK ** -0.5)
        qdT[:, h, :] = (g ** (idx[None, :] + 1.0))
        kd[:, h] = (g ** (127.0 - idx)) * (DK ** -0.5)
    return dict(ident=ident, ones=ones, invf=invf,
                maskT=maskT.reshape(128, H * 128).astype(np.float32),
                qdT=qdT.reshape(128, H * 128).astype(np.float32), kd=kd.astype(np.float32))


def _fm(vec, nblk):
    return np.ascontiguousarray(np.asarray(vec, np.float32).reshape(nblk, 128).T)


def _layer_inputs(li, kind, slot, inp):
    d = {}
    if kind == "conv":
        d[f"gfm{li}"] = _fm(inp["conv_norm"][slot], 8)
        d[f"w_in{li}"] = np.ascontiguousarray(inp["conv_w_in"][slot], dtype=np.float32)
        d[f"w_out{li}"] = np.ascontiguousarray(inp["conv_w_out"][slot], dtype=np.float32)
        dw = np.asarray(inp["conv_dw_w"][slot], np.float32)
        d[f"dww{li}"] = np.ascontiguousarray(dw.reshape(KW, 16, 128).transpose(2, 1, 0).reshape(128, 16 * KW))
        d[f"dwb{li}"] = _fm(inp["conv_dw_b"][slot], 16)
        d[f"lng{li}"] = _fm(inp["conv_ln_g"][slot], 16)
        d[f"lnb{li}"] = _fm(inp["conv_ln_b"][slot], 16)
    else:
        d[f"gfm{li}"] = _fm(inp["ret_norm"][slot], 8)
        d[f"w_in{li}"] = np.ascontiguousarray(inp["ret_w_in"][slot], dtype=np.float32)
        d[f"w_out{li}"] = np.ascontiguousarray(inp["ret_w_out"][slot], dtype=np.float32)
    return d


_PROG_CACHE = {}


def run_layers(x, positions, inp, layer_ids, do_final, n_cores, pair=False):
    B, S, _ = x.shape
    T = S // 2 if pair else S
    layers = [("conv" if i % 2 == 0 else "ret", i // 2) for i in layer_ids]
    key = (T, tuple(layers), do_final, pair, n_cores)
    if key not in _PROG_CACHE:
        _PROG_CACHE[key] = build_program(T, layers, True, True, do_final, pair=pair, n_cores=n_cores)
    nc = _PROG_CACHE[key]
    shared = dict(_consts())
    shared["gfin"] = np.ascontiguousarray(np.broadcast_to(np.asarray(inp["final_norm"], np.float32)[None, :], (128, D)))
    for li, (kind, slot) in enumerate(layers):
        shared.update(_layer_inputs(li, kind, slot, inp))
    in_maps = []
    for c in range(n_cores):
        m = dict(shared)
        if pair:
            b, hf = (c // 2) % B, c % 2
            lo = hf * T
            m["xprev0"] = (np.ascontiguousarray(x[b, lo - 128:lo], dtype=np.float32) if hf == 1
                           else np.zeros((128, D), np.float32))
            m["flag"] = np.full((128, 1), float(hf), np.float32)
        else:
            b, lo = c % B, 0
        m["x"] = np.ascontiguousarray(x[b, lo:lo + T], dtype=np.float32)
        m["posT"] = np.ascontiguousarray(np.asarray(positions[b, lo:lo + T], np.int32).reshape(T // 128, 128).T)
        in_maps.append(m)
    res = run_bass_kernel_spmd(nc, in_maps, core_ids=list(range(n_cores)))
    if pair:
        return np.stack([np.concatenate([res.results[2 * b]["out"], res.results[2 * b + 1]["out"]], axis=0)
                         for b in range(B)], axis=0)
    return np.stack([res.results[b]["out"] for b in range(B)], axis=0)


def kernel(x, positions, conv_norm, conv_w_in, conv_dw_w, conv_dw_b, conv_ln_g, conv_ln_b, conv_w_out,
           ret_norm, ret_w_in, ret_w_out, final_norm):
    inp = dict(conv_norm=np.asarray(conv_norm), conv_w_in=np.asarray(conv_w_in), conv_dw_w=np.asarray(conv_dw_w),
               conv_dw_b=np.asarray(conv_dw_b), conv_ln_g=np.asarray(conv_ln_g), conv_ln_b=np.asarray(conv_ln_b),
               conv_w_out=np.asarray(conv_w_out), ret_norm=np.asarray(ret_norm), ret_w_in=np.asarray(ret_w_in),
               ret_w_out=np.asarray(ret_w_out), final_norm=np.asarray(final_norm))
    x = np.asarray(x, dtype=np.float32)
    positions = np.asarray(positions)
    out = run_layers(x, positions, inp, [0, 1, 2, 3], True, 8, pair=True)
    return out.astype(np.float32)
```
